# Optimizing a Trainium2 kernel written in Bass

```python
import math
import jax
import jax.numpy as jnp
from jax import lax
import numpy as np

D_MODEL = 1024
BATCH = 4
SEQ = 8192
DEPTH = 4

GRID_W = 64
CTX_LEN = 256
EPS = 1e-6
ROPE_THETA = 10000.0
ROT_DIM = 32
Q_BLOCK = 128

DA_HEADS = 6
DA_DIM = ROT_DIM
DA_VDIM = 2 * DA_DIM
DA_W = DA_HEADS * DA_VDIM

MLA_HEADS = 6
MLA_Q_RANK = 256
MLA_KV_RANK = 128
MLA_NOPE = 64
MLA_ROPE = ROT_DIM
MLA_VDIM = 64
MLA_W = MLA_HEADS * MLA_VDIM

HG_HEADS = 4
HG_KDIM = 128
HG_VDIM = 64
HG_CHUNK = 64
HG_KW = HG_HEADS * HG_KDIM
HG_VW = HG_HEADS * HG_VDIM
FORGET_FLOOR = 1e-30

D_MIX = DA_W + MLA_W + HG_VW
D_IN = 3 * DA_W + MLA_Q_RANK + MLA_KV_RANK + MLA_ROPE + 3 * HG_KW + 2 * HG_VW
D_FF = 4 * D_MODEL
N_MOD = 6

kernel_name = 'hybrid_parallel_group_dit_block'


def rmsnorm(x, g):
    xf = x.astype(jnp.float32)
    y = xf * lax.rsqrt(jnp.mean(jnp.square(xf), axis=-1, keepdims=True) + EPS)
    return (y * g.astype(jnp.float32)).astype(x.dtype)


def ada_norm(x, g, shift, scale):
    return rmsnorm(x, g) * (1 + scale[..., None, :]) + shift[..., None, :]


def axial_rope_tables(n):
    rows = n // GRID_W
    row = jnp.repeat(jnp.arange(rows, dtype=jnp.float32), GRID_W)
    col = jnp.tile(jnp.arange(GRID_W, dtype=jnp.float32), rows)
    n_freq = ROT_DIM // 4
    inv = ROPE_THETA ** (-jnp.arange(n_freq, dtype=jnp.float32) / n_freq)
    ang = jnp.stack([row[:, None] * inv, col[:, None] * inv], axis=1)
    return jnp.cos(ang), jnp.sin(ang)


def apply_rope(x, cos, sin):
    shp = x.shape
    xr = x.astype(jnp.float32).reshape(shp[:-1] + (2, 2, shp[-1] // 4))
    cc = cos[None, :, None]
    ss = sin[None, :, None]
    x1 = xr[..., 0, :]
    x2 = xr[..., 1, :]
    out = jnp.stack([x1 * cc - x2 * ss, x2 * cc + x1 * ss], axis=-2)
    return out.reshape(shp).astype(x.dtype)


def project(h, w_in, w_uq, w_ukv, g_cq, g_ckv, rope):
    bsz, n = h.shape[0], h.shape[1]
    p = jnp.einsum('bnd,de->bne', h, w_in)
    sizes = (DA_W, DA_W, DA_W, MLA_Q_RANK, MLA_KV_RANK, MLA_ROPE, HG_KW, HG_KW, HG_KW, HG_VW, HG_VW)
    idx = []
    acc = 0
    for s in sizes[:-1]:
        acc += s
        idx.append(acc)
    da_q, da_k, da_v, cq, ckv, kr, hq, hzf, hzb, hv, hgate = jnp.split(p, idx, axis=-1)
    da_q = da_q.reshape(bsz, n, DA_HEADS * 2, DA_DIM)
    da_k = da_k.reshape(bsz, n, DA_HEADS * 2, DA_DIM)
    qu = (rmsnorm(cq, g_cq) @ w_uq).reshape(bsz, n, MLA_HEADS, MLA_NOPE + MLA_ROPE)
    kvu = (rmsnorm(ckv, g_ckv) @ w_ukv).reshape(bsz, n, MLA_HEADS, MLA_NOPE + MLA_VDIM)
    q_nope, q_rope = qu[..., :MLA_NOPE], qu[..., MLA_NOPE:]
    k_nope, mla_v = kvu[..., :MLA_NOPE], kvu[..., MLA_NOPE:]
    kr = kr[:, :, None, :]
    if rope is not None:
        cos, sin = rope
        da_q = apply_rope(da_q, cos, sin)
        da_k = apply_rope(da_k, cos, sin)
        q_rope = apply_rope(q_rope, cos, sin)
        kr = apply_rope(kr, cos, sin)
    return {
        'da_q': da_q.reshape(bsz, n, DA_HEADS, 2, DA_DIM),
        'da_k': da_k.reshape(bsz, n, DA_HEADS, 2, DA_DIM),
        'da_v': da_v.reshape(bsz, n, DA_HEADS, DA_VDIM),
        'q_nope': q_nope, 'q_rope': q_rope, 'k_nope': k_nope, 'k_rope': kr[:, :, 0], 'mla_v': mla_v,
        'hg_q': hq.reshape(bsz, n, HG_HEADS, HG_KDIM),
        'hg_zf': hzf.reshape(bsz, n, HG_HEADS, HG_KDIM),
        'hg_zb': hzb.reshape(bsz, n, HG_HEADS, HG_KDIM),
        'hg_v': hv.reshape(bsz, n, HG_HEADS, HG_VDIM),
        'hg_gate': hgate.reshape(bsz, n, HG_HEADS, HG_VDIM),
    }


def sweep_query_blocks(fn, *q_arrays):
    n = q_arrays[0].shape[1]
    nb = n // Q_BLOCK
    blocks = tuple(a.reshape((a.shape[0], nb, Q_BLOCK) + a.shape[2:]).swapaxes(0, 1) for a in q_arrays)
    out = lax.map(lambda qs: fn(*qs), blocks)
    out = out.swapaxes(0, 1)
    return out.reshape((out.shape[0], n) + out.shape[3:])


def diff_attn_core(q, k, v, lam):
    s = jnp.einsum('bqhcd,bkhcd->bhcqk', q, k).astype(jnp.float32) * (DA_DIM ** -0.5)
    p = jax.nn.softmax(s, axis=-1)
    w = p[:, :, 0] - lam * p[:, :, 1]
    return jnp.einsum('bhqk,bkhe->bqhe', w.astype(v.dtype), v)


def da_post(o, g, lam_init):
    o = rmsnorm(o, g) * (1.0 - lam_init)
    return o.reshape(o.shape[:2] + (-1,))


def mla_core(q_nope, q_rope, k_nope, k_rope, v):
    s = (jnp.einsum('bqhd,bkhd->bhqk', q_nope, k_nope)
         + jnp.einsum('bqhr,bkr->bhqk', q_rope, k_rope)).astype(jnp.float32) * ((MLA_NOPE + MLA_ROPE) ** -0.5)
    p = jax.nn.softmax(s, axis=-1)
    return jnp.einsum('bhqk,bkhd->bqhd', p.astype(v.dtype), v)


def forget_gate(z, lb):
    lb = lb.reshape(HG_HEADS, HG_KDIM)
    zf = z.astype(jnp.float32)
    f = lb + (1.0 - lb) * jax.nn.sigmoid(zf)
    key = (1.0 - lb) * jax.nn.sigmoid(-zf)
    return jnp.log(jnp.maximum(f, FORGET_FLOOR)), key


def hgrn2_chunk_scan(q, k, v, logf, s0):
    bsz, n = q.shape[0], q.shape[1]
    nc = n // HG_CHUNK

    def to_chunks(a):
        return a.astype(jnp.float32).reshape(bsz, nc, HG_CHUNK, HG_HEADS, a.shape[-1]).transpose(1, 0, 3, 2, 4)

    mask = jnp.tril(jnp.ones((HG_CHUNK, HG_CHUNK), dtype=bool))[:, :, None]

    def step(S, xs):
        qc, kc, vc, gc = xs
        b = jnp.cumsum(gc, axis=-2)
        o_inter = jnp.einsum('bhtk,bhkv->bhtv', qc * jnp.exp(b), S)
        rel = jnp.where(mask, b[..., :, None, :] - b[..., None, :, :], 0.0)
        decay = jnp.where(mask, jnp.exp(rel), 0.0)
        A = jnp.einsum('bhtsk,bhsk->bhts', qc[..., :, None, :] * decay, kc)
        o = o_inter + jnp.einsum('bhts,bhsv->bhtv', A, vc)
        b_last = b[..., -1:, :]
        S_new = jnp.exp(b_last[..., 0, :])[..., None] * S + jnp.einsum('bhsk,bhsv->bhkv', kc * jnp.exp(b_last - b), vc)
        return S_new, o

    S, o = lax.scan(step, s0, (to_chunks(q), to_chunks(k), to_chunks(v), to_chunks(logf)))
    o = o.transpose(1, 0, 3, 2, 4).reshape(bsz, n, HG_HEADS, v.shape[-1])
    return o, S


def hgrn2_bidir(p, lb_f, lb_b, s0_f, s0_b):
    lf_f, k_f = forget_gate(p['hg_zf'], lb_f)
    lf_b, k_b = forget_gate(p['hg_zb'], lb_b)
    q, v = p['hg_q'], p['hg_v']
    o_f, s_f = hgrn2_chunk_scan(q, k_f, v, lf_f, s0_f)
    flip = lambda a: a[:, ::-1]
    o_b, s_b = hgrn2_chunk_scan(flip(q), flip(k_b), flip(v), flip(lf_b), s0_b)
    return (o_f + flip(o_b)).astype(q.dtype), s_f, s_b


def hgrn2_out(o, p, g):
    o = rmsnorm(o, g) * jax.nn.silu(p['hg_gate'])
    return o.reshape(o.shape[:2] + (-1,))


def sqrelu_mlp(h, w1, w2):
    return jnp.square(jax.nn.relu(h @ w1)) @ w2


def setup_inputs(seed: int = 0) -> dict:
    key = jax.random.key(seed)
    ks = jax.random.split(key, 24)
    f32 = jnp.float32

    def nrm(k, shape, scale):
        return jax.random.normal(k, shape, f32) * scale

    return {
        'x': nrm(ks[0], (BATCH, SEQ, D_MODEL), 1.0),
        'c': nrm(ks[1], (BATCH, D_MODEL), 1.0),
        'ctx': nrm(ks[2], (BATCH, CTX_LEN, D_MODEL), 1.0),
        'c_ctx': nrm(ks[3], (D_MODEL,), 1.0),
        'w_mod': nrm(ks[4], (DEPTH, D_MODEL, N_MOD * D_MODEL), 0.5 * D_MODEL ** -0.5),
        'b_mod': nrm(ks[5], (DEPTH, N_MOD * D_MODEL), 0.01),
        'g_mix': 1.0 + nrm(ks[6], (DEPTH, D_MODEL), 0.05),
        'g_mlp': 1.0 + nrm(ks[7], (DEPTH, D_MODEL), 0.05),
        'w_in': nrm(ks[8], (DEPTH, D_MODEL, D_IN), D_MODEL ** -0.5),
        'w_out': nrm(ks[9], (DEPTH, D_MIX, D_MODEL), D_MIX ** -0.5),
        'da_lambda': nrm(ks[10], (DEPTH, 4, DA_DIM), 0.1),
        'da_subln_g': 1.0 + nrm(ks[11], (DEPTH, DA_VDIM), 0.05),
        'mla_g_cq': 1.0 + nrm(ks[12], (DEPTH, MLA_Q_RANK), 0.05),
        'mla_g_ckv': 1.0 + nrm(ks[13], (DEPTH, MLA_KV_RANK), 0.05),
        'mla_w_uq': nrm(ks[14], (DEPTH, MLA_Q_RANK, MLA_HEADS * (MLA_NOPE + MLA_ROPE)), MLA_Q_RANK ** -0.5),
        'mla_w_ukv': nrm(ks[15], (DEPTH, MLA_KV_RANK, MLA_HEADS * (MLA_NOPE + MLA_VDIM)), MLA_KV_RANK ** -0.5),
        'hg_lower_bounds': 1.0 + nrm(ks[16], (2, DEPTH, HG_KW), 0.1),
        'hg_norm_g': 1.0 + nrm(ks[17], (DEPTH, HG_VDIM), 0.05),
        'w_ff1': nrm(ks[18], (DEPTH, D_MODEL, D_FF), D_MODEL ** -0.5),
        'w_ff2': nrm(ks[19], (DEPTH, D_FF, D_MODEL), D_FF ** -0.5),
        'g_final': 1.0 + nrm(ks[20], (D_MODEL,), 0.05),
    }


def reference(x, c, ctx, c_ctx, w_mod, b_mod, g_mix, g_mlp, w_in, w_out, da_lambda, da_subln_g,
              mla_g_cq, mla_g_ckv, mla_w_uq, mla_w_ukv, hg_lower_bounds, hg_norm_g, w_ff1, w_ff2, g_final):
    bsz, n = x.shape[0], x.shape[1]
    rope = axial_rope_tables(n)
    lb = jax.nn.softmax(hg_lower_bounds.astype(jnp.float32), axis=1)
    lb = jnp.cumsum(lb, axis=1) - lb[:, :1]
    silu_c = jax.nn.silu(c)
    silu_cc = jax.nn.silu(c_ctx)
    xc = ctx
    for l in range(DEPTH):
        last = l == DEPTH - 1
        mod = jnp.split(silu_c @ w_mod[l] + b_mod[l], N_MOD, axis=-1)
        modc = jnp.split(silu_cc @ w_mod[l] + b_mod[l], N_MOD, axis=-1)
        lam_init = 0.8 - 0.6 * math.exp(-0.3 * l)
        lam = (jnp.exp(jnp.sum(da_lambda[l, 0] * da_lambda[l, 1]))
               - jnp.exp(jnp.sum(da_lambda[l, 2] * da_lambda[l, 3])) + lam_init).astype(jnp.float32)

        hl = ada_norm(x, g_mix[l], mod[0], mod[1])
        hc = ada_norm(xc, g_mix[l], modc[0], modc[1])
        pl = project(hl, w_in[l], mla_w_uq[l], mla_w_ukv[l], mla_g_cq[l], mla_g_ckv[l], rope)
        pc = project(hc, w_in[l], mla_w_uq[l], mla_w_ukv[l], mla_g_cq[l], mla_g_ckv[l], None)

        k_da = jnp.concatenate([pc['da_k'], pl['da_k']], axis=1)
        v_da = jnp.concatenate([pc['da_v'], pl['da_v']], axis=1)
        o_da = sweep_query_blocks(lambda qb: diff_attn_core(qb, k_da, v_da, lam), pl['da_q'])

        kn_all = jnp.concatenate([pc['k_nope'], pl['k_nope']], axis=1)
        kr_all = jnp.concatenate([pc['k_rope'], pl['k_rope']], axis=1)
        v_all = jnp.concatenate([pc['mla_v'], pl['mla_v']], axis=1)
        o_mla = sweep_query_blocks(lambda qn, qr: mla_core(qn, qr, kn_all, kr_all, v_all), pl['q_nope'], pl['q_rope'])

        zeros = jnp.zeros((bsz, HG_HEADS, HG_KDIM, HG_VDIM), jnp.float32)
        oc_hg, s_f, s_b = hgrn2_bidir(pc, lb[0, l], lb[1, l], zeros, zeros)
        ol_hg, _, _ = hgrn2_bidir(pl, lb[0, l], lb[1, l], s_f, s_b)

        y = jnp.concatenate([da_post(o_da, da_subln_g[l], lam_init),
                             o_mla.reshape(bsz, n, MLA_W),
                             hgrn2_out(ol_hg, pl, hg_norm_g[l])], axis=-1)
        x = x + mod[2][..., None, :] * (y @ w_out[l])
        x = x + mod[5][..., None, :] * sqrelu_mlp(ada_norm(x, g_mlp[l], mod[3], mod[4]), w_ff1[l], w_ff2[l])

        if not last:
            oc_da = diff_attn_core(pc['da_q'], pc['da_k'], pc['da_v'], lam)
            oc_mla = mla_core(pc['q_nope'], pc['q_rope'], pc['k_nope'], pc['k_rope'], pc['mla_v'])
            yc = jnp.concatenate([da_post(oc_da, da_subln_g[l], lam_init),
                                  oc_mla.reshape(bsz, oc_mla.shape[1], MLA_W),
                                  hgrn2_out(oc_hg, pc, hg_norm_g[l])], axis=-1)
            xc = xc + modc[2][..., None, :] * (yc @ w_out[l])
            xc = xc + modc[5][..., None, :] * sqrelu_mlp(ada_norm(xc, g_mlp[l], modc[3], modc[4]), w_ff1[l], w_ff2[l])
    return rmsnorm(x, g_final)
```

```python
import math
from contextlib import ExitStack
import numpy as np
import concourse.bass as bass
import concourse.mybir as mybir
from concourse.bass_utils import run_bass_kernel_spmd

F32 = mybir.dt.float32
BF16 = mybir.dt.bfloat16
AF = mybir.ActivationFunctionType
ALU = mybir.AluOpType
AX = mybir.AxisListType

D = 1024
GRID_W = 64
EPS = 1e-6
DA_H = 6
MLA_H = 6
HG_H = 4
D_IN = 3616
D_FF = 4096
C_DAQ, C_DAK, C_DAV, C_CQ, C_CKV, C_KR, C_HQ, C_HZF, C_HZB, C_HV, C_HG = 0, 384, 768, 1152, 1408, 1536, 1568, 2080, 2592, 3104, 3360


class Buf:
    __slots__ = ("ap", "w", "r", "ps")

    def __init__(self, ap, ps=False):
        self.ap = ap
        self.w = []
        self.r = []
        self.ps = ps


class Eng:
    def __init__(self, e, sem, name):
        self.e = e
        self.sem = sem
        self.cnt = 0
        self.name = name
        self.waited = {}


class K:
    def __init__(self, nc, es):
        self.nc = nc
        self.es = es
        self.pe = Eng(nc.tensor, es.enter_context(nc.semaphore("s_pe")), "pe")
        self.act = Eng(nc.scalar, es.enter_context(nc.semaphore("s_act")), "act")
        self.dve = Eng(nc.vector, es.enter_context(nc.semaphore("s_dve")), "dve")
        self.pool = Eng(nc.gpsimd, es.enter_context(nc.semaphore("s_pool")), "pool")
        self.sp = Eng(nc.sync, es.enter_context(nc.semaphore("s_sp")), "sp")
        self.engs = [self.pe, self.act, self.dve, self.pool, self.sp]
        self.dma_sems = {}
        self.nsem = 0
        self.pending_dma = []
        self.limit = None
        self.nops = 0

    def muted(self):
        self.nops += 1
        return self.limit is not None and self.nops > self.limit

    def wait(self, eng, tok):
        sem, val, owner = tok
        key = id(sem)
        if eng.waited.get(key, 0) >= val:
            return
        if owner is self.pe and eng is self.pe:
            return
        eng.e.wait_ge(sem, val)
        eng.waited[key] = val

    def op(self, eng, reads, writes, fn, extra=()):
        if self.muted():
            return (eng.sem, 0, eng)
        psr = [b for b in reads if b.ps]
        if psr:
            reads = [b for b in reads if not b.ps]
            writes = list(writes) + [b for b in psr if b not in writes]
        for b in reads:
            for t in b.w:
                self.wait(eng, t)
        for b in writes:
            for t in b.r:
                if t[2] is not eng:
                    self.wait(eng, t)
            for t in b.w:
                if t[2] is not eng:
                    self.wait(eng, t)
        for t in extra:
            self.wait(eng, t)
        ins = fn(eng.e)
        eng.cnt += 1
        ins.then_inc(eng.sem, 1)
        tok = (eng.sem, eng.cnt, eng)
        for b in reads:
            b.r.append(tok)
        for b in writes:
            b.w = [tok]
            b.r = []
        return tok

    def dma(self, q, out_buf, in_buf, out_ap, in_ap, sem_key):
        if self.muted():
            return (q.sem, 0, None)
        if sem_key not in self.dma_sems:
            self.dma_sems[sem_key] = [self.es.enter_context(self.nc.semaphore("d%d" % self.nsem)), 0]
            self.nsem += 1
        ent = self.dma_sems[sem_key]
        if in_buf is not None:
            for t in in_buf.w:
                self.wait(q, t)
        if out_buf is not None:
            for t in out_buf.r:
                self.wait(q, t)
            for t in out_buf.w:
                if not (t[2] is None and t[0] is ent[0]):
                    self.wait(q, t)
        ins = q.e.dma_start(out=out_ap, in_=in_ap)
        ent[1] += 16
        ins.then_inc(ent[0], 16)
        tok = (ent[0], ent[1], None)
        if in_buf is not None:
            in_buf.r.append(tok)
        if out_buf is not None:
            out_buf.w = [tok]
            out_buf.r = []
        else:
            self.pending_dma.append(tok)
        return tok

    def barrier(self):
        toks = [(e.sem, e.cnt, e) for e in self.engs if e.cnt > 0]
        for e in self.engs:
            for t in toks:
                if t[2] is not e:
                    self.wait(e, t)
            for t in self.pending_dma:
                self.wait(e, t)
        self.pending_dma = []


LIMIT = None


class _Stop(Exception):
    pass


def build(SEQ, CTX, DEPTH, stop=None):
    def chk(name):
        if stop == name:
            raise _Stop()

    NT = SEQ // 128
    NCT = CTX // 128
    NTT = NT + NCT
    T = SEQ + CTX
    NCH = T // 32
    nc = bass.Bass("TRN2", target_bir_lowering=False)
    es = ExitStack()
    _uid = [0]

    def SBT(name, shape, dt):
        _uid[0] += 1
        return nc.sbuf_tensor("%s_u%d" % (name, _uid[0]), shape, dt)

    def din(name, shape, dt=F32):
        return nc.dram_tensor(name, list(shape), dt, kind="ExternalInput").ap()

    def dscr(name, shape, dt):
        return nc.dram_tensor(name, list(shape), dt, kind="Internal").ap()

    x_in = din("x", [SEQ, D])
    ctx_in = din("ctx", [CTX, D])
    cT_in = din("cT", [128, 8, 2])
    w_mod = din("w_mod", [DEPTH, D, 6 * D])
    b_modT = din("b_modT", [128, DEPTH, 48])
    g_mixT = din("g_mixT", [128, DEPTH, 8])
    g_mlpT = din("g_mlpT", [128, DEPTH, 8])
    w_in = din("w_in", [DEPTH, D, D_IN])
    w_out = din("w_out", [DEPTH, D, D])
    da_lambda = din("da_lambda", [DEPTH, 128])
    da_g = din("da_g", [DEPTH, 64])
    g_cqT = din("g_cqT", [128, DEPTH, 2])
    g_ckvT = din("g_ckvT", [128, DEPTH, 1])
    w_uq = din("w_uq", [DEPTH, 256, 576])
    w_ukn = din("w_ukn", [DEPTH, 128, 384])
    w_uv = din("w_uv", [DEPTH, 128, 384])
    hg_lbT = din("hg_lbT", [128, 2, HG_H, DEPTH])
    hg_g = din("hg_g", [DEPTH, 64])
    w_ff1 = din("w_ff1", [DEPTH, D, D_FF])
    w_ff2 = din("w_ff2", [DEPTH, D_FF, D])
    g_final = din("g_final", [1, D])
    rope_cs = din("rope_cs", [128, NT, 2, 16])
    consts_f = din("consts_f", [128, 9, 128])
    out = nc.dram_tensor("out", [SEQ, D], F32, kind="ExternalOutput").ap()

    xc_res = dscr("xc_res", [CTX, D], F32)
    daqT = dscr("daqT", [384, T], BF16)
    dakT = dscr("dakT", [384, T], BF16)
    dav = dscr("dav", [T, 384], BF16)
    mqT = dscr("mqT", [6, 96, T], BF16)
    mknT = dscr("mknT", [384, T], BF16)
    mkrT = dscr("mkrT", [32, T], BF16)
    mv = dscr("mv", [T, 384], BF16)
    hq0T = dscr("hq0T", [2, HG_H, 128, T], BF16)
    hk0 = dscr("hk0", [2, HG_H, T, 128], BF16)
    hdk = dscr("hdk", [2, HG_H, 128, NCH], F32)
    hvs = dscr("hvs", [T, 256], BF16)
    hgs = dscr("hgs", [T, 256], BF16)
    ycat = dscr("ycat", [T, D], BF16)
    hoi = dscr("hoi", [T, 256], F32)
    h2s = dscr("h2s", [8, 128, T], BF16)

    k = K(nc, es)
    k.limit = LIMIT
    pe, act, dve, pool, sp = k.pe, k.act, k.dve, k.pool, k.sp

    def sb(name, shape, dt):
        return es.enter_context(SBT(name, list(shape), dt))

    PS = es.enter_context(nc.psum_tensor("PS", [128, 8, 512], F32))

    def psf(b, n=512):
        return PS[:, b, 0:n]

    def psbf(b):
        return PS[:, b, :].bitcast(BF16)

    psB = [Buf(None, ps=True) for _ in range(8)]

    cst = sb("cst", [128, 9, 128], F32)
    cstB = Buf(cst)
    k.dma(sp, cstB, None, cst[:], consts_f[:, :, :], "cst")
    identb = sb("identb", [128, 128], BF16)
    identB = Buf(identb)
    k.op(dve, [cstB], [identB], lambda e: e.tensor_copy(out=identb[:], in_=cst[:, 0, :]))
    ident_f = cst[:, 0, :]
    ones_f = cst[:, 1, :]
    eps_t = sb("eps_t", [128, 1], F32)
    epsB = Buf(eps_t)
    k.op(dve, [], [epsB], lambda e: e.memset(eps_t[:], EPS))

    bmod = sb("bmod", [128, DEPTH, 48], F32)
    gmix = sb("gmix", [128, DEPTH, 8], F32)
    gmlp = sb("gmlp", [128, DEPTH, 8], F32)
    gcq = sb("gcq", [128, DEPTH, 2], F32)
    gckv = sb("gckv", [128, DEPTH, 1], F32)
    lbraw = sb("lbraw", [128, 2 * HG_H, DEPTH], F32)
    smallB = Buf(None)
    k.dma(sp, smallB, None, bmod[:], b_modT[:, :, :], "small")
    k.dma(sp, smallB, None, gmix[:], g_mixT[:, :, :], "small")
    k.dma(sp, smallB, None, gmlp[:], g_mlpT[:, :, :], "small")
    k.dma(sp, smallB, None, gcq[:], g_cqT[:, :, :], "small")
    k.dma(sp, smallB, None, gckv[:], g_ckvT[:, :, :], "small")
    k.dma(sp, smallB, None, lbraw[:], hg_lbT.rearrange("p a h l -> p (a h) l"), "small")
    cT = sb("cT", [128, 8, 2], F32)
    k.dma(sp, smallB, None, cT[:], cT_in[:, :, :], "small")
    lamraw = sb("lamraw", [128, DEPTH, 128], F32)
    k.dma(sp, smallB, None, lamraw[:], da_lambda.partition_broadcast(128), "small")
    dag = sb("dag", [128, DEPTH, 64], F32)
    k.dma(sp, smallB, None, dag[:], da_g.partition_broadcast(128), "small")
    hgg = sb("hgg", [128, DEPTH, 64], F32)
    k.dma(sp, smallB, None, hgg[:], hg_g.partition_broadcast(128), "small")

    scT = sb("scT", [128, 8, 2], F32)
    scB = Buf(scT)
    k.op(act, [smallB], [scB], lambda e: e.activation(out=scT[:], in_=cT[:], func=AF.Silu))

    lbe = sb("lbe", [128, 2 * HG_H, DEPTH], F32)
    lbs = sb("lbs", [128, 2 * HG_H], F32)
    lb = sb("lb", [128, 2 * HG_H, DEPTH], F32)
    oml = sb("oml", [128, 2 * HG_H, DEPTH], F32)
    lbB = Buf(None)
    k.op(act, [smallB], [lbB], lambda e: e.activation(out=lbe[:], in_=lbraw[:], func=AF.Exp))
    k.op(dve, [lbB], [lbB], lambda e: e.tensor_reduce(out=lbs[:], in_=lbe[:], axis=AX.X, op=ALU.add))
    k.op(dve, [lbB], [lbB], lambda e: e.reciprocal(out=lbs[:], in_=lbs[:]))
    for l in range(DEPTH):
        k.op(dve, [lbB], [lbB], lambda e, l=l: e.tensor_tensor(out=lbe[:, :, l], in0=lbe[:, :, l], in1=lbs[:], op=ALU.mult))
    k.op(dve, [lbB], [lbB], lambda e: e.memset(lb[:, :, 0], 0.0))
    for l in range(1, DEPTH):
        k.op(dve, [lbB], [lbB], lambda e, l=l: e.tensor_tensor(out=lb[:, :, l], in0=lb[:, :, l - 1], in1=lbe[:, :, l], op=ALU.add))
    k.op(dve, [lbB], [lbB], lambda e: e.tensor_scalar(out=oml[:], in0=lb[:], scalar1=-1.0, scalar2=1.0, op0=ALU.mult, op1=ALU.add))

    k.dma(sp, None, None, xc_res[:, :], ctx_in[:, :], "xc_copy")

    modsb = sb("modsb", [128, 48, 2], F32)
    A1 = sb("A1", [128, 8, 2], F32)
    A2 = sb("A2", [128, 8, 2], F32)
    lam_t = sb("lam_t", [128, 4], F32)
    dagl = sb("dagl", [128, 64], F32)
    modB = Buf(None)

    k.barrier()

    def make_gates(ph_sb, mi):
        gt = ph_sb("gate_bc", [128, 2, D], F32)
        dg = ph_sb("dg", [128, 128], F32)
        dgB = Buf(dg)
        gB = Buf(gt)
        for j in range(2):
            for half in range(2):
                bank = 1 + half
                for cc in range(4):
                    ch = half * 4 + cc
                    k.op(dve, [modB, cstB], [dgB], lambda e, ch=ch, j=j: e.tensor_scalar(
                        out=dg[:], in0=ident_f, scalar1=modsb[:, mi * 8 + ch, j:j + 1], scalar2=None, op0=ALU.mult))
                    k.op(pe, [dgB, cstB], [psB[bank]], lambda e, bank=bank, cc=cc: e.matmul(
                        PS[:, bank, cc * 128:(cc + 1) * 128], lhsT=ones_f, rhs=dg[:], start=True, stop=True, skip_group_check=True))
                k.op(act, [psB[bank]], [gB], lambda e, j=j, half=half, bank=bank: e.activation(
                    out=gt[:, j, half * 512:(half + 1) * 512], in_=PS[:, bank, :], func=AF.Copy))
        return gt, gB

    try:
      for l in range(DEPTH):
        last = l == DEPTH - 1
        lam_init = 0.8 - 0.6 * math.exp(-0.3 * l)
        x_src = x_in if l == 0 else out

        with ExitStack() as ph:
            def psb_(name, shape, dt):
                return ph.enter_context(SBT(name, list(shape), dt))
            wm = [psb_("wm%d" % i, [128, 8, 1024], F32) for i in range(2)]
            wmB = [Buf(wm[i]) for i in range(2)]
            msb = psb_("msb", [128, 48, 2], F32)
            for pc in range(6):
                bi = pc % 2
                for kk in range(8):
                    k.dma(sp, wmB[bi], None, wm[bi][:, kk, :], w_mod[l, kk * 128:(kk + 1) * 128, pc * 1024:(pc + 1) * 1024], ("wm", bi))

                def mm(e, pc=pc, bi=bi):
                    ins = None
                    for j in range(8):
                        for kk in range(8):
                            ins = e.matmul(PS[:, 0, (pc * 8 + j) * 2:(pc * 8 + j) * 2 + 2], lhsT=wm[bi][:, kk, j * 128:(j + 1) * 128],
                                           rhs=scT[:, kk, :], start=(kk == 0), stop=(kk == 7), skip_group_check=True)
                    return ins
                k.op(pe, [wmB[bi], scB], [psB[0]], mm)
            k.op(dve, [psB[0], smallB], [modB], lambda e: e.tensor_tensor(
                out=modsb[:], in0=PS[:, 0, 0:96].rearrange("p (j c) -> p j c", c=2),
                in1=bmod[:, l, :].unsqueeze(2).broadcast_to([128, 48, 2]), op=ALU.add))
            for (At, gsrc, mi) in ((A1, gmix, 1), (A2, gmlp, 4)):
                k.op(dve, [modB], [modB], lambda e, At=At, mi=mi: e.tensor_scalar(
                    out=At[:], in0=modsb[:, mi * 8:(mi + 1) * 8, :], scalar1=1.0, scalar2=None, op0=ALU.add))
                k.op(dve, [modB], [modB], lambda e, At=At, gsrc=gsrc: e.tensor_tensor(
                    out=At[:], in0=At[:], in1=gsrc[:, l, :].unsqueeze(2).broadcast_to([128, 8, 2]), op=ALU.mult))
            lt = psb_("lt", [128, 2, 32], F32)
            k.op(dve, [smallB], [modB], lambda e: e.tensor_tensor(
                out=lt[:], in0=lamraw[:, l, :].rearrange("p (a b c) -> p a b c", a=2, b=2)[:, :, 0, :],
                in1=lamraw[:, l, :].rearrange("p (a b c) -> p a b c", a=2, b=2)[:, :, 1, :], op=ALU.mult))
            k.op(dve, [modB], [modB], lambda e: e.tensor_reduce(out=lam_t[:, 1:3], in_=lt[:], axis=AX.X, op=ALU.add))
            k.op(act, [modB], [modB], lambda e: e.activation(out=lam_t[:, 1:3], in_=lam_t[:, 1:3], func=AF.Exp))
            k.op(dve, [modB], [modB], lambda e: e.tensor_tensor(out=lam_t[:, 0:1], in0=lam_t[:, 2:3], in1=lam_t[:, 1:2], op=ALU.subtract))
            k.op(dve, [modB], [modB], lambda e: e.tensor_scalar(out=lam_t[:, 0:1], in0=lam_t[:, 0:1], scalar1=-lam_init, scalar2=None, op0=ALU.add))
            k.op(dve, [smallB], [modB], lambda e: e.tensor_scalar(out=dagl[:], in0=dag[:, l, :], scalar1=1.0 - lam_init, scalar2=None, op0=ALU.mult))
            k.barrier()
        chk("P0")

        with ExitStack() as ph:
            def psb_(name, shape, dt):
                return ph.enter_context(SBT(name, list(shape), dt))
            win = psb_("win", [128, 8, D_IN], BF16)
            wuq = psb_("wuq", [128, 2, 576], BF16)
            wukn = psb_("wukn", [128, 384], BF16)
            wuv = psb_("wuv", [128, 384], BF16)
            wB = Buf(None)
            with ExitStack() as st_:
                stg = [st_.enter_context(SBT("stg%d" % i, [128, D_IN], F32)) for i in range(2)]
                stgB = [Buf(stg[i]) for i in range(2)]
                cnt = 0
                for kk in range(8):
                    bi = cnt % 2
                    cnt += 1
                    k.dma(sp, stgB[bi], None, stg[bi][:, :], w_in[l, kk * 128:(kk + 1) * 128, :], ("stg", bi))
                    k.op(pool if kk % 2 else dve, [stgB[bi]], [wB], lambda e, kk=kk, bi=bi: e.tensor_copy(out=win[:, kk, :], in_=stg[bi][:, :]))
                for kk in range(2):
                    bi = cnt % 2
                    cnt += 1
                    k.dma(sp, stgB[bi], None, stg[bi][:, 0:576], w_uq[l, kk * 128:(kk + 1) * 128, :], ("stg", bi))
                    k.op(dve, [stgB[bi]], [wB], lambda e, kk=kk, bi=bi: e.tensor_copy(out=wuq[:, kk, :], in_=stg[bi][:, 0:576]))
                for (wt, src) in ((wukn, w_ukn), (wuv, w_uv)):
                    bi = cnt % 2
                    cnt += 1
                    k.dma(sp, stgB[bi], None, stg[bi][:, 0:384], src[l, :, :], ("stg", bi))
                    k.op(dve, [stgB[bi]], [wB], lambda e, wt=wt, bi=bi: e.tensor_copy(out=wt[:], in_=stg[bi][:, 0:384]))


            k.barrier()
            rope = psb_("rope", [128, NT, 2, 16], F32)
            ropeB = Buf(rope)
            k.dma(sp, ropeB, None, rope[:], rope_cs[:, :, :, :], "rope")
            oit = [psb_("oit%d" % i, [128, 256], F32) for i in range(2)]
            oitB = [Buf(None) for i in range(2)]
            xt = [psb_("xt%d" % i, [128, D], F32) for i in range(2)]
            xtB = [Buf(xt[i]) for i in range(2)]
            junk = psb_("junk", [128, D], F32)
            junkB = Buf(junk)
            st = psb_("st", [128, 8], F32)
            stB = Buf(st)
            xn = psb_("xn", [128, D], BF16)
            xnB = Buf(xn)
            hT = psb_("hT", [128, 8, 128], BF16)
            hTB = Buf(hT)
            tmA = psb_("tmA", [128, 1280], BF16)
            tmAB = Buf(tmA)
            rt = [psb_("rt%d" % i, [128, 24, 2, 8], F32) for i in range(4)]
            rtB = Buf(None)
            cqn = psb_("cqn", [128, 384], BF16)
            cqnB = Buf(cqn)
            cqT = psb_("cqT", [128, 3, 128], BF16)
            cqTB = Buf(cqT)
            fmo = [psb_("fmo%d" % i, [128, 1024], BF16) for i in range(2)]
            fmoB = [Buf(fmo[i]) for i in range(2)]
            qu = psb_("qu", [128, 6, 128], BF16)
            quB = Buf(qu)
            qrt = [psb_("qrt%d" % i, [128, 6, 2, 8], F32) for i in range(4)]
            mvt = psb_("mvt", [128, 384], BF16)
            mvtB = Buf(mvt)
            hvt = psb_("hvt", [128, 512], BF16)
            hvtB = Buf(hvt)
            sig = psb_("sig", [128, 8, 128], F32)
            sgn = psb_("sgn", [128, 8, 128], F32)
            gT = psb_("gT", [128, 8, 128], F32)
            hgB = Buf(None)
            gtm = psb_("gtm", [128, 2, 128], F32)
            gtmB = [Buf(None), Buf(None)]
            eq = psb_("eq", [128, 2, 128], F32)
            ek = psb_("ek", [128, 2, 128], F32)
            e0 = psb_("e0", [128, 2, 128], F32)
            eB = [Buf(None), Buf(None)]
            q0T = psb_("q0T", [128, 2, 128], BF16)
            k1T = psb_("k1T", [128, 2, 128], BF16)
            k0T = psb_("k0T", [128, 2, 128], BF16)
            qkB = [Buf(None), Buf(None)]
            k0tm = psb_("k0tm", [128, 2, 128], BF16)
            k0tmB = [Buf(None), Buf(None)]
            atm = psb_("atm", [128, 2, 128], BF16)
            atmB = [Buf(None), Buf(None)]
            dkt = psb_("dkt", [128, 2, 4], F32)
            dktB = [Buf(None), Buf(None)]
            qTs = psb_("qTs", [128, 4, 128], F32)
            qTsB = Buf(qTs)

            k.op(pool, [], [tmAB], lambda e: e.memset(tmA[:], 0.0))
            k.op(pool, [], [quB], lambda e: e.memset(qu[:], 0.0))

            def rms_stats(src_ap, col, nfeat, srcB):
                k.op(act, [srcB], [junkB], lambda e: e.activation(out=junk[:, 0:nfeat], in_=src_ap, func=AF.Square))
                k.op(dve, [junkB], [stB], lambda e: e.tensor_reduce(out=st[:, col:col + 1], in_=junk[:, 0:nfeat], axis=AX.X, op=ALU.add))
                k.op(act, [stB, epsB], [stB], lambda e: e.activation(out=st[:, col:col + 1], in_=st[:, col:col + 1], func=AF.Ln, scale=1.0 / nfeat, bias=eps_t[:]))
                k.op(act, [stB], [stB], lambda e: e.activation(out=st[:, col:col + 1], in_=st[:, col:col + 1], func=AF.Exp, scale=-0.5))

            k.barrier()
            for ti in range(NTT if stop not in ("P1a", "P1b") else (1 if stop == "P1a" else 3)):
                is_ctx = ti < NCT
                mj = 1 if is_ctx else 0
                t0 = ti * 128
                src = xc_res[t0:t0 + 128, :] if is_ctx else x_src[t0 - CTX:t0 - CTX + 128, :]
                xb_ = xt[ti % 2]
                xB_ = xtB[ti % 2]
                k.dma(sp, xB_, None, xb_[:], src, ("xt", ti % 2))
                k.op(act, [xB_], [junkB, stB], lambda e: e.activation(out=junk[:], in_=xb_[:], func=AF.Square, accum_out=st[:, 0:1]))
                k.op(act, [stB, epsB], [stB], lambda e: e.activation(out=st[:, 0:1], in_=st[:, 0:1], func=AF.Ln, scale=1.0 / D, bias=eps_t[:]))
                k.op(act, [stB], [stB], lambda e: e.activation(out=st[:, 0:1], in_=st[:, 0:1], func=AF.Exp, scale=-0.5))
                k.op(dve, [xB_, stB], [xnB], lambda e: e.tensor_scalar(out=xn[:], in0=xb_[:], scalar1=st[:, 0:1], scalar2=None, op0=ALU.mult))

                def tr8(e):
                    ins = None
                    for c in range(8):
                        ins = e.transpose(out=psbf(0)[:, c * 128:(c + 1) * 128], in_=xn[:, c * 128:(c + 1) * 128], identity=identb[:])
                    return ins
                k.op(pe, [xnB, identB], [psB[0]], tr8)
                for c in range(8):
                    k.op(act, [psB[0], modB], [hTB], lambda e, c=c: e.activation(
                        out=hT[:, c, :], in_=psbf(0)[:, c * 128:(c + 1) * 128], func=AF.Identity,
                        scale=A1[:, c, mj:mj + 1], bias=modsb[:, 0 * 8 + c, mj:mj + 1]))
                def tm1(e):
                    ins = None
                    for bnk, (c0, c1) in enumerate(((0, 512), (512, 1024), (1024, 1536), (1536, 1568))):
                        for kk in range(8):
                            ins = e.matmul(PS[:, 1 + bnk, 0:c1 - c0], lhsT=hT[:, kk, :], rhs=win[:, kk, c0:c1], start=(kk == 0), stop=(kk == 7))
                    return ins
                k.op(pe, [hTB, wB], [psB[1], psB[2], psB[3], psB[4]], tm1)
                if is_ctx:
                    k.op(dve, [psB[1]], [tmAB], lambda e: e.tensor_copy(out=tmA[:, 0:512], in_=PS[:, 1, :]))
                    k.op(dve, [psB[2]], [tmAB], lambda e: e.tensor_copy(out=tmA[:, 512:1024], in_=PS[:, 2, :]))
                    k.op(dve, [psB[3]], [tmAB], lambda e: e.tensor_copy(out=tmA[:, 1024:1152], in_=PS[:, 3, 0:128]))
                    k.op(dve, [psB[4]], [tmAB], lambda e: e.tensor_copy(out=tmA[:, 1152:1184], in_=PS[:, 4, 0:32]))
                else:
                    lt_i = ti - NCT
                    for (psrc, nh, h0, dst0) in ((PS[:, 1, :], 16, 0, 0), (PS[:, 2, 0:256], 8, 16, 512), (PS[:, 4, 0:32], 1, 24, 1152)):
                        pv = psrc.rearrange("p (h a b f) -> p h a b f", a=2, b=2, f=8)
                        x1 = pv[:, :, :, 0, :]
                        x2 = pv[:, :, :, 1, :]
                        cos = rope[:, lt_i, 0, :].rearrange("p (a f) -> p a f", a=2).unsqueeze(1).broadcast_to([128, nh, 2, 8])
                        sin = rope[:, lt_i, 1, :].rearrange("p (a f) -> p a f", a=2).unsqueeze(1).broadcast_to([128, nh, 2, 8])
                        dv = tmA[:, dst0:dst0 + nh * 32].rearrange("p (h a b f) -> p h a b f", a=2, b=2, f=8)
                        bankB = psB[1] if h0 == 0 else (psB[2] if h0 == 16 else psB[4])
                        r0, r1, r2, r3 = (rt[i][:, h0:h0 + nh, :, :] if h0 + nh <= 24 else None for i in range(4))
                        if h0 == 24:
                            r0, r1, r2, r3 = (rt[i][:, 0:1, :, :] for i in range(4))
                        k.op(dve, [bankB, ropeB], [rtB], lambda e, x1=x1, cos=cos, r0=r0: e.tensor_tensor(out=r0, in0=x1, in1=cos, op=ALU.mult))
                        k.op(dve, [bankB, ropeB], [rtB], lambda e, x2=x2, sin=sin, r1=r1: e.tensor_tensor(out=r1, in0=x2, in1=sin, op=ALU.mult))
                        k.op(dve, [bankB, ropeB], [rtB], lambda e, x2=x2, cos=cos, r2=r2: e.tensor_tensor(out=r2, in0=x2, in1=cos, op=ALU.mult))
                        k.op(dve, [bankB, ropeB], [rtB], lambda e, x1=x1, sin=sin, r3=r3: e.tensor_tensor(out=r3, in0=x1, in1=sin, op=ALU.mult))
                        k.op(pool, [rtB], [tmAB], lambda e, dv=dv, r0=r0, r1=r1: e.tensor_tensor(out=dv[:, :, :, 0, :], in0=r0, in1=r1, op=ALU.subtract))
                        k.op(pool, [rtB], [tmAB], lambda e, dv=dv, r2=r2, r3=r3: e.tensor_tensor(out=dv[:, :, :, 1, :], in0=r2, in1=r3, op=ALU.add))
                    k.op(act, [psB[2], psB[3]], [tmAB], lambda e: e.activation(out=tmA[:, 768:1024], in_=PS[:, 2, 256:512], func=AF.Copy))
                    k.op(act, [psB[3]], [tmAB], lambda e: e.activation(out=tmA[:, 1024:1152], in_=PS[:, 3, 0:128], func=AF.Copy))
                k.dma(pool, None, tmAB, dav[t0:t0 + 128, :], tmA[:, 768:1152], "st_dav")
                rms_stats(PS[:, 3, 128:384], 1, 256, psB[3])
                rms_stats(PS[:, 3, 384:512], 2, 128, psB[3])
                k.op(dve, [psB[3], stB], [cqnB], lambda e: e.tensor_scalar(out=cqn[:, 0:256], in0=PS[:, 3, 128:384], scalar1=st[:, 1:2], scalar2=None, op0=ALU.mult))
                k.op(dve, [psB[3], stB], [cqnB], lambda e: e.tensor_scalar(out=cqn[:, 256:384], in0=PS[:, 3, 384:512], scalar1=st[:, 2:3], scalar2=None, op0=ALU.mult))
                def trA(e):
                    ins = None
                    for c in range(6):
                        ins = e.transpose(out=psbf(5)[:, c * 128:(c + 1) * 128], in_=tmA[:, c * 128:(c + 1) * 128], identity=identb[:])
                    return ins
                k.op(pe, [tmAB, identB], [psB[5]], trA)
                fb = fmo[0]
                fB = fmoB[0]
                k.op(act, [psB[5]], [fB], lambda e: e.activation(out=fb[:, 0:768], in_=psbf(5)[:, 0:768], func=AF.Copy))
                k.dma(pool, None, fB, daqT.rearrange("(c p) t -> p c t", p=128)[:, :, t0:t0 + 128], fb[:, 0:384].rearrange("p (c t) -> p c t", c=3), "st_daq")
                k.dma(pool, None, fB, dakT.rearrange("(c p) t -> p c t", p=128)[:, :, t0:t0 + 128], fb[:, 384:768].rearrange("p (c t) -> p c t", c=3), "st_dak")

                def trB(e):
                    ins = None
                    for c in range(3):
                        ins = e.transpose(out=psbf(6)[:, c * 128:(c + 1) * 128], in_=cqn[:, c * 128:(c + 1) * 128], identity=identb[:])
                    ins = e.transpose(out=psbf(6)[:, 384:512], in_=tmA[:, 1152:1280], identity=identb[:])
                    return ins
                k.op(pe, [cqnB, tmAB, identB], [psB[6]], trB)
                for c in range(3):
                    sc_ap = gcq[:, l, c:c + 1] if c < 2 else gckv[:, l, 0:1]
                    k.op(act, [psB[6], smallB], [cqTB], lambda e, c=c, sc_ap=sc_ap: e.activation(
                        out=cqT[:, c, :], in_=psbf(6)[:, c * 128:(c + 1) * 128], func=AF.Identity, scale=sc_ap))
                fb1 = fmo[1]
                fB1 = fmoB[1]
                k.op(dve, [psB[6]], [fB1], lambda e: e.tensor_copy(out=fb1[0:32, 0:128], in_=psbf(6)[0:32, 384:512]))
                k.dma(pool, None, fB1, mkrT[:, t0:t0 + 128], fb1[0:32, 0:128], "st_kr")
                def upq(e):
                    ins = None
                    for bnk, (c0, c1) in ((1, (0, 512)), (2, (512, 576))):
                        for kk in range(2):
                            ins = e.matmul(PS[:, bnk, 0:c1 - c0], lhsT=cqT[:, kk, :], rhs=wuq[:, kk, c0:c1], start=(kk == 0), stop=(kk == 1))
                    return ins
                k.op(pe, [cqTB, wB], [psB[1], psB[2]], upq)
                k.op(act, [psB[1]], [quB], lambda e: e.activation(out=qu[:, 0:5, 0:96], in_=PS[:, 1, 0:480].rearrange("p (h c) -> p h c", c=96), func=AF.Copy))
                k.op(act, [psB[1]], [quB], lambda e: e.activation(out=qu[:, 5, 0:32], in_=PS[:, 1, 480:512], func=AF.Copy))
                k.op(act, [psB[2]], [quB], lambda e: e.activation(out=qu[:, 5, 32:96], in_=PS[:, 2, 0:64], func=AF.Copy))
                if not is_ctx:
                    lt_i = ti - NCT
                    qv = qu[:, :, 64:96].rearrange("p h (a b f) -> p h a b f", a=2, b=2)
                    x1 = qv[:, :, :, 0, :]
                    x2 = qv[:, :, :, 1, :]
                    cos = rope[:, lt_i, 0, :].rearrange("p (a f) -> p a f", a=2).unsqueeze(1).broadcast_to([128, 6, 2, 8])
                    sin = rope[:, lt_i, 1, :].rearrange("p (a f) -> p a f", a=2).unsqueeze(1).broadcast_to([128, 6, 2, 8])
                    k.op(dve, [quB, ropeB], [rtB], lambda e: e.tensor_tensor(out=qrt[0][:], in0=x1, in1=cos, op=ALU.mult))
                    k.op(dve, [quB, ropeB], [rtB], lambda e: e.tensor_tensor(out=qrt[1][:], in0=x2, in1=sin, op=ALU.mult))
                    k.op(dve, [quB, ropeB], [rtB], lambda e: e.tensor_tensor(out=qrt[2][:], in0=x2, in1=cos, op=ALU.mult))
                    k.op(dve, [quB, ropeB], [rtB], lambda e: e.tensor_tensor(out=qrt[3][:], in0=x1, in1=sin, op=ALU.mult))
                    k.op(pool, [rtB], [quB], lambda e: e.tensor_tensor(out=x1, in0=qrt[0][:], in1=qrt[1][:], op=ALU.subtract))
                    k.op(pool, [rtB], [quB], lambda e: e.tensor_tensor(out=x2, in0=qrt[2][:], in1=qrt[3][:], op=ALU.add))
                def trQ(e):
                    ins = None
                    for h in range(6):
                        ins = e.transpose(out=psbf(5)[:, h * 128:(h + 1) * 128], in_=qu[:, h, :], identity=identb[:])
                    return ins
                k.op(pe, [quB, identB], [psB[5]], trQ)
                k.op(act, [psB[5]], [fB], lambda e: e.activation(out=fb[:, 0:768], in_=psbf(5)[:, 0:768], func=AF.Copy))
                k.dma(pool, None, fB, mqT.rearrange("h d t -> d h t")[:, :, t0:t0 + 128], fb[0:96, 0:768].rearrange("p (h t) -> p h t", h=6), "st_mq")
                def upk(e):
                    ins = None
                    for c in range(3):
                        ins = e.matmul(PS[:, 6, c * 128:(c + 1) * 128], lhsT=wukn[:, c * 128:(c + 1) * 128], rhs=cqT[:, 2, :], start=True, stop=True, skip_group_check=True)
                    ins = e.matmul(PS[:, 7, 0:384], lhsT=cqT[:, 2, :], rhs=wuv[:, :], start=True, stop=True)
                    return ins
                k.op(pe, [cqTB, wB], [psB[6], psB[7]], upk)
                k.op(dve, [psB[6]], [fB1], lambda e: e.tensor_copy(out=fb1[:, 128:512], in_=PS[:, 6, 0:384]))
                k.dma(pool, None, fB1, mknT.rearrange("(c p) t -> p c t", p=128)[:, :, t0:t0 + 128], fb1[:, 128:512].rearrange("p (c t) -> p c t", c=3), "st_mkn")
                k.op(act, [psB[7]], [mvtB], lambda e: e.activation(out=mvt[:], in_=PS[:, 7, 0:384], func=AF.Copy))
                k.dma(pool, None, mvtB, mv[t0:t0 + 128, :], mvt[:], "st_mv")
                def tm2(e):
                    ins = None
                    for kk in range(8):
                        ins = e.matmul(PS[:, 1, :], lhsT=hT[:, kk, :], rhs=win[:, kk, C_HV:C_HV + 512], start=(kk == 0), stop=(kk == 7))
                    return ins
                k.op(pe, [hTB, wB], [psB[1]], tm2)
                k.op(dve, [psB[1]], [hvtB], lambda e: e.tensor_copy(out=hvt[:, 0:256], in_=PS[:, 1, 0:256]))
                k.op(act, [psB[1]], [hvtB], lambda e: e.activation(out=hvt[:, 256:512], in_=PS[:, 1, 256:512], func=AF.Silu))
                k.dma(pool, None, hvtB, hvs[t0:t0 + 128, :], hvt[:, 0:256], "st_hv")
                k.dma(pool, None, hvtB, hgs[t0:t0 + 128, :], hvt[:, 256:512], "st_hg")
                def fmp(e):
                    ins = None
                    for g, c0 in enumerate((C_HQ, C_HZF, C_HZB)):
                        for h in range(4):
                            for kk in range(8):
                                ins = e.matmul(PS[:, 2 + g, h * 128:(h + 1) * 128], lhsT=win[:, kk, c0 + h * 128:c0 + (h + 1) * 128], rhs=hT[:, kk, :],
                                               start=(kk == 0), stop=(kk == 7), skip_group_check=True)
                    return ins
                k.op(pe, [hTB, wB], [psB[2], psB[3], psB[4]], fmp)
                k.op(dve, [psB[2]], [qTsB], lambda e: e.tensor_copy(out=qTs[:].rearrange("p h t -> p (h t)"), in_=PS[:, 2, :]))
                for d_ in range(2):
                    k.op(act, [psB[3 + d_]], [hgB], lambda e, d_=d_: e.activation(
                        out=sig[:, d_ * 4:(d_ + 1) * 4, :].rearrange("p h t -> p (h t)"), in_=PS[:, 3 + d_, :], func=AF.Sigmoid))
                    k.op(act, [psB[3 + d_]], [hgB], lambda e, d_=d_: e.activation(
                        out=sgn[:, d_ * 4:(d_ + 1) * 4, :].rearrange("p h t -> p (h t)"), in_=PS[:, 3 + d_, :], func=AF.Sigmoid, scale=-1.0))
                for dh in range(8):
                    k.op(dve, [hgB, lbB], [hgB], lambda e, dh=dh: e.tensor_scalar(
                        out=sig[:, dh, :], in0=sig[:, dh, :], scalar1=oml[:, dh, l:l + 1], scalar2=lb[:, dh, l:l + 1], op0=ALU.mult, op1=ALU.add))
                k.op(act, [hgB], [hgB], lambda e: e.activation(out=gT[:], in_=sig[:], func=AF.Ln))
                for dh in range(8):
                    k.op(pool, [hgB, lbB], [hgB], lambda e, dh=dh: e.tensor_scalar(
                        out=sgn[:, dh, :], in0=sgn[:, dh, :], scalar1=oml[:, dh, l:l + 1], scalar2=None, op0=ALU.mult))
                first_o = True
                for dh in range(8):
                    d_, h = dh // 4, dh % 4
                    s2 = dh % 2
                    bT, bC = (0, 1) if s2 == 0 else (5, 6)
                    mi = 2 + 3 * d_
                    k.op(pe, [hgB, cstB], [psB[bT]], lambda e, dh=dh, bT=bT: e.transpose(out=PS[:, bT, 0:128], in_=gT[:, dh, :], identity=ident_f))
                    k.op(dve, [psB[bT]], [gtmB[s2]], lambda e, s2=s2, bT=bT: e.tensor_copy(out=gtm[:, s2, :], in_=PS[:, bT, 0:128]))
                    def cum(e, s2=s2, mi=mi, bC=bC):
                        e.matmul(PS[:, bC, 0:128], lhsT=gtm[:, s2, :], rhs=cst[:, mi, :], start=True, stop=True, skip_group_check=True)
                        return e.matmul(PS[:, bC, 128:256], lhsT=gtm[:, s2, :], rhs=cst[:, mi + 1, :], start=True, stop=True, skip_group_check=True)
                    k.op(pe, [gtmB[s2], cstB], [psB[bC]], cum)
                    k.op(act, [psB[bC]], [eB[s2]], lambda e, s2=s2, bC=bC: e.activation(out=eq[:, s2, :], in_=PS[:, bC, 0:128], func=AF.Exp))
                    k.op(act, [psB[bC]], [eB[s2]], lambda e, s2=s2, bC=bC: e.activation(out=ek[:, s2, :], in_=PS[:, bC, 0:128], func=AF.Exp, scale=-1.0))
                    k.op(act, [psB[bC]], [eB[s2]], lambda e, s2=s2, bC=bC: e.activation(out=e0[:, s2, :], in_=PS[:, bC, 128:256], func=AF.Exp))
                    k.op(dve, [eB[s2], qTsB], [qkB[s2]], lambda e, s2=s2, h=h: e.tensor_tensor(out=q0T[:, s2, :], in0=qTs[:, h, :], in1=eq[:, s2, :], op=ALU.mult))
                    k.op(dve, [eB[s2], hgB], [qkB[s2]], lambda e, s2=s2, dh=dh: e.tensor_tensor(out=k1T[:, s2, :], in0=sgn[:, dh, :], in1=ek[:, s2, :], op=ALU.mult))
                    k.op(pool, [eB[s2], hgB], [qkB[s2]], lambda e, s2=s2, dh=dh: e.tensor_tensor(out=k0T[:, s2, :], in0=sgn[:, dh, :], in1=e0[:, s2, :], op=ALU.mult))
                    off = 31 if d_ == 0 else 0
                    k.op(dve, [eB[s2]], [dktB[s2]], lambda e, s2=s2, off=off: e.tensor_copy(
                        out=dkt[:, s2, :], in_=eq[:, s2, :].rearrange("p (c j) -> p c j", j=32)[:, :, off]))
                    ch0 = t0 // 32
                    k.dma(pool, None, dktB[s2], hdk[d_, h, :, ch0:ch0 + 4], dkt[:, s2, :], "st_dk")
                    k.dma(pool, None, qkB[s2], hq0T[d_, h, :, t0:t0 + 128], q0T[:, s2, :], "st_q0")
                    k.op(pe, [qkB[s2], identB], [psB[bT]], lambda e, s2=s2, bT=bT: e.transpose(out=psbf(bT)[:, 512:640], in_=k0T[:, s2, :], identity=identb[:]))
                    k.op(act, [psB[bT]], [k0tmB[s2]], lambda e, s2=s2, bT=bT: e.activation(out=k0tm[:, s2, :], in_=psbf(bT)[:, 512:640], func=AF.Copy))
                    k.dma(pool, None, k0tmB[s2], hk0[d_, h, t0:t0 + 128, :], k0tm[:, s2, :], "st_k0")
                    k.op(pe, [qkB[s2]], [psB[bC]], lambda e, s2=s2, bC=bC: e.matmul(PS[:, bC, 256:384], lhsT=k1T[:, s2, :], rhs=q0T[:, s2, :], start=True, stop=True, skip_group_check=True))
                    k.op(dve, [psB[bC], cstB], [atmB[s2]], lambda e, s2=s2, mi=mi, bC=bC: e.tensor_tensor(out=atm[:, s2, :], in0=PS[:, bC, 256:384], in1=cst[:, mi + 2, :], op=ALU.mult))
                    k.op(pe, [atmB[s2], hvtB], [psB[7]], lambda e, s2=s2, h=h, d_=d_: e.matmul(
                        PS[:, 7, h * 64:(h + 1) * 64], lhsT=atm[:, s2, :], rhs=hvt[:, h * 64:(h + 1) * 64], start=(d_ == 0 and h == 0), stop=(d_ == 1), skip_group_check=True))
                k.op(dve, [psB[7]], [oitB[ti % 2]], lambda e, ti=ti: e.tensor_copy(out=oit[ti % 2][:], in_=PS[:, 7, 0:256]))
                k.dma(pool, None, oitB[ti % 2], hoi[t0:t0 + 128, :], oit[ti % 2][:], ("st_oi", ti % 2))
            k.barrier()
        chk("P1")
        chk("P1a")
        chk("P1b")

        with ExitStack() as ph:
            def psb_(name, shape, dt):
                return ph.enter_context(SBT(name, list(shape), dt))
            o_acc = psb_("o_acc", [128, NTT, 256], F32)
            oaccB = [Buf(None) for _ in range(NTT)]
            for g0 in range(0, NTT, 8):
                g1 = min(NTT, g0 + 8)
                tk = k.dma(sp, oaccB[g0], None, o_acc[:, g0:g1, :], hoi[g0 * 128:g1 * 128, :].rearrange("(t p) c -> p t c", p=128), "ld_oacc")
                for ti in range(g0, g1):
                    oaccB[ti].w = [tk]

            S32 = psb_("S32", [128, 8, 64], F32)
            S16 = psb_("S16", [128, 8, 64], BF16)
            SB = [Buf(None) for _ in range(2)]
            for d_ in range(2):
                k.op(dve, [], [SB[d_]], lambda e, d_=d_: e.memset(S32[:, d_ * 4:(d_ + 1) * 4, :], 0.0))
                k.op(dve, [], [SB[d_]], lambda e, d_=d_: e.memset(S16[:, d_ * 4:(d_ + 1) * 4, :], 0.0))
            NB2 = 2
            lq = [[psb_("lq%d_%d" % (d_, i), [128, 4, 128], BF16) for i in range(NB2)] for d_ in range(2)]
            lk = [[psb_("lk%d_%d" % (d_, i), [128, 4, 128], BF16) for i in range(NB2)] for d_ in range(2)]
            ld = [[psb_("ld%d_%d" % (d_, i), [128, 4, 4], F32) for i in range(NB2)] for d_ in range(2)]
            lv = [[psb_("lv%d_%d" % (d_, i), [128, 256], BF16) for i in range(NB2)] for d_ in range(2)]
            lB = [[Buf(None) for i in range(NB2)] for d_ in range(2)]
            vm = [[psb_("vm%d_%d" % (d_, i), [128, 4, 256], BF16) for i in range(NB2)] for d_ in range(2)]
            vmB = [[Buf(None) for i in range(NB2)] for d_ in range(2)]
            pbank = {0: (0, 1), 1: (2, 3)}
            for s_ in range(NTT):
                for d_ in range(2):
                    if d_ == 0:
                        ti = s_
                    else:
                        ti = (NCT - 1 - s_) if s_ < NCT else (NTT - 1 - (s_ - NCT))
                    t0 = ti * 128
                    bi = s_ % NB2
                    B_ = lB[d_][bi]
                    k.dma(sp, B_, None, lq[d_][bi][:], hq0T[d_].rearrange("h k t -> k h t")[:, :, t0:t0 + 128], ("l2", d_, bi))
                    k.dma(sp, B_, None, lk[d_][bi][:], hk0[d_].rearrange("h t k -> t h k")[t0:t0 + 128, :, :], ("l2", d_, bi))
                    k.dma(sp, B_, None, ld[d_][bi][:], hdk[d_].rearrange("h k c -> k h c")[:, :, t0 // 32:t0 // 32 + 4], ("l2", d_, bi))
                    k.dma(sp, B_, None, lv[d_][bi][:], hvs[t0:t0 + 128, :], ("l2", d_, bi))
                    for c in range(4):
                        k.op(pool, [B_, cstB], [vmB[d_][bi]], lambda e, c=c, d_=d_, bi=bi: e.tensor_scalar(
                            out=vm[d_][bi][:, c, :], in0=lv[d_][bi][:], scalar1=cst[:, 8, c:c + 1], scalar2=None, op0=ALU.mult))
                    bo, bs = pbank[d_]
                    Sd32 = S32[:, d_ * 4:(d_ + 1) * 4, :]
                    Sd16 = S16[:, d_ * 4:(d_ + 1) * 4, :]
                    for c in (range(4) if d_ == 0 else range(3, -1, -1)):
                        def mo(e, d_=d_, bi=bi, bo=bo):
                            ins = None
                            for h in range(4):
                                ins = e.matmul(PS[:, bo, h * 64:(h + 1) * 64], lhsT=lq[d_][bi][:, h, :], rhs=S16[:, d_ * 4 + h, :], start=True, stop=True, skip_group_check=True)
                            return ins
                        k.op(pe, [B_, SB[d_]], [psB[bo]], mo)
                        k.op(dve, [psB[bo]], [oaccB[ti]], lambda e, bo=bo, c=c, ti=ti: e.tensor_tensor(
                            out=o_acc[c * 32:(c + 1) * 32, ti, :], in0=o_acc[c * 32:(c + 1) * 32, ti, :],
                            in1=PS[c * 32:(c + 1) * 32, bo, 0:256], op=ALU.add))
                        def ms(e, d_=d_, bi=bi, bs=bs, c=c):
                            ins = None
                            for h in range(4):
                                ins = e.matmul(PS[:, bs, h * 64:(h + 1) * 64], lhsT=lk[d_][bi][:, h, :], rhs=vm[d_][bi][:, c, h * 64:(h + 1) * 64], start=True, stop=True, skip_group_check=True)
                            return ins
                        k.op(pe, [B_, vmB[d_][bi]], [psB[bs]], ms)
                        k.op(dve, [B_, SB[d_]], [SB[d_]], lambda e, Sd32=Sd32, d_=d_, bi=bi, c=c: e.tensor_tensor(
                            out=Sd32, in0=Sd32, in1=ld[d_][bi][:, :, c:c + 1].broadcast_to([128, 4, 64]), op=ALU.mult))
                        k.op(dve, [psB[bs], SB[d_]], [SB[d_]], lambda e, Sd32=Sd32, bs=bs: e.tensor_tensor(
                            out=Sd32, in0=Sd32, in1=PS[:, bs, 0:256].rearrange("p (h v) -> p h v", h=4), op=ALU.add))
                        k.op(act, [SB[d_]], [SB[d_]], lambda e, Sd32=Sd32, Sd16=Sd16: e.activation(out=Sd16, in_=Sd32, func=AF.Copy))
            lg = [psb_("lg%d" % i, [128, 256], BF16) for i in range(2)]
            lgB = [Buf(None) for i in range(2)]
            sq = psb_("sq", [128, 4, 64], F32)
            sqB = Buf(None)
            ss = psb_("ss", [128, 4], F32)
            yo = [psb_("yo%d" % i, [128, 256], BF16) for i in range(2)]
            yoB = [Buf(None) for i in range(2)]
            for ti in range(NTT):
                if last and ti < NCT:
                    continue
                t0 = ti * 128
                bi = ti % 2
                k.dma(sp, lgB[bi], None, lg[bi][:], hgs[t0:t0 + 128, :], ("lg", bi))
                ov = o_acc[:, ti, :].rearrange("p (h v) -> p h v", h=4)
                k.op(pool, [oaccB[ti]], [sqB], lambda e, ov=ov: e.tensor_tensor(out=sq[:], in0=ov, in1=ov, op=ALU.mult))
                k.op(dve, [sqB], [sqB], lambda e: e.tensor_reduce(out=ss[:], in_=sq[:], axis=AX.X, op=ALU.add))
                k.op(act, [sqB, epsB], [sqB], lambda e: e.activation(out=ss[:], in_=ss[:], func=AF.Ln, scale=1.0 / 64, bias=eps_t[:]))
                k.op(act, [sqB], [sqB], lambda e: e.activation(out=ss[:], in_=ss[:], func=AF.Exp, scale=-0.5))
                k.op(dve, [sqB, oaccB[ti]], [sqB], lambda e, ov=ov: e.tensor_tensor(out=sq[:], in0=ov, in1=ss[:].unsqueeze(2).broadcast_to([128, 4, 64]), op=ALU.mult))
                k.op(dve, [sqB, smallB], [sqB], lambda e: e.tensor_tensor(out=sq[:], in0=sq[:], in1=hgg[:, l, :].unsqueeze(1).broadcast_to([128, 4, 64]), op=ALU.mult))
                k.op(dve, [sqB, lgB[bi]], [yoB[bi]], lambda e, bi=bi: e.tensor_tensor(out=yo[bi][:], in0=sq[:].rearrange("p h v -> p (h v)"), in1=lg[bi][:], op=ALU.mult))
                k.dma(pool, None, yoB[bi], ycat[t0:t0 + 128, 768:1024], yo[bi][:], ("st_yo", bi))
            k.barrier()
        chk("P2")

        for which in ("da", "mla"):
            with ExitStack() as ph:
                def psb_(name, shape, dt):
                    return ph.enter_context(SBT(name, list(shape), dt))
                if which == "da":
                    units = [(c, hh) for c in range(3) for hh in range(2)]
                    scale = 32 ** -0.5
                else:
                    units = [(h, 0) for h in range(6)]
                    scale = 96 ** -0.5
                nacc = 2 if which == "da" else 1
                KT = psb_("KT", [128, T], BF16)
                KTB = Buf(KT)
                QM = [psb_("QM%d" % j, [128, T], BF16) for j in range(4 if which == "da" else 1)]
                QMB = Buf(None)
                VA = psb_("VA", [128, NTT, 2, 65], BF16)
                VAB = Buf(VA)
                if which == "da":
                    for j in range(4):
                        k.op(pool, [], [QMB], lambda e, j=j: e.memset(QM[j][:], 0.0))
                k.op(pool, [], [VAB], lambda e: e.memset(VA[:], 1.0))
                NPB = 3
                pt = [psb_("pt%d" % i, [128, 2, 512], BF16) for i in range(NPB)]
                ptB = [Buf(None) for i in range(NPB)]
                osb = psb_("osb", [65, 2, 512], F32)
                osbB = Buf(None)
                otm = [psb_("otm%d" % i, [128, 2, 65], F32) for i in range(4)]
                rr = [psb_("rr%d" % i, [128, 4], F32) for i in range(4)]
                oo = [psb_("oo%d" % i, [128, 64], F32) for i in range(4)]
                o2 = psb_("o2", [128, 64], F32)
                otmB = [Buf(None) for i in range(4)]
                yb = [psb_("yb%d" % i, [128, 64], BF16) for i in range(4)]
                ybB = [Buf(None) for i in range(4)]
                sbank = [(0, 1), (2, 3)]
                asets = [(4, 5), (6, 7)]
                chunk_ctr = [0]
                pend = []
                cur_chunk = -1

                def make_epilogue(aset, q0, nq, ycol):
                    nsub = nq // 128

                    def stage1():
                        for cm in range(nacc):
                            k.op(dve, [psB[aset[cm]]], [osbB], lambda e, cm=cm: e.tensor_copy(out=osb[:, cm, 0:nq], in_=PS[0:65, aset[cm], 0:nq]))
                        for qs in range(nsub):
                            def trO(e, qs=qs):
                                ins = None
                                for cm in range(nacc):
                                    ins = e.transpose(out=PS[:, aset[cm], qs * 66:qs * 66 + 65], in_=osb[:, cm, qs * 128:(qs + 1) * 128], identity=cst[0:65, 0, 0:65])
                                return ins
                            k.op(pe, [osbB, cstB], [psB[aset[i]] for i in range(nacc)], trO)
                        for qs in range(nsub):
                            for cm in range(nacc):
                                k.op(dve, [psB[aset[cm]]], [otmB[qs]], lambda e, qs=qs, cm=cm: e.tensor_copy(
                                    out=otm[qs][:, cm, :], in_=PS[:, aset[cm], qs * 66:qs * 66 + 65]))
                            if which == "da":
                                k.op(dve, [otmB[qs]], [otmB[qs]], lambda e, qs=qs: e.reciprocal(out=rr[qs][:, 0:2], in_=otm[qs][:, :, 64]))
                                k.op(dve, [otmB[qs], modB], [otmB[qs]], lambda e, qs=qs: e.tensor_tensor(out=rr[qs][:, 1:2], in0=rr[qs][:, 1:2], in1=lam_t[:, 0:1], op=ALU.mult))
                                k.op(dve, [otmB[qs]], [otmB[qs]], lambda e, qs=qs: e.tensor_scalar(out=oo[qs][:], in0=otm[qs][:, 0, 0:64], scalar1=rr[qs][:, 0:1], scalar2=None, op0=ALU.mult))
                                k.op(dve, [otmB[qs]], [otmB[qs]], lambda e, qs=qs: e.scalar_tensor_tensor(out=oo[qs][:], in0=otm[qs][:, 1, 0:64], scalar=rr[qs][:, 1:2], in1=oo[qs][:], op0=ALU.mult, op1=ALU.add))
                                k.op(dve, [otmB[qs]], [otmB[qs]], lambda e, qs=qs: e.tensor_tensor(out=o2[:], in0=oo[qs][:], in1=oo[qs][:], op=ALU.mult))
                                k.op(dve, [otmB[qs]], [otmB[qs]], lambda e, qs=qs: e.tensor_reduce(out=rr[qs][:, 2:3], in_=o2[:], axis=AX.X, op=ALU.add))
                            else:
                                k.op(dve, [otmB[qs]], [otmB[qs]], lambda e, qs=qs: e.reciprocal(out=rr[qs][:, 0:1], in_=otm[qs][:, 0, 64:65]))

                    def stage2():
                        for qs in range(nsub):
                            if which == "da":
                                k.op(act, [otmB[qs], epsB], [otmB[qs]], lambda e, qs=qs: e.activation(out=rr[qs][:, 2:3], in_=rr[qs][:, 2:3], func=AF.Ln, scale=1.0 / 64, bias=eps_t[:]))
                                k.op(act, [otmB[qs]], [otmB[qs]], lambda e, qs=qs: e.activation(out=rr[qs][:, 2:3], in_=rr[qs][:, 2:3], func=AF.Exp, scale=-0.5))
                        for qs in range(nsub):
                            if which == "da":
                                k.op(dve, [otmB[qs], modB], [ybB[qs]], lambda e, qs=qs: e.scalar_tensor_tensor(
                                    out=yb[qs][:], in0=oo[qs][:], scalar=rr[qs][:, 2:3], in1=dagl[:], op0=ALU.mult, op1=ALU.mult))
                            else:
                                k.op(dve, [otmB[qs]], [ybB[qs]], lambda e, qs=qs: e.tensor_scalar(out=yb[qs][:], in0=otm[qs][:, 0, 0:64], scalar1=rr[qs][:, 0:1], scalar2=None, op0=ALU.mult))
                            r0 = q0 + qs * 128
                            k.dma(pool, None, ybB[qs], ycat[r0:r0 + 128, ycol:ycol + 64], yb[qs][:], ("st_y", qs))
                    return [stage1, stage2]

                for (c, hh) in units:
                    if which == "da":
                        if c != cur_chunk:
                            cur_chunk = c
                            k.dma(sp, KTB, None, KT[:], dakT[c * 128:(c + 1) * 128, :], "ld_kt")
                            for j in range(4):
                                k.dma(sp, QMB, None, QM[j][j * 32:(j + 1) * 32, :], daqT[c * 128 + j * 32:c * 128 + (j + 1) * 32, :], "ld_qm")
                            for g0 in range(0, NTT, 8):
                                g1 = min(NTT, g0 + 8)
                                for h2 in range(2):
                                    k.dma(sp, VAB, None, VA[:, g0:g1, h2, 0:64],
                                          dav[g0 * 128:g1 * 128, c * 128 + h2 * 64:c * 128 + (h2 + 1) * 64].rearrange("(kt p) e -> p kt e", p=128), "ld_va")
                        head = c * 2 + hh
                        ycol = head * 64
                    else:
                        h = c
                        k.dma(sp, KTB, None, KT[0:64, :], mknT[h * 64:(h + 1) * 64, :], "ld_kt")
                        k.dma(sp, KTB, None, KT[64:96, :], mkrT[:, :], "ld_kt")
                        k.dma(sp, QMB, None, QM[0][0:96, :], mqT[h, :, :], "ld_qm")
                        for g0 in range(0, NTT, 8):
                            g1 = min(NTT, g0 + 8)
                            k.dma(sp, VAB, None, VA[:, g0:g1, 0, 0:64],
                                  mv[g0 * 128:g1 * 128, h * 64:(h + 1) * 64].rearrange("(kt p) e -> p kt e", p=128), "ld_va")
                        ycol = 384 + h * 64
                    qchunks = []
                    if not last:
                        qchunks.append((0, CTX, NCT))
                    for q0 in range(CTX, T, 512):
                        qchunks.append((q0, min(512, T - q0), NTT))
                    for (q0, nq, nkt) in qchunks:
                        aset = asets[chunk_ctr[0] % 2]
                        chunk_ctr[0] += 1
                        nit = nkt if which == "da" else nkt // 2

                        def scores(it):
                            sbk = sbank[it % 2]
                            def f(e):
                                ins = None
                                for cm in range(2):
                                    if which == "da":
                                        rhs = QM[hh * 2 + cm][:, q0:q0 + nq]
                                        lhsT = KT[:, it * 128:(it + 1) * 128]
                                    else:
                                        kt = it * 2 + cm
                                        rhs = QM[0][0:96, q0:q0 + nq]
                                        lhsT = KT[0:96, kt * 128:(kt + 1) * 128]
                                    ins = e.matmul(PS[:, sbk[cm], 0:nq], lhsT=lhsT, rhs=rhs, start=True, stop=True)
                                return ins
                            k.op(pe, [KTB, QMB], [psB[sbk[0]], psB[sbk[1]]], f)

                        def expo(it):
                            sbk = sbank[it % 2]
                            pb = it % NPB
                            k.op(act, [psB[sbk[0]], psB[sbk[1]]], [ptB[pb]], lambda e: e.activation(
                                out=pt[pb][:, :, 0:nq], in_=PS[:, sbk[0]:sbk[0] + 2, 0:nq], func=AF.Exp, scale=scale))

                        def pv(it):
                            pb = it % NPB
                            def f(e):
                                ins = None
                                for cm in range(2):
                                    if which == "da":
                                        vsel = VA[:, it, hh, :]
                                        ab = aset[cm]
                                        st_, sp_ = (it == 0), (it == nit - 1)
                                    else:
                                        kt = it * 2 + cm
                                        vsel = VA[:, kt, 0, :]
                                        ab = aset[0]
                                        st_, sp_ = (kt == 0), (kt == nkt - 1)
                                    ins = e.matmul(PS[0:65, ab, 0:nq], lhsT=vsel, rhs=pt[pb][:, cm, 0:nq], start=st_, stop=sp_)
                                return ins
                            k.op(pe, [VAB, ptB[pb]], [psB[aset[i]] for i in range(nacc)] if it == 0 else [], f)
                        e1 = min(4, nit + 1)
                        e2 = min(24, nit + 1)
                        for it in range(nit + 2):
                            if it < nit:
                                scores(it)
                                expo(it)
                            if it >= 2:
                                pv(it - 2)
                            if pend and it == e1:
                                pend[0][0]()
                            if pend and it == e2:
                                pend[0][1]()
                                pend.pop(0)
                        tok_last = (pe.sem, pe.cnt, pe)
                        for i in range(nacc):
                            psB[aset[i]].w = [tok_last]
                            psB[aset[i]].r = []
                        pend.append(make_epilogue(aset, q0, nq, ycol))
                while pend:
                    pend[0][0]()
                    pend[0][1]()
                    pend.pop(0)
                k.barrier()
            chk("P3" if which == "da" else "P4")

        tiles = list(range(NTT)) if not last else list(range(NCT, NTT))
        with ExitStack() as ph:
            def psb_(name, shape, dt):
                return ph.enter_context(SBT(name, list(shape), dt))
            wo = psb_("wo", [128, 8, D], BF16)
            wB = Buf(None)
            with ExitStack() as st_:
                stg = [st_.enter_context(SBT("stg5a_%d" % i, [128, D], F32)) for i in range(2)]
                stgB = [Buf(None) for i in range(2)]
                for kk in range(8):
                    bi = kk % 2
                    k.dma(sp, stgB[bi], None, stg[bi][:, :], w_out[l, kk * 128:(kk + 1) * 128, :], ("stg5a", bi))
                    k.op(dve, [stgB[bi]], [wB], lambda e, kk=kk, bi=bi: e.tensor_copy(out=wo[:, kk, :], in_=stg[bi][:, :]))
            k.barrier()
            gate_bc, gateB = make_gates(psb_, 2)
            yt = [psb_("yt%d" % i, [128, D], BF16) for i in range(2)]
            ytB = [Buf(None) for i in range(2)]
            yT = psb_("yT", [128, 8, 128], BF16)
            yTB = Buf(None)
            xt = [psb_("x5_%d" % i, [128, D], F32) for i in range(2)]
            xtB = [Buf(None) for i in range(2)]
            junk = psb_("junk5", [128, D], BF16)
            junkB = Buf(None)
            junk5f = psb_("junk5f", [128, D], F32)
            j5B = Buf(None)
            st = psb_("st5", [128, 4], F32)
            stB = Buf(None)
            xn = psb_("xn5", [128, D], BF16)
            xnB = Buf(None)
            h2T = [psb_("h2T%d" % i, [128, 8, 128], BF16) for i in range(2)]
            h2B = [Buf(None) for i in range(2)]
            for idx, ti in enumerate(tiles):
                is_ctx = ti < NCT
                mj = 1 if is_ctx else 0
                t0 = ti * 128
                bi = idx % 2
                xsrc = xc_res[t0:t0 + 128, :] if is_ctx else x_src[t0 - CTX:t0 - CTX + 128, :]
                xdst = xc_res[t0:t0 + 128, :] if is_ctx else out[t0 - CTX:t0 - CTX + 128, :]
                k.dma(sp, ytB[bi], None, yt[bi][:], ycat[t0:t0 + 128, :], ("yt", bi))
                k.dma(sp, xtB[bi], None, xt[bi][:], xsrc, ("x5", bi))

                def tr8(e, bi=bi):
                    ins = None
                    for c in range(8):
                        ins = e.transpose(out=psbf(0)[:, c * 128:(c + 1) * 128], in_=yt[bi][:, c * 128:(c + 1) * 128], identity=identb[:])
                    return ins
                k.op(pe, [ytB[bi], identB], [psB[0]], tr8)
                k.op(act, [psB[0]], [yTB], lambda e: e.activation(out=yT[:].rearrange("p c t -> p (c t)"), in_=psbf(0)[:, :], func=AF.Copy))

                def mmo(e):
                    ins = None
                    for half in range(2):
                        for kk in range(8):
                            ins = e.matmul(PS[:, 1 + half, :], lhsT=yT[:, kk, :], rhs=wo[:, kk, half * 512:(half + 1) * 512], start=(kk == 0), stop=(kk == 7))
                    return ins
                k.op(pe, [yTB, wB], [psB[1], psB[2]], mmo)
                xb_ = xt[bi]
                xB_ = xtB[bi]
                for half in range(2):
                    k.op(dve, [psB[1 + half], gateB], [j5B], lambda e, half=half, xb_=xb_: e.tensor_tensor(
                        out=junk5f[:, half * 512:(half + 1) * 512], in0=PS[:, 1 + half, :], in1=gate_bc[:, mj, half * 512:(half + 1) * 512], op=ALU.mult))
                k.op(pool, [xB_, j5B], [xB_], lambda e, xb_=xb_: e.tensor_tensor(out=xb_[:], in0=xb_[:], in1=junk5f[:], op=ALU.add))
                k.dma(pool, None, xB_, xdst, xb_[:], ("st_x1", bi))
                k.op(act, [xB_], [junkB, stB], lambda e, xb_=xb_: e.activation(out=junk[:], in_=xb_[:], func=AF.Square, accum_out=st[:, 0:1]))
                k.op(act, [stB, epsB], [stB], lambda e: e.activation(out=st[:, 0:1], in_=st[:, 0:1], func=AF.Ln, scale=1.0 / D, bias=eps_t[:]))
                k.op(act, [stB], [stB], lambda e: e.activation(out=st[:, 0:1], in_=st[:, 0:1], func=AF.Exp, scale=-0.5))
                k.op(dve, [xB_, stB], [xnB], lambda e, xb_=xb_: e.tensor_scalar(out=xn[:], in0=xb_[:], scalar1=st[:, 0:1], scalar2=None, op0=ALU.mult))

                def tr8b(e):
                    ins = None
                    for c in range(8):
                        ins = e.transpose(out=psbf(3)[:, c * 128:(c + 1) * 128], in_=xn[:, c * 128:(c + 1) * 128], identity=identb[:])
                    return ins
                k.op(pe, [xnB, identB], [psB[3]], tr8b)
                for c in range(8):
                    k.op(act, [psB[3], modB], [h2B[bi]], lambda e, c=c, bi=bi: e.activation(
                        out=h2T[bi][:, c, :], in_=psbf(3)[:, c * 128:(c + 1) * 128], func=AF.Identity,
                        scale=A2[:, c, mj:mj + 1], bias=modsb[:, 3 * 8 + c, mj:mj + 1]))
                k.dma(pool, None, h2B[bi], h2s.rearrange("c p t -> p c t")[:, :, t0:t0 + 128], h2T[bi][:], ("st_h2", bi))
            k.barrier()
        chk("P5a")

        with ExitStack() as ph:
            def psb_(name, shape, dt):
                return ph.enter_context(SBT(name, list(shape), dt))
            w1 = psb_("w1", [128, 8, D_FF], BF16)
            w2 = psb_("w2", [128, 32, D], BF16)
            wB = Buf(None)
            with ExitStack() as st_:
                stg = [st_.enter_context(SBT("stg5b_%d" % i, [128, 2048], F32)) for i in range(2)]
                stgB = [Buf(None) for i in range(2)]
                cnt = 0
                for kk in range(8):
                    for hf in range(2):
                        bi = cnt % 2
                        cnt += 1
                        k.dma(sp, stgB[bi], None, stg[bi][:, :], w_ff1[l, kk * 128:(kk + 1) * 128, hf * 2048:(hf + 1) * 2048], ("stg5b", bi))
                        k.op(pool if cnt % 2 else dve, [stgB[bi]], [wB], lambda e, kk=kk, bi=bi, hf=hf: e.tensor_copy(out=w1[:, kk, hf * 2048:(hf + 1) * 2048], in_=stg[bi][:, :]))
                for kk in range(0, 32, 2):
                    bi = cnt % 2
                    cnt += 1
                    k.dma(sp, stgB[bi], None, stg[bi][:, :].rearrange("p (a e) -> p a e", a=2),
                          w_ff2[l, kk * 128:(kk + 2) * 128, :].rearrange("(a p) e -> p a e", p=128), ("stg5b", bi))
                    k.op(pool if cnt % 2 else dve, [stgB[bi]], [wB], lambda e, kk=kk, bi=bi: e.tensor_copy(
                        out=w2[:, kk:kk + 2, :], in_=stg[bi][:, :].rearrange("p (a e) -> p a e", a=2)))
            k.barrier()
            gate_bc, gateB = make_gates(psb_, 5)
            if last:
                gfin = psb_("gfin", [128, D], F32)
                gfinB = Buf(None)
                k.dma(sp, gfinB, None, gfin[:], g_final.partition_broadcast(128), "gfin")
            h2T = [psb_("h2Tb%d" % i, [128, 8, 128], BF16) for i in range(2)]
            h2B = [Buf(None) for i in range(2)]
            xt = [psb_("x5b_%d" % i, [128, D], F32) for i in range(2)]
            xtB = [Buf(None) for i in range(2)]
            rl = [psb_("rl%d" % i, [128, 512], F32) for i in range(2)]
            rlB = [Buf(None) for i in range(2)]
            uT = psb_("uT", [128, 32, 128], BF16)
            uTB = Buf(None)
            gp = psb_("gp", [128, D], F32)
            gpB = Buf(None)
            junk = psb_("junk5b", [128, D], BF16)
            junkB = Buf(None)
            st = psb_("st5b", [128, 4], F32)
            stB = Buf(None)
            for idx, ti in enumerate(tiles):
                is_ctx = ti < NCT
                mj = 1 if is_ctx else 0
                t0 = ti * 128
                bi = idx % 2
                xdst = xc_res[t0:t0 + 128, :] if is_ctx else out[t0 - CTX:t0 - CTX + 128, :]
                k.dma(sp, h2B[bi], None, h2T[bi][:], h2s.rearrange("c p t -> p c t")[:, :, t0:t0 + 128], ("ld_h2", bi))
                k.dma(sp, xtB[bi], None, xt[bi][:], xdst, ("x5b", bi))
                for g in range(8):
                    bank = 3 + g % 4
                    def f1(e, g=g, bank=bank, bi=bi):
                        ins = None
                        for j in range(4):
                            fc = g * 4 + j
                            for kk in range(8):
                                ins = e.matmul(PS[:, bank, j * 128:(j + 1) * 128], lhsT=w1[:, kk, fc * 128:(fc + 1) * 128], rhs=h2T[bi][:, kk, :],
                                               start=(kk == 0), stop=(kk == 7), skip_group_check=True)
                        return ins
                    k.op(pe, [h2B[bi], wB], [psB[bank]], f1)
                    k.op(act, [psB[bank]], [rlB[g % 2]], lambda e, g=g, bank=bank: e.activation(out=rl[g % 2][:], in_=PS[:, bank, :], func=AF.Relu))
                    k.op(dve if g % 2 else pool, [rlB[g % 2]], [uTB], lambda e, g=g: e.tensor_tensor(
                        out=uT[:, g * 4:(g + 1) * 4, :].rearrange("p c t -> p (c t)"), in0=rl[g % 2][:], in1=rl[g % 2][:], op=ALU.mult))
                def f2(e):
                    ins = None
                    for half in range(2):
                        for fc in range(32):
                            ins = e.matmul(PS[:, 1 + half, :], lhsT=uT[:, fc, :], rhs=w2[:, fc, half * 512:(half + 1) * 512], start=(fc == 0), stop=(fc == 31))
                    return ins
                k.op(pe, [uTB, wB], [psB[1], psB[2]], f2)
                xb_ = xt[bi]
                xB_ = xtB[bi]
                for half in range(2):
                    k.op(dve, [psB[1 + half], gateB], [gpB], lambda e, half=half: e.tensor_tensor(
                        out=gp[:, half * 512:(half + 1) * 512], in0=PS[:, 1 + half, :], in1=gate_bc[:, mj, half * 512:(half + 1) * 512], op=ALU.mult))
                k.op(pool, [gpB, xB_], [xB_], lambda e, xb_=xb_: e.tensor_tensor(out=xb_[:], in0=xb_[:], in1=gp[:], op=ALU.add))
                if last:
                    k.op(act, [xB_], [junkB, stB], lambda e, xb_=xb_: e.activation(out=junk[:], in_=xb_[:], func=AF.Square, accum_out=st[:, 1:2]))
                    k.op(act, [stB, epsB], [stB], lambda e: e.activation(out=st[:, 1:2], in_=st[:, 1:2], func=AF.Ln, scale=1.0 / D, bias=eps_t[:]))
                    k.op(act, [stB], [stB], lambda e: e.activation(out=st[:, 1:2], in_=st[:, 1:2], func=AF.Exp, scale=-0.5))
                    k.op(dve, [xB_, stB, gfinB], [xB_], lambda e, xb_=xb_: e.scalar_tensor_tensor(
                        out=xb_[:], in0=xb_[:], scalar=st[:, 1:2], in1=gfin[:], op0=ALU.mult, op1=ALU.mult))
                k.dma(pool, None, xB_, xdst, xb_[:], ("st_x2", bi))
            k.barrier()

    except _Stop:
        print("STOPPED at", stop, "nops", k.nops)
        k.barrier()
    es.close()
    return nc


def rope_tables(SEQ):
    rows = SEQ // GRID_W
    row = np.repeat(np.arange(rows, dtype=np.float32), GRID_W)
    col = np.tile(np.arange(GRID_W, dtype=np.float32), rows)
    inv = (10000.0 ** (-np.arange(8, dtype=np.float32) / 8)).astype(np.float32)
    ang = np.stack([row[:, None] * inv, col[:, None] * inv], axis=1).astype(np.float32)
    return np.cos(ang).reshape(SEQ, 16).astype(np.float32), np.sin(ang).reshape(SEQ, 16).astype(np.float32)


def make_consts():
    c = np.zeros((128, 9, 128), np.float32)
    c[:, 0, :] = np.eye(128, dtype=np.float32)
    c[:, 1, :] = 1.0
    s = np.arange(128)[:, None]
    t = np.arange(128)[None, :]
    same = (s // 32) == (t // 32)
    c[:, 2, :] = (same & (s <= t)).astype(np.float32)
    c[:, 3, :] = (same & (s > t)).astype(np.float32)
    c[:, 4, :] = (same & (s <= t)).astype(np.float32)
    c[:, 5, :] = (same & (s >= t)).astype(np.float32)
    c[:, 6, :] = (same & (s < t)).astype(np.float32)
    c[:, 7, :] = (same & (s >= t)).astype(np.float32)
    for cc in range(4):
        c[cc * 32:(cc + 1) * 32, 8, cc] = 1.0
    return c


def prep_inputs(inputs, b, SEQ, CTX, DEPTH):
    f = lambda a: np.ascontiguousarray(np.asarray(a, dtype=np.float32))
    cos, sin = rope_tables(SEQ)
    NT = SEQ // 128
    rope_cs = np.stack([cos.reshape(NT, 128, 16), sin.reshape(NT, 128, 16)], axis=2).transpose(1, 0, 2, 3)
    cT = np.stack([f(inputs["c"])[b].reshape(8, 128).T, f(inputs["c_ctx"]).reshape(8, 128).T], axis=2)
    w_ukv = f(inputs["mla_w_ukv"]).reshape(DEPTH, 128, 6, 128)
    m = {
        "x": f(inputs["x"])[b],
        "ctx": f(inputs["ctx"])[b],
        "cT": f(cT),
        "w_mod": f(inputs["w_mod"]),
        "b_modT": f(f(inputs["b_mod"]).reshape(DEPTH, 48, 128).transpose(2, 0, 1)),
        "g_mixT": f(f(inputs["g_mix"]).reshape(DEPTH, 8, 128).transpose(2, 0, 1)),
        "g_mlpT": f(f(inputs["g_mlp"]).reshape(DEPTH, 8, 128).transpose(2, 0, 1)),
        "w_in": f(inputs["w_in"]),
        "w_out": f(inputs["w_out"]),
        "da_lambda": f(f(inputs["da_lambda"]).reshape(DEPTH, 128)),
        "da_g": f(inputs["da_subln_g"]),
        "g_cqT": f(f(inputs["mla_g_cq"]).reshape(DEPTH, 2, 128).transpose(2, 0, 1)),
        "g_ckvT": f(f(inputs["mla_g_ckv"]).reshape(DEPTH, 1, 128).transpose(2, 0, 1)),
        "w_uq": f(inputs["mla_w_uq"]),
        "w_ukn": f(w_ukv[:, :, :, 0:64].reshape(DEPTH, 128, 384)),
        "w_uv": f(w_ukv[:, :, :, 64:128].reshape(DEPTH, 128, 384)),
        "hg_lbT": f(f(inputs["hg_lower_bounds"]).reshape(2, DEPTH, HG_H, 128).transpose(3, 0, 2, 1)),
        "hg_g": f(inputs["hg_norm_g"]),
        "w_ff1": f(inputs["w_ff1"]),
        "w_ff2": f(inputs["w_ff2"]),
        "g_final": f(f(inputs["g_final"]).reshape(1, D)),
        "rope_cs": f(rope_cs),
        "consts_f": make_consts(),
    }
    return m


_NC_CACHE = {}


STOP = None
LIMIT = None


def kernel(**inputs):
    x = np.asarray(inputs["x"])
    B, SEQ, _ = x.shape
    CTX = np.asarray(inputs["ctx"]).shape[1]
    DEPTH = np.asarray(inputs["w_in"]).shape[0]
    key = (SEQ, CTX, DEPTH)
    if key not in _NC_CACHE:
        _NC_CACHE[key] = build(SEQ, CTX, DEPTH, stop=STOP)
    nc = _NC_CACHE[key]
    in_maps = [prep_inputs(inputs, c % B, SEQ, CTX, DEPTH) for c in range(8)]
    res = run_bass_kernel_spmd(nc, in_maps, core_ids=list(range(8)))
    outp = np.stack([np.asarray(res.results[b]["out"], dtype=np.float32) for b in range(B)], axis=0)
    return outp
```

```python
import math
from contextlib import ExitStack
import numpy as np
import concourse.bass as bass
import concourse.mybir as mybir
from concourse.bass_utils import run_bass_kernel_spmd

F32 = mybir.dt.float32
BF16 = mybir.dt.bfloat16
AF = mybir.ActivationFunctionType
ALU = mybir.AluOpType
AX = mybir.AxisListType

D = 1024
GRID_W = 64
EPS = 1e-6
DA_H = 6
MLA_H = 6
HG_H = 4
D_IN = 3616
D_FF = 4096
C_DAQ, C_DAK, C_DAV, C_CQ, C_CKV, C_KR, C_HQ, C_HZF, C_HZB, C_HV, C_HG = 0, 384, 768, 1152, 1408, 1536, 1568, 2080, 2592, 3104, 3360


class Buf:
    __slots__ = ("ap", "w", "r", "ps")

    def __init__(self, ap, ps=False):
        self.ap = ap
        self.w = []
        self.r = []
        self.ps = ps


class Eng:
    def __init__(self, e, sem, name):
        self.e = e
        self.sem = sem
        self.cnt = 0
        self.name = name
        self.waited = {}


class K:
    def __init__(self, nc, es):
        self.nc = nc
        self.es = es
        self.pe = Eng(nc.tensor, es.enter_context(nc.semaphore("s_pe")), "pe")
        self.act = Eng(nc.scalar, es.enter_context(nc.semaphore("s_act")), "act")
        self.dve = Eng(nc.vector, es.enter_context(nc.semaphore("s_dve")), "dve")
        self.pool = Eng(nc.gpsimd, es.enter_context(nc.semaphore("s_pool")), "pool")
        self.sp = Eng(nc.sync, es.enter_context(nc.semaphore("s_sp")), "sp")
        self.engs = [self.pe, self.act, self.dve, self.pool, self.sp]
        self.dma_sems = {}
        self.nsem = 0
        self.pending_dma = []
        self.limit = None
        self.nops = 0

    def muted(self):
        self.nops += 1
        return self.limit is not None and self.nops > self.limit

    def wait(self, eng, tok):
        sem, val, owner = tok
        key = id(sem)
        if eng.waited.get(key, 0) >= val:
            return
        if owner is self.pe and eng is self.pe:
            return
        eng.e.wait_ge(sem, val)
        eng.waited[key] = val

    def op(self, eng, reads, writes, fn, extra=()):
        if self.muted():
            return (eng.sem, 0, eng)
        psr = [b for b in reads if b.ps]
        if psr:
            reads = [b for b in reads if not b.ps]
            writes = list(writes) + [b for b in psr if b not in writes]
        for b in reads:
            for t in b.w:
                self.wait(eng, t)
        for b in writes:
            for t in b.r:
                if t[2] is not eng:
                    self.wait(eng, t)
            for t in b.w:
                if t[2] is not eng:
                    self.wait(eng, t)
        for t in extra:
            self.wait(eng, t)
        ins = fn(eng.e)
        eng.cnt += 1
        ins.then_inc(eng.sem, 1)
        tok = (eng.sem, eng.cnt, eng)
        for b in reads:
            b.r.append(tok)
        for b in writes:
            b.w = [tok]
            b.r = []
        return tok

    def dma(self, q, out_buf, in_buf, out_ap, in_ap, sem_key):
        if self.muted():
            return (q.sem, 0, None)
        if sem_key not in self.dma_sems:
            self.dma_sems[sem_key] = [self.es.enter_context(self.nc.semaphore("d%d" % self.nsem)), 0]
            self.nsem += 1
        ent = self.dma_sems[sem_key]
        if in_buf is not None:
            for t in in_buf.w:
                self.wait(q, t)
        if out_buf is not None:
            for t in out_buf.r:
                self.wait(q, t)
            for t in out_buf.w:
                if not (t[2] is None and t[0] is ent[0]):
                    self.wait(q, t)
        ins = q.e.dma_start(out=out_ap, in_=in_ap)
        ent[1] += 16
        ins.then_inc(ent[0], 16)
        tok = (ent[0], ent[1], None)
        if in_buf is not None:
            in_buf.r.append(tok)
        if out_buf is not None:
            out_buf.w = [tok]
            out_buf.r = []
        else:
            self.pending_dma.append(tok)
        return tok

    def barrier(self):
        toks = [(e.sem, e.cnt, e) for e in self.engs if e.cnt > 0]
        for e in self.engs:
            for t in toks:
                if t[2] is not e:
                    self.wait(e, t)
            for t in self.pending_dma:
                self.wait(e, t)
        self.pending_dma = []


LIMIT = None


class _Stop(Exception):
    pass


def build(SEQ, CTX, DEPTH, stop=None):
    def chk(name):
        if stop == name:
            raise _Stop()

    NT = SEQ // 128
    NCT = CTX // 128
    NTT = NT + NCT
    T = SEQ + CTX
    NCH = T // 32
    nc = bass.Bass("TRN2", target_bir_lowering=False)
    es = ExitStack()
    _uid = [0]

    def SBT(name, shape, dt):
        _uid[0] += 1
        return nc.sbuf_tensor("%s_u%d" % (name, _uid[0]), shape, dt)

    def din(name, shape, dt=F32):
        return nc.dram_tensor(name, list(shape), dt, kind="ExternalInput").ap()

    def dscr(name, shape, dt):
        return nc.dram_tensor(name, list(shape), dt, kind="Internal").ap()

    x_in = din("x", [SEQ, D])
    ctx_in = din("ctx", [CTX, D])
    cT_in = din("cT", [128, 8, 2])
    w_mod = din("w_mod", [DEPTH, D, 6 * D])
    b_modT = din("b_modT", [128, DEPTH, 48])
    g_mixT = din("g_mixT", [128, DEPTH, 8])
    g_mlpT = din("g_mlpT", [128, DEPTH, 8])
    w_in = din("w_in", [DEPTH, D, D_IN])
    w_out = din("w_out", [DEPTH, D, D])
    da_lambda = din("da_lambda", [DEPTH, 128])
    da_g = din("da_g", [DEPTH, 64])
    g_cqT = din("g_cqT", [128, DEPTH, 2])
    g_ckvT = din("g_ckvT", [128, DEPTH, 1])
    w_uq = din("w_uq", [DEPTH, 256, 576])
    w_ukn = din("w_ukn", [DEPTH, 128, 384])
    w_uv = din("w_uv", [DEPTH, 128, 384])
    hg_lbT = din("hg_lbT", [128, 2, HG_H, DEPTH])
    hg_g = din("hg_g", [DEPTH, 64])
    w_ff1 = din("w_ff1", [DEPTH, D, D_FF])
    w_ff2 = din("w_ff2", [DEPTH, D_FF, D])
    g_final = din("g_final", [1, D])
    rope_cs = din("rope_cs", [128, NT, 2, 16])
    consts_f = din("consts_f", [128, 9, 128])
    out = nc.dram_tensor("out", [SEQ, D], F32, kind="ExternalOutput").ap()

    xc_res = dscr("xc_res", [CTX, D], F32)
    daqT = dscr("daqT", [384, T], BF16)
    dakT = dscr("dakT", [384, T], BF16)
    dav = dscr("dav", [T, 384], BF16)
    mqT = dscr("mqT", [6, 96, T], BF16)
    mknT = dscr("mknT", [384, T], BF16)
    mkrT = dscr("mkrT", [32, T], BF16)
    mv = dscr("mv", [T, 384], BF16)
    hq0T = dscr("hq0T", [2, HG_H, 128, T], BF16)
    hk0 = dscr("hk0", [2, HG_H, T, 128], BF16)
    hdk = dscr("hdk", [2, HG_H, 128, NCH], F32)
    hvs = dscr("hvs", [T, 256], BF16)
    hgs = dscr("hgs", [T, 256], BF16)
    ycat = dscr("ycat", [T, D], BF16)
    hoi = dscr("hoi", [T, 256], F32)
    h2s = dscr("h2s", [8, 128, T], BF16)

    k = K(nc, es)
    k.limit = LIMIT
    pe, act, dve, pool, sp = k.pe, k.act, k.dve, k.pool, k.sp

    def sb(name, shape, dt):
        return es.enter_context(SBT(name, list(shape), dt))

    PS = es.enter_context(nc.psum_tensor("PS", [128, 8, 512], F32))

    def psf(b, n=512):
        return PS[:, b, 0:n]

    def psbf(b):
        return PS[:, b, :].bitcast(BF16)

    psB = [Buf(None, ps=True) for _ in range(8)]

    cst = sb("cst", [128, 9, 128], F32)
    cstB = Buf(cst)
    k.dma(sp, cstB, None, cst[:], consts_f[:, :, :], "cst")
    identb = sb("identb", [128, 128], BF16)
    identB = Buf(identb)
    k.op(dve, [cstB], [identB], lambda e: e.tensor_copy(out=identb[:], in_=cst[:, 0, :]))
    ident_f = cst[:, 0, :]
    ones_f = cst[:, 1, :]
    eps_t = sb("eps_t", [128, 1], F32)
    epsB = Buf(eps_t)
    k.op(dve, [], [epsB], lambda e: e.memset(eps_t[:], EPS))

    bmod = sb("bmod", [128, DEPTH, 48], F32)
    gmix = sb("gmix", [128, DEPTH, 8], F32)
    gmlp = sb("gmlp", [128, DEPTH, 8], F32)
    gcq = sb("gcq", [128, DEPTH, 2], F32)
    gckv = sb("gckv", [128, DEPTH, 1], F32)
    lbraw = sb("lbraw", [128, 2 * HG_H, DEPTH], F32)
    smallB = Buf(None)
    k.dma(sp, smallB, None, bmod[:], b_modT[:, :, :], "small")
    k.dma(sp, smallB, None, gmix[:], g_mixT[:, :, :], "small")
    k.dma(sp, smallB, None, gmlp[:], g_mlpT[:, :, :], "small")
    k.dma(sp, smallB, None, gcq[:], g_cqT[:, :, :], "small")
    k.dma(sp, smallB, None, gckv[:], g_ckvT[:, :, :], "small")
    k.dma(sp, smallB, None, lbraw[:], hg_lbT.rearrange("p a h l -> p (a h) l"), "small")
    cT = sb("cT", [128, 8, 2], F32)
    k.dma(sp, smallB, None, cT[:], cT_in[:, :, :], "small")
    lamraw = sb("lamraw", [128, DEPTH, 128], F32)
    k.dma(sp, smallB, None, lamraw[:], da_lambda.partition_broadcast(128), "small")
    dag = sb("dag", [128, DEPTH, 64], F32)
    k.dma(sp, smallB, None, dag[:], da_g.partition_broadcast(128), "small")
    hgg = sb("hgg", [128, DEPTH, 64], F32)
    k.dma(sp, smallB, None, hgg[:], hg_g.partition_broadcast(128), "small")

    scT = sb("scT", [128, 8, 2], F32)
    scB = Buf(scT)
    k.op(act, [smallB], [scB], lambda e: e.activation(out=scT[:], in_=cT[:], func=AF.Silu))

    lbe = sb("lbe", [128, 2 * HG_H, DEPTH], F32)
    lbs = sb("lbs", [128, 2 * HG_H], F32)
    lb = sb("lb", [128, 2 * HG_H, DEPTH], F32)
    oml = sb("oml", [128, 2 * HG_H, DEPTH], F32)
    lbB = Buf(None)
    k.op(act, [smallB], [lbB], lambda e: e.activation(out=lbe[:], in_=lbraw[:], func=AF.Exp))
    k.op(dve, [lbB], [lbB], lambda e: e.tensor_reduce(out=lbs[:], in_=lbe[:], axis=AX.X, op=ALU.add))
    k.op(dve, [lbB], [lbB], lambda e: e.reciprocal(out=lbs[:], in_=lbs[:]))
    for l in range(DEPTH):
        k.op(dve, [lbB], [lbB], lambda e, l=l: e.tensor_tensor(out=lbe[:, :, l], in0=lbe[:, :, l], in1=lbs[:], op=ALU.mult))
    k.op(dve, [lbB], [lbB], lambda e: e.memset(lb[:, :, 0], 0.0))
    for l in range(1, DEPTH):
        k.op(dve, [lbB], [lbB], lambda e, l=l: e.tensor_tensor(out=lb[:, :, l], in0=lb[:, :, l - 1], in1=lbe[:, :, l], op=ALU.add))
    k.op(dve, [lbB], [lbB], lambda e: e.tensor_scalar(out=oml[:], in0=lb[:], scalar1=-1.0, scalar2=1.0, op0=ALU.mult, op1=ALU.add))

    k.dma(sp, None, None, xc_res[:, :], ctx_in[:, :], "xc_copy")

    modsb = sb("modsb", [128, 48, 2], F32)
    A1 = sb("A1", [128, 8, 2], F32)
    A2 = sb("A2", [128, 8, 2], F32)
    lam_t = sb("lam_t", [128, 4], F32)
    dagl = sb("dagl", [128, 64], F32)
    modB = Buf(None)

    k.barrier()

    def make_gates(ph_sb, mi):
        gt = ph_sb("gate_bc", [128, 2, D], F32)
        dg = ph_sb("dg", [128, 128], F32)
        dgB = Buf(dg)
        gB = Buf(gt)
        for j in range(2):
            for half in range(2):
                bank = 1 + half
                for cc in range(4):
                    ch = half * 4 + cc
                    k.op(dve, [modB, cstB], [dgB], lambda e, ch=ch, j=j: e.tensor_scalar(
                        out=dg[:], in0=ident_f, scalar1=modsb[:, mi * 8 + ch, j:j + 1], scalar2=None, op0=ALU.mult))
                    k.op(pe, [dgB, cstB], [psB[bank]], lambda e, bank=bank, cc=cc: e.matmul(
                        PS[:, bank, cc * 128:(cc + 1) * 128], lhsT=ones_f, rhs=dg[:], start=True, stop=True, skip_group_check=True))
                k.op(act, [psB[bank]], [gB], lambda e, j=j, half=half, bank=bank: e.activation(
                    out=gt[:, j, half * 512:(half + 1) * 512], in_=PS[:, bank, :], func=AF.Copy))
        return gt, gB

    try:
      for l in range(DEPTH):
        last = l == DEPTH - 1
        lam_init = 0.8 - 0.6 * math.exp(-0.3 * l)
        x_src = x_in if l == 0 else out

        with ExitStack() as ph:
            def psb_(name, shape, dt):
                return ph.enter_context(SBT(name, list(shape), dt))
            wm = [psb_("wm%d" % i, [128, 8, 1024], F32) for i in range(2)]
            wmB = [Buf(wm[i]) for i in range(2)]
            msb = psb_("msb", [128, 48, 2], F32)
            for pc in range(6):
                bi = pc % 2
                for kk in range(8):
                    k.dma(sp, wmB[bi], None, wm[bi][:, kk, :], w_mod[l, kk * 128:(kk + 1) * 128, pc * 1024:(pc + 1) * 1024], ("wm", bi))

                def mm(e, pc=pc, bi=bi):
                    ins = None
                    for j in range(8):
                        for kk in range(8):
                            ins = e.matmul(PS[:, 0, (pc * 8 + j) * 2:(pc * 8 + j) * 2 + 2], lhsT=wm[bi][:, kk, j * 128:(j + 1) * 128],
                                           rhs=scT[:, kk, :], start=(kk == 0), stop=(kk == 7), skip_group_check=True)
                    return ins
                k.op(pe, [wmB[bi], scB], [psB[0]], mm)
            k.op(dve, [psB[0], smallB], [modB], lambda e: e.tensor_tensor(
                out=modsb[:], in0=PS[:, 0, 0:96].rearrange("p (j c) -> p j c", c=2),
                in1=bmod[:, l, :].unsqueeze(2).broadcast_to([128, 48, 2]), op=ALU.add))
            for (At, gsrc, mi) in ((A1, gmix, 1), (A2, gmlp, 4)):
                k.op(dve, [modB], [modB], lambda e, At=At, mi=mi: e.tensor_scalar(
                    out=At[:], in0=modsb[:, mi * 8:(mi + 1) * 8, :], scalar1=1.0, scalar2=None, op0=ALU.add))
                k.op(dve, [modB], [modB], lambda e, At=At, gsrc=gsrc: e.tensor_tensor(
                    out=At[:], in0=At[:], in1=gsrc[:, l, :].unsqueeze(2).broadcast_to([128, 8, 2]), op=ALU.mult))
            lt = psb_("lt", [128, 2, 32], F32)
            k.op(dve, [smallB], [modB], lambda e: e.tensor_tensor(
                out=lt[:], in0=lamraw[:, l, :].rearrange("p (a b c) -> p a b c", a=2, b=2)[:, :, 0, :],
                in1=lamraw[:, l, :].rearrange("p (a b c) -> p a b c", a=2, b=2)[:, :, 1, :], op=ALU.mult))
            k.op(dve, [modB], [modB], lambda e: e.tensor_reduce(out=lam_t[:, 1:3], in_=lt[:], axis=AX.X, op=ALU.add))
            k.op(act, [modB], [modB], lambda e: e.activation(out=lam_t[:, 1:3], in_=lam_t[:, 1:3], func=AF.Exp))
            k.op(dve, [modB], [modB], lambda e: e.tensor_tensor(out=lam_t[:, 0:1], in0=lam_t[:, 2:3], in1=lam_t[:, 1:2], op=ALU.subtract))
            k.op(dve, [modB], [modB], lambda e: e.tensor_scalar(out=lam_t[:, 0:1], in0=lam_t[:, 0:1], scalar1=-lam_init, scalar2=None, op0=ALU.add))
            k.op(dve, [smallB], [modB], lambda e: e.tensor_scalar(out=dagl[:], in0=dag[:, l, :], scalar1=1.0 - lam_init, scalar2=None, op0=ALU.mult))
            k.barrier()
        chk("P0")

        with ExitStack() as ph:
            def psb_(name, shape, dt):
                return ph.enter_context(SBT(name, list(shape), dt))
            win = psb_("win", [128, 8, D_IN], BF16)
            wuq = psb_("wuq", [128, 2, 576], BF16)
            wukn = psb_("wukn", [128, 384], BF16)
            wuv = psb_("wuv", [128, 384], BF16)
            wB = Buf(None)
            with ExitStack() as st_:
                stg = [st_.enter_context(SBT("stg%d" % i, [128, D_IN], F32)) for i in range(2)]
                stgB = [Buf(stg[i]) for i in range(2)]
                cnt = 0
                for kk in range(8):
                    bi = cnt % 2
                    cnt += 1
                    k.dma(sp, stgB[bi], None, stg[bi][:, :], w_in[l, kk * 128:(kk + 1) * 128, :], ("stg", bi))
                    k.op(pool if kk % 2 else dve, [stgB[bi]], [wB], lambda e, kk=kk, bi=bi: e.tensor_copy(out=win[:, kk, :], in_=stg[bi][:, :]))
                for kk in range(2):
                    bi = cnt % 2
                    cnt += 1
                    k.dma(sp, stgB[bi], None, stg[bi][:, 0:576], w_uq[l, kk * 128:(kk + 1) * 128, :], ("stg", bi))
                    k.op(dve, [stgB[bi]], [wB], lambda e, kk=kk, bi=bi: e.tensor_copy(out=wuq[:, kk, :], in_=stg[bi][:, 0:576]))
                for (wt, src) in ((wukn, w_ukn), (wuv, w_uv)):
                    bi = cnt % 2
                    cnt += 1
                    k.dma(sp, stgB[bi], None, stg[bi][:, 0:384], src[l, :, :], ("stg", bi))
                    k.op(dve, [stgB[bi]], [wB], lambda e, wt=wt, bi=bi: e.tensor_copy(out=wt[:], in_=stg[bi][:, 0:384]))


            k.barrier()
            rope = psb_("rope", [128, NT, 2, 16], F32)
            ropeB = Buf(rope)
            k.dma(sp, ropeB, None, rope[:], rope_cs[:, :, :, :], "rope")
            oit = [psb_("oit%d" % i, [128, 256], F32) for i in range(2)]
            oitB = [Buf(None) for i in range(2)]
            xt = [psb_("xt%d" % i, [128, D], F32) for i in range(2)]
            xtB = [Buf(xt[i]) for i in range(2)]
            junk = psb_("junk", [128, D], F32)
            junkB = Buf(junk)
            st = psb_("st", [128, 8], F32)
            stB = Buf(st)
            xn = psb_("xn", [128, D], BF16)
            xnB = Buf(xn)
            hT = psb_("hT", [128, 8, 128], BF16)
            hTB = Buf(hT)
            tmA = psb_("tmA", [128, 1280], BF16)
            tmAB = Buf(tmA)
            rt = [psb_("rt%d" % i, [128, 24, 2, 8], F32) for i in range(4)]
            rtB = Buf(None)
            cqn = psb_("cqn", [128, 384], BF16)
            cqnB = Buf(cqn)
            cqT = psb_("cqT", [128, 3, 128], BF16)
            cqTB = Buf(cqT)
            fmo = [psb_("fmo%d" % i, [128, 1024], BF16) for i in range(2)]
            fmoB = [Buf(fmo[i]) for i in range(2)]
            qu = psb_("qu", [128, 6, 128], BF16)
            quB = Buf(qu)
            qrt = [psb_("qrt%d" % i, [128, 6, 2, 8], F32) for i in range(4)]
            mvt = psb_("mvt", [128, 384], BF16)
            mvtB = Buf(mvt)
            hvt = psb_("hvt", [128, 512], BF16)
            hvtB = Buf(hvt)
            sig = psb_("sig", [128, 8, 128], F32)
            sgn = psb_("sgn", [128, 8, 128], F32)
            gT = psb_("gT", [128, 8, 128], F32)
            hgB = Buf(None)
            gtm = psb_("gtm", [128, 2, 128], F32)
            gtmB = [Buf(None), Buf(None)]
            eq = psb_("eq", [128, 2, 128], F32)
            ek = psb_("ek", [128, 2, 128], F32)
            e0 = psb_("e0", [128, 2, 128], F32)
            eB = [Buf(None), Buf(None)]
            q0T = psb_("q0T", [128, 2, 128], BF16)
            k1T = psb_("k1T", [128, 2, 128], BF16)
            k0T = psb_("k0T", [128, 2, 128], BF16)
            qkB = [Buf(None), Buf(None)]
            k0tm = psb_("k0tm", [128, 2, 128], BF16)
            k0tmB = [Buf(None), Buf(None)]
            atm = psb_("atm", [128, 2, 128], BF16)
            atmB = [Buf(None), Buf(None)]
            dkt = psb_("dkt", [128, 2, 4], F32)
            dktB = [Buf(None), Buf(None)]
            qTs = psb_("qTs", [128, 4, 128], F32)
            qTsB = Buf(qTs)
            gtm4 = [psb_("gtm4_%d" % i, [128, 512], F32) for i in range(2)]
            eq4 = [psb_("eq4_%d" % i, [128, 512], F32) for i in range(2)]
            ek4 = [psb_("ek4_%d" % i, [128, 512], F32) for i in range(2)]
            e04 = [psb_("e04_%d" % i, [128, 512], F32) for i in range(2)]
            q0T4 = [psb_("q0T4_%d" % i, [128, 512], BF16) for i in range(2)]
            k1T4 = [psb_("k1T4_%d" % i, [128, 512], BF16) for i in range(2)]
            k0T4 = [psb_("k0T4_%d" % i, [128, 512], BF16) for i in range(2)]
            k0tm4 = [psb_("k0tm4_%d" % i, [128, 512], BF16) for i in range(2)]
            atm4 = [psb_("atm4_%d" % i, [128, 512], BF16) for i in range(2)]
            dkt4 = [psb_("dkt4_%d" % i, [128, 4, 4], F32) for i in range(2)]

            k.op(pool, [], [tmAB], lambda e: e.memset(tmA[:], 0.0))
            k.op(pool, [], [quB], lambda e: e.memset(qu[:], 0.0))

            def rms_stats(src_ap, col, nfeat, srcB):
                k.op(act, [srcB], [junkB], lambda e: e.activation(out=junk[:, 0:nfeat], in_=src_ap, func=AF.Square))
                k.op(dve, [junkB], [stB], lambda e: e.tensor_reduce(out=st[:, col:col + 1], in_=junk[:, 0:nfeat], axis=AX.X, op=ALU.add))
                k.op(act, [stB, epsB], [stB], lambda e: e.activation(out=st[:, col:col + 1], in_=st[:, col:col + 1], func=AF.Ln, scale=1.0 / nfeat, bias=eps_t[:]))
                k.op(act, [stB], [stB], lambda e: e.activation(out=st[:, col:col + 1], in_=st[:, col:col + 1], func=AF.Exp, scale=-0.5))

            k.barrier()
            for ti in range(NTT if stop not in ("P1a", "P1b") else (1 if stop == "P1a" else 3)):
                is_ctx = ti < NCT
                mj = 1 if is_ctx else 0
                t0 = ti * 128
                src = xc_res[t0:t0 + 128, :] if is_ctx else x_src[t0 - CTX:t0 - CTX + 128, :]
                xb_ = xt[ti % 2]
                xB_ = xtB[ti % 2]
                k.dma(sp, xB_, None, xb_[:], src, ("xt", ti % 2))
                k.op(act, [xB_], [junkB, stB], lambda e: e.activation(out=junk[:], in_=xb_[:], func=AF.Square, accum_out=st[:, 0:1]))
                k.op(act, [stB, epsB], [stB], lambda e: e.activation(out=st[:, 0:1], in_=st[:, 0:1], func=AF.Ln, scale=1.0 / D, bias=eps_t[:]))
                k.op(act, [stB], [stB], lambda e: e.activation(out=st[:, 0:1], in_=st[:, 0:1], func=AF.Exp, scale=-0.5))
                k.op(dve, [xB_, stB], [xnB], lambda e: e.tensor_scalar(out=xn[:], in0=xb_[:], scalar1=st[:, 0:1], scalar2=None, op0=ALU.mult))

                def tr8(e):
                    ins = None
                    for c in range(8):
                        ins = e.transpose(out=psbf(0)[:, c * 128:(c + 1) * 128], in_=xn[:, c * 128:(c + 1) * 128], identity=identb[:])
                    return ins
                k.op(pe, [xnB, identB], [psB[0]], tr8)
                for c in range(8):
                    k.op(act, [psB[0], modB], [hTB], lambda e, c=c: e.activation(
                        out=hT[:, c, :], in_=psbf(0)[:, c * 128:(c + 1) * 128], func=AF.Identity,
                        scale=A1[:, c, mj:mj + 1], bias=modsb[:, 0 * 8 + c, mj:mj + 1]))
                def tm1(e):
                    ins = None
                    for bnk, (c0, c1) in enumerate(((0, 512), (512, 1024), (1024, 1536), (1536, 1568))):
                        for kk in range(8):
                            ins = e.matmul(PS[:, 1 + bnk, 0:c1 - c0], lhsT=hT[:, kk, :], rhs=win[:, kk, c0:c1], start=(kk == 0), stop=(kk == 7))
                    return ins
                k.op(pe, [hTB, wB], [psB[1], psB[2], psB[3], psB[4]], tm1)
                if is_ctx:
                    k.op(dve, [psB[1]], [tmAB], lambda e: e.tensor_copy(out=tmA[:, 0:512], in_=PS[:, 1, :]))
                    k.op(dve, [psB[2]], [tmAB], lambda e: e.tensor_copy(out=tmA[:, 512:1024], in_=PS[:, 2, :]))
                    k.op(dve, [psB[3]], [tmAB], lambda e: e.tensor_copy(out=tmA[:, 1024:1152], in_=PS[:, 3, 0:128]))
                    k.op(dve, [psB[4]], [tmAB], lambda e: e.tensor_copy(out=tmA[:, 1152:1184], in_=PS[:, 4, 0:32]))
                else:
                    lt_i = ti - NCT
                    for (psrc, nh, h0, dst0) in ((PS[:, 1, :], 16, 0, 0), (PS[:, 2, 0:256], 8, 16, 512), (PS[:, 4, 0:32], 1, 24, 1152)):
                        pv = psrc.rearrange("p (h a b f) -> p h a b f", a=2, b=2, f=8)
                        x1 = pv[:, :, :, 0, :]
                        x2 = pv[:, :, :, 1, :]
                        cos = rope[:, lt_i, 0, :].rearrange("p (a f) -> p a f", a=2).unsqueeze(1).broadcast_to([128, nh, 2, 8])
                        sin = rope[:, lt_i, 1, :].rearrange("p (a f) -> p a f", a=2).unsqueeze(1).broadcast_to([128, nh, 2, 8])
                        dv = tmA[:, dst0:dst0 + nh * 32].rearrange("p (h a b f) -> p h a b f", a=2, b=2, f=8)
                        bankB = psB[1] if h0 == 0 else (psB[2] if h0 == 16 else psB[4])
                        r0, r1, r2, r3 = (rt[i][:, h0:h0 + nh, :, :] if h0 + nh <= 24 else None for i in range(4))
                        if h0 == 24:
                            r0, r1, r2, r3 = (rt[i][:, 0:1, :, :] for i in range(4))
                        k.op(dve, [bankB, ropeB], [rtB], lambda e, x1=x1, cos=cos, r0=r0: e.tensor_tensor(out=r0, in0=x1, in1=cos, op=ALU.mult))
                        k.op(dve, [bankB, ropeB], [rtB], lambda e, x2=x2, sin=sin, r1=r1: e.tensor_tensor(out=r1, in0=x2, in1=sin, op=ALU.mult))
                        k.op(dve, [bankB, ropeB], [rtB], lambda e, x2=x2, cos=cos, r2=r2: e.tensor_tensor(out=r2, in0=x2, in1=cos, op=ALU.mult))
                        k.op(dve, [bankB, ropeB], [rtB], lambda e, x1=x1, sin=sin, r3=r3: e.tensor_tensor(out=r3, in0=x1, in1=sin, op=ALU.mult))
                        k.op(pool, [rtB], [tmAB], lambda e, dv=dv, r0=r0, r1=r1: e.tensor_tensor(out=dv[:, :, :, 0, :], in0=r0, in1=r1, op=ALU.subtract))
                        k.op(pool, [rtB], [tmAB], lambda e, dv=dv, r2=r2, r3=r3: e.tensor_tensor(out=dv[:, :, :, 1, :], in0=r2, in1=r3, op=ALU.add))
                    k.op(act, [psB[2], psB[3]], [tmAB], lambda e: e.activation(out=tmA[:, 768:1024], in_=PS[:, 2, 256:512], func=AF.Copy))
                    k.op(act, [psB[3]], [tmAB], lambda e: e.activation(out=tmA[:, 1024:1152], in_=PS[:, 3, 0:128], func=AF.Copy))
                k.dma(pool, None, tmAB, dav[t0:t0 + 128, :], tmA[:, 768:1152], "st_dav")
                rms_stats(PS[:, 3, 128:384], 1, 256, psB[3])
                rms_stats(PS[:, 3, 384:512], 2, 128, psB[3])
                k.op(dve, [psB[3], stB], [cqnB], lambda e: e.tensor_scalar(out=cqn[:, 0:256], in0=PS[:, 3, 128:384], scalar1=st[:, 1:2], scalar2=None, op0=ALU.mult))
                k.op(dve, [psB[3], stB], [cqnB], lambda e: e.tensor_scalar(out=cqn[:, 256:384], in0=PS[:, 3, 384:512], scalar1=st[:, 2:3], scalar2=None, op0=ALU.mult))
                def trA(e):
                    ins = None
                    for c in range(6):
                        ins = e.transpose(out=psbf(5)[:, c * 128:(c + 1) * 128], in_=tmA[:, c * 128:(c + 1) * 128], identity=identb[:])
                    return ins
                k.op(pe, [tmAB, identB], [psB[5]], trA)
                fb = fmo[0]
                fB = fmoB[0]
                k.op(act, [psB[5]], [fB], lambda e: e.activation(out=fb[:, 0:768], in_=psbf(5)[:, 0:768], func=AF.Copy))
                k.dma(pool, None, fB, daqT.rearrange("(c p) t -> p c t", p=128)[:, :, t0:t0 + 128], fb[:, 0:384].rearrange("p (c t) -> p c t", c=3), "st_daq")
                k.dma(pool, None, fB, dakT.rearrange("(c p) t -> p c t", p=128)[:, :, t0:t0 + 128], fb[:, 384:768].rearrange("p (c t) -> p c t", c=3), "st_dak")

                def trB(e):
                    ins = None
                    for c in range(3):
                        ins = e.transpose(out=psbf(6)[:, c * 128:(c + 1) * 128], in_=cqn[:, c * 128:(c + 1) * 128], identity=identb[:])
                    ins = e.transpose(out=psbf(6)[:, 384:512], in_=tmA[:, 1152:1280], identity=identb[:])
                    return ins
                k.op(pe, [cqnB, tmAB, identB], [psB[6]], trB)
                for c in range(3):
                    sc_ap = gcq[:, l, c:c + 1] if c < 2 else gckv[:, l, 0:1]
                    k.op(act, [psB[6], smallB], [cqTB], lambda e, c=c, sc_ap=sc_ap: e.activation(
                        out=cqT[:, c, :], in_=psbf(6)[:, c * 128:(c + 1) * 128], func=AF.Identity, scale=sc_ap))
                fb1 = fmo[1]
                fB1 = fmoB[1]
                k.op(dve, [psB[6]], [fB1], lambda e: e.tensor_copy(out=fb1[0:32, 0:128], in_=psbf(6)[0:32, 384:512]))
                k.dma(pool, None, fB1, mkrT[:, t0:t0 + 128], fb1[0:32, 0:128], "st_kr")
                def upq(e):
                    ins = None
                    for bnk, (c0, c1) in ((1, (0, 512)), (2, (512, 576))):
                        for kk in range(2):
                            ins = e.matmul(PS[:, bnk, 0:c1 - c0], lhsT=cqT[:, kk, :], rhs=wuq[:, kk, c0:c1], start=(kk == 0), stop=(kk == 1))
                    return ins
                k.op(pe, [cqTB, wB], [psB[1], psB[2]], upq)
                k.op(act, [psB[1]], [quB], lambda e: e.activation(out=qu[:, 0:5, 0:96], in_=PS[:, 1, 0:480].rearrange("p (h c) -> p h c", c=96), func=AF.Copy))
                k.op(act, [psB[1]], [quB], lambda e: e.activation(out=qu[:, 5, 0:32], in_=PS[:, 1, 480:512], func=AF.Copy))
                k.op(act, [psB[2]], [quB], lambda e: e.activation(out=qu[:, 5, 32:96], in_=PS[:, 2, 0:64], func=AF.Copy))
                if not is_ctx:
                    lt_i = ti - NCT
                    qv = qu[:, :, 64:96].rearrange("p h (a b f) -> p h a b f", a=2, b=2)
                    x1 = qv[:, :, :, 0, :]
                    x2 = qv[:, :, :, 1, :]
                    cos = rope[:, lt_i, 0, :].rearrange("p (a f) -> p a f", a=2).unsqueeze(1).broadcast_to([128, 6, 2, 8])
                    sin = rope[:, lt_i, 1, :].rearrange("p (a f) -> p a f", a=2).unsqueeze(1).broadcast_to([128, 6, 2, 8])
                    k.op(dve, [quB, ropeB], [rtB], lambda e: e.tensor_tensor(out=qrt[0][:], in0=x1, in1=cos, op=ALU.mult))
                    k.op(dve, [quB, ropeB], [rtB], lambda e: e.tensor_tensor(out=qrt[1][:], in0=x2, in1=sin, op=ALU.mult))
                    k.op(dve, [quB, ropeB], [rtB], lambda e: e.tensor_tensor(out=qrt[2][:], in0=x2, in1=cos, op=ALU.mult))
                    k.op(dve, [quB, ropeB], [rtB], lambda e: e.tensor_tensor(out=qrt[3][:], in0=x1, in1=sin, op=ALU.mult))
                    k.op(pool, [rtB], [quB], lambda e: e.tensor_tensor(out=x1, in0=qrt[0][:], in1=qrt[1][:], op=ALU.subtract))
                    k.op(pool, [rtB], [quB], lambda e: e.tensor_tensor(out=x2, in0=qrt[2][:], in1=qrt[3][:], op=ALU.add))
                def trQ(e):
                    ins = None
                    for h in range(6):
                        ins = e.transpose(out=psbf(5)[:, h * 128:(h + 1) * 128], in_=qu[:, h, :], identity=identb[:])
                    return ins
                k.op(pe, [quB, identB], [psB[5]], trQ)
                k.op(act, [psB[5]], [fB], lambda e: e.activation(out=fb[:, 0:768], in_=psbf(5)[:, 0:768], func=AF.Copy))
                k.dma(pool, None, fB, mqT.rearrange("h d t -> d h t")[:, :, t0:t0 + 128], fb[0:96, 0:768].rearrange("p (h t) -> p h t", h=6), "st_mq")
                def upk(e):
                    ins = None
                    for c in range(3):
                        ins = e.matmul(PS[:, 6, c * 128:(c + 1) * 128], lhsT=wukn[:, c * 128:(c + 1) * 128], rhs=cqT[:, 2, :], start=True, stop=True, skip_group_check=True)
                    ins = e.matmul(PS[:, 7, 0:384], lhsT=cqT[:, 2, :], rhs=wuv[:, :], start=True, stop=True)
                    return ins
                k.op(pe, [cqTB, wB], [psB[6], psB[7]], upk)
                k.op(dve, [psB[6]], [fB1], lambda e: e.tensor_copy(out=fb1[:, 128:512], in_=PS[:, 6, 0:384]))
                k.dma(pool, None, fB1, mknT.rearrange("(c p) t -> p c t", p=128)[:, :, t0:t0 + 128], fb1[:, 128:512].rearrange("p (c t) -> p c t", c=3), "st_mkn")
                k.op(act, [psB[7]], [mvtB], lambda e: e.activation(out=mvt[:], in_=PS[:, 7, 0:384], func=AF.Copy))
                k.dma(pool, None, mvtB, mv[t0:t0 + 128, :], mvt[:], "st_mv")
                def tm2(e):
                    ins = None
                    for kk in range(8):
                        ins = e.matmul(PS[:, 1, :], lhsT=hT[:, kk, :], rhs=win[:, kk, C_HV:C_HV + 512], start=(kk == 0), stop=(kk == 7))
                    return ins
                k.op(pe, [hTB, wB], [psB[1]], tm2)
                k.op(dve, [psB[1]], [hvtB], lambda e: e.tensor_copy(out=hvt[:, 0:256], in_=PS[:, 1, 0:256]))
                k.op(act, [psB[1]], [hvtB], lambda e: e.activation(out=hvt[:, 256:512], in_=PS[:, 1, 256:512], func=AF.Silu))
                k.dma(pool, None, hvtB, hvs[t0:t0 + 128, :], hvt[:, 0:256], "st_hv")
                k.dma(pool, None, hvtB, hgs[t0:t0 + 128, :], hvt[:, 256:512], "st_hg")
                def fmp(e):
                    ins = None
                    for g, c0 in enumerate((C_HQ, C_HZF, C_HZB)):
                        for h in range(4):
                            for kk in range(8):
                                ins = e.matmul(PS[:, 2 + g, h * 128:(h + 1) * 128], lhsT=win[:, kk, c0 + h * 128:c0 + (h + 1) * 128], rhs=hT[:, kk, :],
                                               start=(kk == 0), stop=(kk == 7), skip_group_check=True)
                    return ins
                k.op(pe, [hTB, wB], [psB[2], psB[3], psB[4]], fmp)
                k.op(dve, [psB[2]], [qTsB], lambda e: e.tensor_copy(out=qTs[:].rearrange("p h t -> p (h t)"), in_=PS[:, 2, :]))
                for d_ in range(2):
                    k.op(act, [psB[3 + d_]], [hgB], lambda e, d_=d_: e.activation(
                        out=sig[:, d_ * 4:(d_ + 1) * 4, :].rearrange("p h t -> p (h t)"), in_=PS[:, 3 + d_, :], func=AF.Sigmoid))
                    k.op(act, [psB[3 + d_]], [hgB], lambda e, d_=d_: e.activation(
                        out=sgn[:, d_ * 4:(d_ + 1) * 4, :].rearrange("p h t -> p (h t)"), in_=PS[:, 3 + d_, :], func=AF.Sigmoid, scale=-1.0))
                for dh in range(8):
                    k.op(dve, [hgB, lbB], [hgB], lambda e, dh=dh: e.tensor_scalar(
                        out=sig[:, dh, :], in0=sig[:, dh, :], scalar1=oml[:, dh, l:l + 1], scalar2=lb[:, dh, l:l + 1], op0=ALU.mult, op1=ALU.add))
                k.op(act, [hgB], [hgB], lambda e: e.activation(out=gT[:], in_=sig[:], func=AF.Ln))
                for dh in range(8):
                    k.op(pool, [hgB, lbB], [hgB], lambda e, dh=dh: e.tensor_scalar(
                        out=sgn[:, dh, :], in0=sgn[:, dh, :], scalar1=oml[:, dh, l:l + 1], scalar2=None, op0=ALU.mult))
                for d_ in range(2):
                    mi = 2 + 3 * d_
                    bT, bC, bE = (0, 1, 2) if d_ == 0 else (4, 5, 6)
                    sg4 = sgn[:, d_ * 4:(d_ + 1) * 4, :].rearrange("p h t -> p (h t)")
                    def trg(e, d_=d_, bT=bT):
                        ins = None
                        for h in range(4):
                            ins = e.transpose(out=PS[:, bT, h * 128:(h + 1) * 128], in_=gT[:, d_ * 4 + h, :], identity=ident_f)
                        return ins
                    k.op(pe, [hgB, cstB], [psB[bT]], trg)
                    k.op(dve, [psB[bT]], [gtmB[d_]], lambda e, d_=d_, bT=bT: e.tensor_copy(out=gtm4[d_][:], in_=PS[:, bT, :]))
                    def cum(e, d_=d_, mi=mi, bC=bC, bE=bE):
                        ins = None
                        for h in range(4):
                            e.matmul(PS[:, bC, h * 128:(h + 1) * 128], lhsT=gtm4[d_][:, h * 128:(h + 1) * 128], rhs=cst[:, mi, :], start=True, stop=True, skip_group_check=True)
                            ins = e.matmul(PS[:, bE, h * 128:(h + 1) * 128], lhsT=gtm4[d_][:, h * 128:(h + 1) * 128], rhs=cst[:, mi + 1, :], start=True, stop=True, skip_group_check=True)
                        return ins
                    k.op(pe, [gtmB[d_], cstB], [psB[bC], psB[bE]], cum)
                    k.op(act, [psB[bC]], [eB[d_]], lambda e, d_=d_, bC=bC: e.activation(out=eq4[d_][:], in_=PS[:, bC, :], func=AF.Exp))
                    k.op(act, [psB[bC]], [eB[d_]], lambda e, d_=d_, bC=bC: e.activation(out=ek4[d_][:], in_=PS[:, bC, :], func=AF.Exp, scale=-1.0))
                    k.op(act, [psB[bE]], [eB[d_]], lambda e, d_=d_, bE=bE: e.activation(out=e04[d_][:], in_=PS[:, bE, :], func=AF.Exp))
                    k.op(dve, [eB[d_], qTsB], [qkB[d_]], lambda e, d_=d_: e.tensor_tensor(out=q0T4[d_][:], in0=qTs[:].rearrange("p h t -> p (h t)"), in1=eq4[d_][:], op=ALU.mult))
                    k.op(dve, [eB[d_], hgB], [qkB[d_]], lambda e, d_=d_, sg4=sg4: e.tensor_tensor(out=k1T4[d_][:], in0=sg4, in1=ek4[d_][:], op=ALU.mult))
                    k.op(pool, [eB[d_], hgB], [qkB[d_]], lambda e, d_=d_, sg4=sg4: e.tensor_tensor(out=k0T4[d_][:], in0=sg4, in1=e04[d_][:], op=ALU.mult))
                    off = 31 if d_ == 0 else 0
                    k.op(dve, [eB[d_]], [dktB[d_]], lambda e, d_=d_, off=off: e.tensor_copy(
                        out=dkt4[d_][:], in_=eq4[d_][:].rearrange("p (h c j) -> p h c j", h=4, c=4)[:, :, :, off]))
                    ch0 = t0 // 32
                    k.dma(pool, None, dktB[d_], hdk[d_].rearrange("h k c -> k h c")[:, :, ch0:ch0 + 4], dkt4[d_][:], "st_dk")
                    k.dma(pool, None, qkB[d_], hq0T[d_].rearrange("h k t -> k h t")[:, :, t0:t0 + 128], q0T4[d_][:].rearrange("p (h t) -> p h t", h=4), "st_q0")
                    def trk(e, d_=d_, bT=bT):
                        ins = None
                        for h in range(4):
                            ins = e.transpose(out=psbf(bT)[:, h * 128:(h + 1) * 128], in_=k0T4[d_][:, h * 128:(h + 1) * 128], identity=identb[:])
                        return ins
                    k.op(pe, [qkB[d_], identB], [psB[bT]], trk)
                    k.op(act, [psB[bT]], [k0tmB[d_]], lambda e, d_=d_, bT=bT: e.activation(out=k0tm4[d_][:], in_=psbf(bT)[:, 0:512], func=AF.Copy))
                    k.dma(pool, None, k0tmB[d_], hk0[d_].rearrange("h t k -> t h k")[t0:t0 + 128, :, :], k0tm4[d_][:].rearrange("p (h k) -> p h k", h=4), "st_k0")
                    def mat(e, d_=d_, bC=bC):
                        ins = None
                        for h in range(4):
                            ins = e.matmul(PS[:, bC, h * 128:(h + 1) * 128], lhsT=k1T4[d_][:, h * 128:(h + 1) * 128], rhs=q0T4[d_][:, h * 128:(h + 1) * 128], start=True, stop=True, skip_group_check=True)
                        return ins
                    k.op(pe, [qkB[d_]], [psB[bC]], mat)
                    k.op(dve, [psB[bC], cstB], [atmB[d_]], lambda e, d_=d_, mi=mi, bC=bC: e.tensor_tensor(
                        out=atm4[d_][:].rearrange("p (h t) -> p h t", h=4), in0=PS[:, bC, :].rearrange("p (h t) -> p h t", h=4),
                        in1=cst[:, mi + 2, :].unsqueeze(1).broadcast_to([128, 4, 128]), op=ALU.mult))
                    def mo_(e, d_=d_):
                        ins = None
                        for h in range(4):
                            ins = e.matmul(PS[:, 7, h * 64:(h + 1) * 64], lhsT=atm4[d_][:, h * 128:(h + 1) * 128], rhs=hvt[:, h * 64:(h + 1) * 64],
                                           start=(d_ == 0 and h == 0), stop=(d_ == 1), skip_group_check=True)
                        return ins
                    k.op(pe, [atmB[d_], hvtB], [psB[7]], mo_)
                k.op(dve, [psB[7]], [oitB[ti % 2]], lambda e, ti=ti: e.tensor_copy(out=oit[ti % 2][:], in_=PS[:, 7, 0:256]))
                k.dma(pool, None, oitB[ti % 2], hoi[t0:t0 + 128, :], oit[ti % 2][:], ("st_oi", ti % 2))
            k.barrier()
        chk("P1")
        chk("P1a")
        chk("P1b")

        with ExitStack() as ph:
            def psb_(name, shape, dt):
                return ph.enter_context(SBT(name, list(shape), dt))
            o_acc = psb_("o_acc", [128, NTT, 256], F32)
            oaccB = [Buf(None) for _ in range(NTT)]
            for g0 in range(0, NTT, 8):
                g1 = min(NTT, g0 + 8)
                tk = k.dma(sp, oaccB[g0], None, o_acc[:, g0:g1, :], hoi[g0 * 128:g1 * 128, :].rearrange("(t p) c -> p t c", p=128), "ld_oacc")
                for ti in range(g0, g1):
                    oaccB[ti].w = [tk]

            S32 = psb_("S32", [128, 8, 64], F32)
            S16 = psb_("S16", [128, 8, 64], BF16)
            SB = [Buf(None) for _ in range(2)]
            for d_ in range(2):
                k.op(dve, [], [SB[d_]], lambda e, d_=d_: e.memset(S32[:, d_ * 4:(d_ + 1) * 4, :], 0.0))
                k.op(dve, [], [SB[d_]], lambda e, d_=d_: e.memset(S16[:, d_ * 4:(d_ + 1) * 4, :], 0.0))
            NB2 = 2
            lq = [[psb_("lq%d_%d" % (d_, i), [128, 4, 128], BF16) for i in range(NB2)] for d_ in range(2)]
            lk = [[psb_("lk%d_%d" % (d_, i), [128, 4, 128], BF16) for i in range(NB2)] for d_ in range(2)]
            ld = [[psb_("ld%d_%d" % (d_, i), [128, 4, 4], F32) for i in range(NB2)] for d_ in range(2)]
            lv = [[psb_("lv%d_%d" % (d_, i), [128, 256], BF16) for i in range(NB2)] for d_ in range(2)]
            lB = [[Buf(None) for i in range(NB2)] for d_ in range(2)]
            vm = [[psb_("vm%d_%d" % (d_, i), [128, 4, 256], BF16) for i in range(NB2)] for d_ in range(2)]
            vmB = [[Buf(None) for i in range(NB2)] for d_ in range(2)]
            pbank = {0: (0, 1), 1: (2, 3)}
            for s_ in range(NTT):
                for d_ in range(2):
                    if d_ == 0:
                        ti = s_
                    else:
                        ti = (NCT - 1 - s_) if s_ < NCT else (NTT - 1 - (s_ - NCT))
                    t0 = ti * 128
                    bi = s_ % NB2
                    B_ = lB[d_][bi]
                    k.dma(sp, B_, None, lq[d_][bi][:], hq0T[d_].rearrange("h k t -> k h t")[:, :, t0:t0 + 128], ("l2", d_, bi))
                    k.dma(sp, B_, None, lk[d_][bi][:], hk0[d_].rearrange("h t k -> t h k")[t0:t0 + 128, :, :], ("l2", d_, bi))
                    k.dma(sp, B_, None, ld[d_][bi][:], hdk[d_].rearrange("h k c -> k h c")[:, :, t0 // 32:t0 // 32 + 4], ("l2", d_, bi))
                    k.dma(sp, B_, None, lv[d_][bi][:], hvs[t0:t0 + 128, :], ("l2", d_, bi))
                    for c in range(4):
                        k.op(pool, [B_, cstB], [vmB[d_][bi]], lambda e, c=c, d_=d_, bi=bi: e.tensor_scalar(
                            out=vm[d_][bi][:, c, :], in0=lv[d_][bi][:], scalar1=cst[:, 8, c:c + 1], scalar2=None, op0=ALU.mult))
                    bo, bs = pbank[d_]
                    Sd32 = S32[:, d_ * 4:(d_ + 1) * 4, :]
                    Sd16 = S16[:, d_ * 4:(d_ + 1) * 4, :]
                    for c in (range(4) if d_ == 0 else range(3, -1, -1)):
                        def mo(e, d_=d_, bi=bi, bo=bo):
                            ins = None
                            for h in range(4):
                                ins = e.matmul(PS[:, bo, h * 64:(h + 1) * 64], lhsT=lq[d_][bi][:, h, :], rhs=S16[:, d_ * 4 + h, :], start=True, stop=True, skip_group_check=True)
                            return ins
                        k.op(pe, [B_, SB[d_]], [psB[bo]], mo)
                        k.op(dve, [psB[bo]], [oaccB[ti]], lambda e, bo=bo, c=c, ti=ti: e.tensor_tensor(
                            out=o_acc[c * 32:(c + 1) * 32, ti, :], in0=o_acc[c * 32:(c + 1) * 32, ti, :],
                            in1=PS[c * 32:(c + 1) * 32, bo, 0:256], op=ALU.add))
                        def ms(e, d_=d_, bi=bi, bs=bs, c=c):
                            ins = None
                            for h in range(4):
                                ins = e.matmul(PS[:, bs, h * 64:(h + 1) * 64], lhsT=lk[d_][bi][:, h, :], rhs=vm[d_][bi][:, c, h * 64:(h + 1) * 64], start=True, stop=True, skip_group_check=True)
                            return ins
                        k.op(pe, [B_, vmB[d_][bi]], [psB[bs]], ms)
                        k.op(dve, [B_, SB[d_]], [SB[d_]], lambda e, Sd32=Sd32, d_=d_, bi=bi, c=c: e.tensor_tensor(
                            out=Sd32, in0=Sd32, in1=ld[d_][bi][:, :, c:c + 1].broadcast_to([128, 4, 64]), op=ALU.mult))
                        k.op(dve, [psB[bs], SB[d_]], [SB[d_]], lambda e, Sd32=Sd32, bs=bs: e.tensor_tensor(
                            out=Sd32, in0=Sd32, in1=PS[:, bs, 0:256].rearrange("p (h v) -> p h v", h=4), op=ALU.add))
                        k.op(act, [SB[d_]], [SB[d_]], lambda e, Sd32=Sd32, Sd16=Sd16: e.activation(out=Sd16, in_=Sd32, func=AF.Copy))
            lg = [psb_("lg%d" % i, [128, 256], BF16) for i in range(2)]
            lgB = [Buf(None) for i in range(2)]
            sq = psb_("sq", [128, 4, 64], F32)
            sqB = Buf(None)
            ss = psb_("ss", [128, 4], F32)
            yo = [psb_("yo%d" % i, [128, 256], BF16) for i in range(2)]
            yoB = [Buf(None) for i in range(2)]
            for ti in range(NTT):
                if last and ti < NCT:
                    continue
                t0 = ti * 128
                bi = ti % 2
                k.dma(sp, lgB[bi], None, lg[bi][:], hgs[t0:t0 + 128, :], ("lg", bi))
                ov = o_acc[:, ti, :].rearrange("p (h v) -> p h v", h=4)
                k.op(pool, [oaccB[ti]], [sqB], lambda e, ov=ov: e.tensor_tensor(out=sq[:], in0=ov, in1=ov, op=ALU.mult))
                k.op(dve, [sqB], [sqB], lambda e: e.tensor_reduce(out=ss[:], in_=sq[:], axis=AX.X, op=ALU.add))
                k.op(act, [sqB, epsB], [sqB], lambda e: e.activation(out=ss[:], in_=ss[:], func=AF.Ln, scale=1.0 / 64, bias=eps_t[:]))
                k.op(act, [sqB], [sqB], lambda e: e.activation(out=ss[:], in_=ss[:], func=AF.Exp, scale=-0.5))
                k.op(dve, [sqB, oaccB[ti]], [sqB], lambda e, ov=ov: e.tensor_tensor(out=sq[:], in0=ov, in1=ss[:].unsqueeze(2).broadcast_to([128, 4, 64]), op=ALU.mult))
                k.op(dve, [sqB, smallB], [sqB], lambda e: e.tensor_tensor(out=sq[:], in0=sq[:], in1=hgg[:, l, :].unsqueeze(1).broadcast_to([128, 4, 64]), op=ALU.mult))
                k.op(dve, [sqB, lgB[bi]], [yoB[bi]], lambda e, bi=bi: e.tensor_tensor(out=yo[bi][:], in0=sq[:].rearrange("p h v -> p (h v)"), in1=lg[bi][:], op=ALU.mult))
                k.dma(pool, None, yoB[bi], ycat[t0:t0 + 128, 768:1024], yo[bi][:], ("st_yo", bi))
            k.barrier()
        chk("P2")

        for which in ("da", "mla"):
            with ExitStack() as ph:
                def psb_(name, shape, dt):
                    return ph.enter_context(SBT(name, list(shape), dt))
                if which == "da":
                    units = [(c, hh) for c in range(3) for hh in range(2)]
                    scale = 32 ** -0.5
                else:
                    units = [(h, 0) for h in range(6)]
                    scale = 96 ** -0.5
                nacc = 2 if which == "da" else 1
                KT = psb_("KT", [128, T], BF16)
                KTB = Buf(KT)
                QM = [psb_("QM%d" % j, [128, T], BF16) for j in range(4 if which == "da" else 1)]
                QMB = Buf(None)
                VA = psb_("VA", [128, NTT, 2, 65], BF16)
                VAB = Buf(VA)
                if which == "da":
                    for j in range(4):
                        k.op(pool, [], [QMB], lambda e, j=j: e.memset(QM[j][:], 0.0))
                k.op(pool, [], [VAB], lambda e: e.memset(VA[:], 1.0))
                NPB = 3
                pt = [psb_("pt%d" % i, [128, 2, 512], BF16) for i in range(NPB)]
                ptB = [Buf(None) for i in range(NPB)]
                osb = psb_("osb", [65, 2, 512], F32)
                osbB = Buf(None)
                otm = [psb_("otm%d" % i, [128, 2, 65], F32) for i in range(4)]
                rr = [psb_("rr%d" % i, [128, 4], F32) for i in range(4)]
                oo = [psb_("oo%d" % i, [128, 64], F32) for i in range(4)]
                o2 = psb_("o2", [128, 64], F32)
                otmB = [Buf(None) for i in range(4)]
                yb = [psb_("yb%d" % i, [128, 64], BF16) for i in range(4)]
                ybB = [Buf(None) for i in range(4)]
                sbank = [(0, 1), (2, 3)]
                asets = [(4, 5), (6, 7)]
                chunk_ctr = [0]
                pend = []
                cur_chunk = -1

                def make_epilogue(aset, q0, nq, ycol):
                    nsub = nq // 128

                    def stage1():
                        for cm in range(nacc):
                            k.op(dve, [psB[aset[cm]]], [osbB], lambda e, cm=cm: e.tensor_copy(out=osb[:, cm, 0:nq], in_=PS[0:65, aset[cm], 0:nq]))
                        for qs in range(nsub):
                            def trO(e, qs=qs):
                                ins = None
                                for cm in range(nacc):
                                    ins = e.transpose(out=PS[:, aset[cm], qs * 66:qs * 66 + 65], in_=osb[:, cm, qs * 128:(qs + 1) * 128], identity=cst[0:65, 0, 0:65])
                                return ins
                            k.op(pe, [osbB, cstB], [psB[aset[i]] for i in range(nacc)], trO)
                        for qs in range(nsub):
                            for cm in range(nacc):
                                k.op(dve, [psB[aset[cm]]], [otmB[qs]], lambda e, qs=qs, cm=cm: e.tensor_copy(
                                    out=otm[qs][:, cm, :], in_=PS[:, aset[cm], qs * 66:qs * 66 + 65]))
                            if which == "da":
                                k.op(dve, [otmB[qs]], [otmB[qs]], lambda e, qs=qs: e.reciprocal(out=rr[qs][:, 0:2], in_=otm[qs][:, :, 64]))
                                k.op(dve, [otmB[qs], modB], [otmB[qs]], lambda e, qs=qs: e.tensor_tensor(out=rr[qs][:, 1:2], in0=rr[qs][:, 1:2], in1=lam_t[:, 0:1], op=ALU.mult))
                                k.op(dve, [otmB[qs]], [otmB[qs]], lambda e, qs=qs: e.tensor_scalar(out=oo[qs][:], in0=otm[qs][:, 0, 0:64], scalar1=rr[qs][:, 0:1], scalar2=None, op0=ALU.mult))
                                k.op(dve, [otmB[qs]], [otmB[qs]], lambda e, qs=qs: e.scalar_tensor_tensor(out=oo[qs][:], in0=otm[qs][:, 1, 0:64], scalar=rr[qs][:, 1:2], in1=oo[qs][:], op0=ALU.mult, op1=ALU.add))
                                k.op(dve, [otmB[qs]], [otmB[qs]], lambda e, qs=qs: e.tensor_tensor(out=o2[:], in0=oo[qs][:], in1=oo[qs][:], op=ALU.mult))
                                k.op(dve, [otmB[qs]], [otmB[qs]], lambda e, qs=qs: e.tensor_reduce(out=rr[qs][:, 2:3], in_=o2[:], axis=AX.X, op=ALU.add))
                            else:
                                k.op(dve, [otmB[qs]], [otmB[qs]], lambda e, qs=qs: e.reciprocal(out=rr[qs][:, 0:1], in_=otm[qs][:, 0, 64:65]))

                    def stage2():
                        for qs in range(nsub):
                            if which == "da":
                                k.op(act, [otmB[qs], epsB], [otmB[qs]], lambda e, qs=qs: e.activation(out=rr[qs][:, 2:3], in_=rr[qs][:, 2:3], func=AF.Ln, scale=1.0 / 64, bias=eps_t[:]))
                                k.op(act, [otmB[qs]], [otmB[qs]], lambda e, qs=qs: e.activation(out=rr[qs][:, 2:3], in_=rr[qs][:, 2:3], func=AF.Exp, scale=-0.5))
                        for qs in range(nsub):
                            if which == "da":
                                k.op(dve, [otmB[qs], modB], [ybB[qs]], lambda e, qs=qs: e.scalar_tensor_tensor(
                                    out=yb[qs][:], in0=oo[qs][:], scalar=rr[qs][:, 2:3], in1=dagl[:], op0=ALU.mult, op1=ALU.mult))
                            else:
                                k.op(dve, [otmB[qs]], [ybB[qs]], lambda e, qs=qs: e.tensor_scalar(out=yb[qs][:], in0=otm[qs][:, 0, 0:64], scalar1=rr[qs][:, 0:1], scalar2=None, op0=ALU.mult))
                            r0 = q0 + qs * 128
                            k.dma(pool, None, ybB[qs], ycat[r0:r0 + 128, ycol:ycol + 64], yb[qs][:], ("st_y", qs))
                    return [stage1, stage2]

                for (c, hh) in units:
                    if which == "da":
                        if c != cur_chunk:
                            cur_chunk = c
                            k.dma(sp, KTB, None, KT[:], dakT[c * 128:(c + 1) * 128, :], "ld_kt")
                            for j in range(4):
                                k.dma(sp, QMB, None, QM[j][j * 32:(j + 1) * 32, :], daqT[c * 128 + j * 32:c * 128 + (j + 1) * 32, :], "ld_qm")
                            for g0 in range(0, NTT, 8):
                                g1 = min(NTT, g0 + 8)
                                for h2 in range(2):
                                    k.dma(sp, VAB, None, VA[:, g0:g1, h2, 0:64],
                                          dav[g0 * 128:g1 * 128, c * 128 + h2 * 64:c * 128 + (h2 + 1) * 64].rearrange("(kt p) e -> p kt e", p=128), "ld_va")
                        head = c * 2 + hh
                        ycol = head * 64
                    else:
                        h = c
                        k.dma(sp, KTB, None, KT[0:64, :], mknT[h * 64:(h + 1) * 64, :], "ld_kt")
                        k.dma(sp, KTB, None, KT[64:96, :], mkrT[:, :], "ld_kt")
                        k.dma(sp, QMB, None, QM[0][0:96, :], mqT[h, :, :], "ld_qm")
                        for g0 in range(0, NTT, 8):
                            g1 = min(NTT, g0 + 8)
                            k.dma(sp, VAB, None, VA[:, g0:g1, 0, 0:64],
                                  mv[g0 * 128:g1 * 128, h * 64:(h + 1) * 64].rearrange("(kt p) e -> p kt e", p=128), "ld_va")
                        ycol = 384 + h * 64
                    qchunks = []
                    if not last:
                        qchunks.append((0, CTX, NCT))
                    for q0 in range(CTX, T, 512):
                        qchunks.append((q0, min(512, T - q0), NTT))
                    for (q0, nq, nkt) in qchunks:
                        aset = asets[chunk_ctr[0] % 2]
                        chunk_ctr[0] += 1
                        nit = nkt if which == "da" else nkt // 2

                        def scores(it):
                            sbk = sbank[it % 2]
                            def f(e):
                                ins = None
                                for cm in range(2):
                                    if which == "da":
                                        rhs = QM[hh * 2 + cm][:, q0:q0 + nq]
                                        lhsT = KT[:, it * 128:(it + 1) * 128]
                                    else:
                                        kt = it * 2 + cm
                                        rhs = QM[0][0:96, q0:q0 + nq]
                                        lhsT = KT[0:96, kt * 128:(kt + 1) * 128]
                                    ins = e.matmul(PS[:, sbk[cm], 0:nq], lhsT=lhsT, rhs=rhs, start=True, stop=True)
                                return ins
                            k.op(pe, [KTB, QMB], [psB[sbk[0]], psB[sbk[1]]], f)

                        def expo(it):
                            sbk = sbank[it % 2]
                            pb = it % NPB
                            k.op(act, [psB[sbk[0]], psB[sbk[1]]], [ptB[pb]], lambda e: e.activation(
                                out=pt[pb][:, :, 0:nq], in_=PS[:, sbk[0]:sbk[0] + 2, 0:nq], func=AF.Exp, scale=scale))

                        def pv(it):
                            pb = it % NPB
                            def f(e):
                                ins = None
                                for cm in range(2):
                                    if which == "da":
                                        vsel = VA[:, it, hh, :]
                                        ab = aset[cm]
                                        st_, sp_ = (it == 0), (it == nit - 1)
                                    else:
                                        kt = it * 2 + cm
                                        vsel = VA[:, kt, 0, :]
                                        ab = aset[0]
                                        st_, sp_ = (kt == 0), (kt == nkt - 1)
                                    ins = e.matmul(PS[0:65, ab, 0:nq], lhsT=vsel, rhs=pt[pb][:, cm, 0:nq], start=st_, stop=sp_)
                                return ins
                            k.op(pe, [VAB, ptB[pb]], [psB[aset[i]] for i in range(nacc)] if it == 0 else [], f)
                        e1 = min(4, nit + 1)
                        e2 = min(24, nit + 1)
                        for it in range(nit + 2):
                            if it < nit:
                                scores(it)
                                expo(it)
                            if it >= 2:
                                pv(it - 2)
                            if pend and it == e1:
                                pend[0][0]()
                            if pend and it == e2:
                                pend[0][1]()
                                pend.pop(0)
                        tok_last = (pe.sem, pe.cnt, pe)
                        for i in range(nacc):
                            psB[aset[i]].w = [tok_last]
                            psB[aset[i]].r = []
                        pend.append(make_epilogue(aset, q0, nq, ycol))
                while pend:
                    pend[0][0]()
                    pend[0][1]()
                    pend.pop(0)
                k.barrier()
            chk("P3" if which == "da" else "P4")

        tiles = list(range(NTT)) if not last else list(range(NCT, NTT))
        with ExitStack() as ph:
            def psb_(name, shape, dt):
                return ph.enter_context(SBT(name, list(shape), dt))
            wo = psb_("wo", [128, 8, D], BF16)
            wB = Buf(None)
            with ExitStack() as st_:
                stg = [st_.enter_context(SBT("stg5a_%d" % i, [128, D], F32)) for i in range(2)]
                stgB = [Buf(None) for i in range(2)]
                for kk in range(8):
                    bi = kk % 2
                    k.dma(sp, stgB[bi], None, stg[bi][:, :], w_out[l, kk * 128:(kk + 1) * 128, :], ("stg5a", bi))
                    k.op(dve, [stgB[bi]], [wB], lambda e, kk=kk, bi=bi: e.tensor_copy(out=wo[:, kk, :], in_=stg[bi][:, :]))
            k.barrier()
            gate_bc, gateB = make_gates(psb_, 2)
            yt = [psb_("yt%d" % i, [128, D], BF16) for i in range(2)]
            ytB = [Buf(None) for i in range(2)]
            yT = psb_("yT", [128, 8, 128], BF16)
            yTB = Buf(None)
            xt = [psb_("x5_%d" % i, [128, D], F32) for i in range(2)]
            xtB = [Buf(None) for i in range(2)]
            junk = psb_("junk5", [128, D], BF16)
            junkB = Buf(None)
            junk5f = psb_("junk5f", [128, D], F32)
            j5B = Buf(None)
            st = psb_("st5", [128, 4], F32)
            stB = Buf(None)
            xn = psb_("xn5", [128, D], BF16)
            xnB = Buf(None)
            h2T = [psb_("h2T%d" % i, [128, 8, 128], BF16) for i in range(2)]
            h2B = [Buf(None) for i in range(2)]
            for idx, ti in enumerate(tiles):
                is_ctx = ti < NCT
                mj = 1 if is_ctx else 0
                t0 = ti * 128
                bi = idx % 2
                xsrc = xc_res[t0:t0 + 128, :] if is_ctx else x_src[t0 - CTX:t0 - CTX + 128, :]
                xdst = xc_res[t0:t0 + 128, :] if is_ctx else out[t0 - CTX:t0 - CTX + 128, :]
                k.dma(sp, ytB[bi], None, yt[bi][:], ycat[t0:t0 + 128, :], ("yt", bi))
                k.dma(sp, xtB[bi], None, xt[bi][:], xsrc, ("x5", bi))

                def tr8(e, bi=bi):
                    ins = None
                    for c in range(8):
                        ins = e.transpose(out=psbf(0)[:, c * 128:(c + 1) * 128], in_=yt[bi][:, c * 128:(c + 1) * 128], identity=identb[:])
                    return ins
                k.op(pe, [ytB[bi], identB], [psB[0]], tr8)
                k.op(act, [psB[0]], [yTB], lambda e: e.activation(out=yT[:].rearrange("p c t -> p (c t)"), in_=psbf(0)[:, :], func=AF.Copy))

                def mmo(e):
                    ins = None
                    for half in range(2):
                        for kk in range(8):
                            ins = e.matmul(PS[:, 1 + half, :], lhsT=yT[:, kk, :], rhs=wo[:, kk, half * 512:(half + 1) * 512], start=(kk == 0), stop=(kk == 7))
                    return ins
                k.op(pe, [yTB, wB], [psB[1], psB[2]], mmo)
                xb_ = xt[bi]
                xB_ = xtB[bi]
                for half in range(2):
                    k.op(dve, [psB[1 + half], gateB], [j5B], lambda e, half=half, xb_=xb_: e.tensor_tensor(
                        out=junk5f[:, half * 512:(half + 1) * 512], in0=PS[:, 1 + half, :], in1=gate_bc[:, mj, half * 512:(half + 1) * 512], op=ALU.mult))
                k.op(pool, [xB_, j5B], [xB_], lambda e, xb_=xb_: e.tensor_tensor(out=xb_[:], in0=xb_[:], in1=junk5f[:], op=ALU.add))
                k.dma(pool, None, xB_, xdst, xb_[:], ("st_x1", bi))
                k.op(act, [xB_], [junkB, stB], lambda e, xb_=xb_: e.activation(out=junk[:], in_=xb_[:], func=AF.Square, accum_out=st[:, 0:1]))
                k.op(act, [stB, epsB], [stB], lambda e: e.activation(out=st[:, 0:1], in_=st[:, 0:1], func=AF.Ln, scale=1.0 / D, bias=eps_t[:]))
                k.op(act, [stB], [stB], lambda e: e.activation(out=st[:, 0:1], in_=st[:, 0:1], func=AF.Exp, scale=-0.5))
                k.op(dve, [xB_, stB], [xnB], lambda e, xb_=xb_: e.tensor_scalar(out=xn[:], in0=xb_[:], scalar1=st[:, 0:1], scalar2=None, op0=ALU.mult))

                def tr8b(e):
                    ins = None
                    for c in range(8):
                        ins = e.transpose(out=psbf(3)[:, c * 128:(c + 1) * 128], in_=xn[:, c * 128:(c + 1) * 128], identity=identb[:])
                    return ins
                k.op(pe, [xnB, identB], [psB[3]], tr8b)
                for c in range(8):
                    k.op(act, [psB[3], modB], [h2B[bi]], lambda e, c=c, bi=bi: e.activation(
                        out=h2T[bi][:, c, :], in_=psbf(3)[:, c * 128:(c + 1) * 128], func=AF.Identity,
                        scale=A2[:, c, mj:mj + 1], bias=modsb[:, 3 * 8 + c, mj:mj + 1]))
                k.dma(pool, None, h2B[bi], h2s.rearrange("c p t -> p c t")[:, :, t0:t0 + 128], h2T[bi][:], ("st_h2", bi))
            k.barrier()
        chk("P5a")

        with ExitStack() as ph:
            def psb_(name, shape, dt):
                return ph.enter_context(SBT(name, list(shape), dt))
            w1 = psb_("w1", [128, 8, D_FF], BF16)
            w2 = psb_("w2", [128, 32, D], BF16)
            wB = Buf(None)
            with ExitStack() as st_:
                stg = [st_.enter_context(SBT("stg5b_%d" % i, [128, 2048], F32)) for i in range(2)]
                stgB = [Buf(None) for i in range(2)]
                cnt = 0
                for kk in range(8):
                    for hf in range(2):
                        bi = cnt % 2
                        cnt += 1
                        k.dma(sp, stgB[bi], None, stg[bi][:, :], w_ff1[l, kk * 128:(kk + 1) * 128, hf * 2048:(hf + 1) * 2048], ("stg5b", bi))
                        k.op(pool if cnt % 2 else dve, [stgB[bi]], [wB], lambda e, kk=kk, bi=bi, hf=hf: e.tensor_copy(out=w1[:, kk, hf * 2048:(hf + 1) * 2048], in_=stg[bi][:, :]))
                for kk in range(0, 32, 2):
                    bi = cnt % 2
                    cnt += 1
                    k.dma(sp, stgB[bi], None, stg[bi][:, :].rearrange("p (a e) -> p a e", a=2),
                          w_ff2[l, kk * 128:(kk + 2) * 128, :].rearrange("(a p) e -> p a e", p=128), ("stg5b", bi))
                    k.op(pool if cnt % 2 else dve, [stgB[bi]], [wB], lambda e, kk=kk, bi=bi: e.tensor_copy(
                        out=w2[:, kk:kk + 2, :], in_=stg[bi][:, :].rearrange("p (a e) -> p a e", a=2)))
            k.barrier()
            gate_bc, gateB = make_gates(psb_, 5)
            if last:
                gfin = psb_("gfin", [128, D], F32)
                gfinB = Buf(None)
                k.dma(sp, gfinB, None, gfin[:], g_final.partition_broadcast(128), "gfin")
            h2T = [psb_("h2Tb%d" % i, [128, 8, 128], BF16) for i in range(2)]
            h2B = [Buf(None) for i in range(2)]
            xt = [psb_("x5b_%d" % i, [128, D], F32) for i in range(2)]
            xtB = [Buf(None) for i in range(2)]
            rl = [psb_("rl%d" % i, [128, 512], F32) for i in range(2)]
            rlB = [Buf(None) for i in range(2)]
            uT = psb_("uT", [128, 32, 128], BF16)
            uTB = Buf(None)
            gp = psb_("gp", [128, D], F32)
            gpB = Buf(None)
            junk = psb_("junk5b", [128, D], BF16)
            junkB = Buf(None)
            st = psb_("st5b", [128, 4], F32)
            stB = Buf(None)
            for idx, ti in enumerate(tiles):
                is_ctx = ti < NCT
                mj = 1 if is_ctx else 0
                t0 = ti * 128
                bi = idx % 2
                xdst = xc_res[t0:t0 + 128, :] if is_ctx else out[t0 - CTX:t0 - CTX + 128, :]
                k.dma(sp, h2B[bi], None, h2T[bi][:], h2s.rearrange("c p t -> p c t")[:, :, t0:t0 + 128], ("ld_h2", bi))
                k.dma(sp, xtB[bi], None, xt[bi][:], xdst, ("x5b", bi))
                for g in range(8):
                    bank = 3 + g % 4
                    def f1(e, g=g, bank=bank, bi=bi):
                        ins = None
                        for j in range(4):
                            fc = g * 4 + j
                            for kk in range(8):
                                ins = e.matmul(PS[:, bank, j * 128:(j + 1) * 128], lhsT=w1[:, kk, fc * 128:(fc + 1) * 128], rhs=h2T[bi][:, kk, :],
                                               start=(kk == 0), stop=(kk == 7), skip_group_check=True)
                        return ins
                    k.op(pe, [h2B[bi], wB], [psB[bank]], f1)
                    k.op(act, [psB[bank]], [rlB[g % 2]], lambda e, g=g, bank=bank: e.activation(out=rl[g % 2][:], in_=PS[:, bank, :], func=AF.Relu))
                    k.op(dve if g % 2 else pool, [rlB[g % 2]], [uTB], lambda e, g=g: e.tensor_tensor(
                        out=uT[:, g * 4:(g + 1) * 4, :].rearrange("p c t -> p (c t)"), in0=rl[g % 2][:], in1=rl[g % 2][:], op=ALU.mult))
                def f2(e):
                    ins = None
                    for half in range(2):
                        for fc in range(32):
                            ins = e.matmul(PS[:, 1 + half, :], lhsT=uT[:, fc, :], rhs=w2[:, fc, half * 512:(half + 1) * 512], start=(fc == 0), stop=(fc == 31))
                    return ins
                k.op(pe, [uTB, wB], [psB[1], psB[2]], f2)
                xb_ = xt[bi]
                xB_ = xtB[bi]
                for half in range(2):
                    k.op(dve, [psB[1 + half], gateB], [gpB], lambda e, half=half: e.tensor_tensor(
                        out=gp[:, half * 512:(half + 1) * 512], in0=PS[:, 1 + half, :], in1=gate_bc[:, mj, half * 512:(half + 1) * 512], op=ALU.mult))
                k.op(pool, [gpB, xB_], [xB_], lambda e, xb_=xb_: e.tensor_tensor(out=xb_[:], in0=xb_[:], in1=gp[:], op=ALU.add))
                if last:
                    k.op(act, [xB_], [junkB, stB], lambda e, xb_=xb_: e.activation(out=junk[:], in_=xb_[:], func=AF.Square, accum_out=st[:, 1:2]))
                    k.op(act, [stB, epsB], [stB], lambda e: e.activation(out=st[:, 1:2], in_=st[:, 1:2], func=AF.Ln, scale=1.0 / D, bias=eps_t[:]))
                    k.op(act, [stB], [stB], lambda e: e.activation(out=st[:, 1:2], in_=st[:, 1:2], func=AF.Exp, scale=-0.5))
                    k.op(dve, [xB_, stB, gfinB], [xB_], lambda e, xb_=xb_: e.scalar_tensor_tensor(
                        out=xb_[:], in0=xb_[:], scalar=st[:, 1:2], in1=gfin[:], op0=ALU.mult, op1=ALU.mult))
                k.dma(pool, None, xB_, xdst, xb_[:], ("st_x2", bi))
            k.barrier()

    except _Stop:
        print("STOPPED at", stop, "nops", k.nops)
        k.barrier()
    es.close()
    return nc


def rope_tables(SEQ):
    rows = SEQ // GRID_W
    row = np.repeat(np.arange(rows, dtype=np.float32), GRID_W)
    col = np.tile(np.arange(GRID_W, dtype=np.float32), rows)
    inv = (10000.0 ** (-np.arange(8, dtype=np.float32) / 8)).astype(np.float32)
    ang = np.stack([row[:, None] * inv, col[:, None] * inv], axis=1).astype(np.float32)
    return np.cos(ang).reshape(SEQ, 16).astype(np.float32), np.sin(ang).reshape(SEQ, 16).astype(np.float32)


def make_consts():
    c = np.zeros((128, 9, 128), np.float32)
    c[:, 0, :] = np.eye(128, dtype=np.float32)
    c[:, 1, :] = 1.0
    s = np.arange(128)[:, None]
    t = np.arange(128)[None, :]
    same = (s // 32) == (t // 32)
    c[:, 2, :] = (same & (s <= t)).astype(np.float32)
    c[:, 3, :] = (same & (s > t)).astype(np.float32)
    c[:, 4, :] = (same & (s <= t)).astype(np.float32)
    c[:, 5, :] = (same & (s >= t)).astype(np.float32)
    c[:, 6, :] = (same & (s < t)).astype(np.float32)
    c[:, 7, :] = (same & (s >= t)).astype(np.float32)
    for cc in range(4):
        c[cc * 32:(cc + 1) * 32, 8, cc] = 1.0
    return c


def prep_inputs(inputs, b, SEQ, CTX, DEPTH):
    f = lambda a: np.ascontiguousarray(np.asarray(a, dtype=np.float32))
    cos, sin = rope_tables(SEQ)
    NT = SEQ // 128
    rope_cs = np.stack([cos.reshape(NT, 128, 16), sin.reshape(NT, 128, 16)], axis=2).transpose(1, 0, 2, 3)
    cT = np.stack([f(inputs["c"])[b].reshape(8, 128).T, f(inputs["c_ctx"]).reshape(8, 128).T], axis=2)
    w_ukv = f(inputs["mla_w_ukv"]).reshape(DEPTH, 128, 6, 128)
    m = {
        "x": f(inputs["x"])[b],
        "ctx": f(inputs["ctx"])[b],
        "cT": f(cT),
        "w_mod": f(inputs["w_mod"]),
        "b_modT": f(f(inputs["b_mod"]).reshape(DEPTH, 48, 128).transpose(2, 0, 1)),
        "g_mixT": f(f(inputs["g_mix"]).reshape(DEPTH, 8, 128).transpose(2, 0, 1)),
        "g_mlpT": f(f(inputs["g_mlp"]).reshape(DEPTH, 8, 128).transpose(2, 0, 1)),
        "w_in": f(inputs["w_in"]),
        "w_out": f(inputs["w_out"]),
        "da_lambda": f(f(inputs["da_lambda"]).reshape(DEPTH, 128)),
        "da_g": f(inputs["da_subln_g"]),
        "g_cqT": f(f(inputs["mla_g_cq"]).reshape(DEPTH, 2, 128).transpose(2, 0, 1)),
        "g_ckvT": f(f(inputs["mla_g_ckv"]).reshape(DEPTH, 1, 128).transpose(2, 0, 1)),
        "w_uq": f(inputs["mla_w_uq"]),
        "w_ukn": f(w_ukv[:, :, :, 0:64].reshape(DEPTH, 128, 384)),
        "w_uv": f(w_ukv[:, :, :, 64:128].reshape(DEPTH, 128, 384)),
        "hg_lbT": f(f(inputs["hg_lower_bounds"]).reshape(2, DEPTH, HG_H, 128).transpose(3, 0, 2, 1)),
        "hg_g": f(inputs["hg_norm_g"]),
        "w_ff1": f(inputs["w_ff1"]),
        "w_ff2": f(inputs["w_ff2"]),
        "g_final": f(f(inputs["g_final"]).reshape(1, D)),
        "rope_cs": f(rope_cs),
        "consts_f": make_consts(),
    }
    return m


_NC_CACHE = {}


STOP = None
LIMIT = None


def kernel(**inputs):
    x = np.asarray(inputs["x"])
    B, SEQ, _ = x.shape
    CTX = np.asarray(inputs["ctx"]).shape[1]
    DEPTH = np.asarray(inputs["w_in"]).shape[0]
    key = (SEQ, CTX, DEPTH)
    if key not in _NC_CACHE:
        _NC_CACHE[key] = build(SEQ, CTX, DEPTH, stop=STOP)
    nc = _NC_CACHE[key]
    in_maps = [prep_inputs(inputs, c % B, SEQ, CTX, DEPTH) for c in range(8)]
    res = run_bass_kernel_spmd(nc, in_maps, core_ids=list(range(8)))
    outp = np.stack([np.asarray(res.results[b]["out"], dtype=np.float32) for b in range(B)], axis=0)
    return outp
```

```python
import math
from contextlib import ExitStack
import numpy as np
import concourse.bass as bass
import concourse.mybir as mybir
from concourse.bass_utils import run_bass_kernel_spmd

F32 = mybir.dt.float32
BF16 = mybir.dt.bfloat16
AF = mybir.ActivationFunctionType
ALU = mybir.AluOpType
AX = mybir.AxisListType

D = 1024
GRID_W = 64
EPS = 1e-6
DA_H = 6
MLA_H = 6
HG_H = 4
D_IN = 3616
D_FF = 4096
C_DAQ, C_DAK, C_DAV, C_CQ, C_CKV, C_KR, C_HQ, C_HZF, C_HZB, C_HV, C_HG = 0, 384, 768, 1152, 1408, 1536, 1568, 2080, 2592, 3104, 3360


class Buf:
    __slots__ = ("ap", "w", "r", "ps")

    def __init__(self, ap, ps=False):
        self.ap = ap
        self.w = []
        self.r = []
        self.ps = ps


class Eng:
    def __init__(self, e, sem, name):
        self.e = e
        self.sem = sem
        self.cnt = 0
        self.name = name
        self.waited = {}


class K:
    def __init__(self, nc, es):
        self.nc = nc
        self.es = es
        self.pe = Eng(nc.tensor, es.enter_context(nc.semaphore("s_pe")), "pe")
        self.act = Eng(nc.scalar, es.enter_context(nc.semaphore("s_act")), "act")
        self.dve = Eng(nc.vector, es.enter_context(nc.semaphore("s_dve")), "dve")
        self.pool = Eng(nc.gpsimd, es.enter_context(nc.semaphore("s_pool")), "pool")
        self.sp = Eng(nc.sync, es.enter_context(nc.semaphore("s_sp")), "sp")
        self.engs = [self.pe, self.act, self.dve, self.pool, self.sp]
        self.dma_sems = {}
        self.nsem = 0
        self.pending_dma = []
        self.limit = None
        self.nops = 0

    def muted(self):
        self.nops += 1
        return self.limit is not None and self.nops > self.limit

    def wait(self, eng, tok):
        sem, val, owner = tok
        key = id(sem)
        if eng.waited.get(key, 0) >= val:
            return
        if owner is self.pe and eng is self.pe:
            return
        eng.e.wait_ge(sem, val)
        eng.waited[key] = val

    def op(self, eng, reads, writes, fn, extra=()):
        if self.muted():
            return (eng.sem, 0, eng)
        psr = [b for b in reads if b.ps]
        if psr:
            reads = [b for b in reads if not b.ps]
            writes = list(writes) + [b for b in psr if b not in writes]
        for b in reads:
            for t in b.w:
                self.wait(eng, t)
        for b in writes:
            for t in b.r:
                if t[2] is not eng:
                    self.wait(eng, t)
            for t in b.w:
                if t[2] is not eng:
                    self.wait(eng, t)
        for t in extra:
            self.wait(eng, t)
        ins = fn(eng.e)
        eng.cnt += 1
        ins.then_inc(eng.sem, 1)
        tok = (eng.sem, eng.cnt, eng)
        for b in reads:
            b.r.append(tok)
        for b in writes:
            b.w = [tok]
            b.r = []
        return tok

    def dma(self, q, out_buf, in_buf, out_ap, in_ap, sem_key):
        if self.muted():
            return (q.sem, 0, None)
        if sem_key not in self.dma_sems:
            self.dma_sems[sem_key] = [self.es.enter_context(self.nc.semaphore("d%d" % self.nsem)), 0]
            self.nsem += 1
        ent = self.dma_sems[sem_key]
        if in_buf is not None:
            for t in in_buf.w:
                self.wait(q, t)
        if out_buf is not None:
            for t in out_buf.r:
                self.wait(q, t)
            for t in out_buf.w:
                if not (t[2] is None and t[0] is ent[0]):
                    self.wait(q, t)
        ins = q.e.dma_start(out=out_ap, in_=in_ap)
        ent[1] += 16
        ins.then_inc(ent[0], 16)
        tok = (ent[0], ent[1], None)
        if in_buf is not None:
            in_buf.r.append(tok)
        if out_buf is not None:
            out_buf.w = [tok]
            out_buf.r = []
        else:
            self.pending_dma.append(tok)
        return tok

    def barrier(self):
        toks = [(e.sem, e.cnt, e) for e in self.engs if e.cnt > 0]
        for e in self.engs:
            for t in toks:
                if t[2] is not e:
                    self.wait(e, t)
            for t in self.pending_dma:
                self.wait(e, t)
        self.pending_dma = []


LIMIT = None


class _Stop(Exception):
    pass


def build(SEQ, CTX, DEPTH, stop=None):
    def chk(name):
        if stop == name:
            raise _Stop()

    NT = SEQ // 128
    NCT = CTX // 128
    NTT = NT + NCT
    T = SEQ + CTX
    NCH = T // 32
    nc = bass.Bass("TRN2", target_bir_lowering=False)
    es = ExitStack()
    _uid = [0]

    def SBT(name, shape, dt):
        _uid[0] += 1
        return nc.sbuf_tensor("%s_u%d" % (name, _uid[0]), shape, dt)

    def din(name, shape, dt=F32):
        return nc.dram_tensor(name, list(shape), dt, kind="ExternalInput").ap()

    def dscr(name, shape, dt):
        return nc.dram_tensor(name, list(shape), dt, kind="Internal").ap()

    x_in = din("x", [SEQ, D])
    ctx_in = din("ctx", [CTX, D])
    cT_in = din("cT", [128, 8, 2])
    w_mod = din("w_mod", [DEPTH, D, 6 * D])
    b_modT = din("b_modT", [128, DEPTH, 48])
    g_mixT = din("g_mixT", [128, DEPTH, 8])
    g_mlpT = din("g_mlpT", [128, DEPTH, 8])
    w_in = din("w_in", [DEPTH, D, D_IN])
    w_out = din("w_out", [DEPTH, D, D])
    da_lambda = din("da_lambda", [DEPTH, 128])
    da_g = din("da_g", [DEPTH, 64])
    g_cqT = din("g_cqT", [128, DEPTH, 2])
    g_ckvT = din("g_ckvT", [128, DEPTH, 1])
    w_uq = din("w_uq", [DEPTH, 256, 576])
    w_ukn = din("w_ukn", [DEPTH, 128, 384])
    w_uv = din("w_uv", [DEPTH, 128, 384])
    hg_lbT = din("hg_lbT", [128, 2, HG_H, DEPTH])
    hg_g = din("hg_g", [DEPTH, 64])
    w_ff1 = din("w_ff1", [DEPTH, D, D_FF])
    w_ff2 = din("w_ff2", [DEPTH, D_FF, D])
    g_final = din("g_final", [1, D])
    rope_cs = din("rope_cs", [128, NT, 2, 16])
    consts_f = din("consts_f", [128, 9, 128])
    out = nc.dram_tensor("out", [SEQ, D], F32, kind="ExternalOutput").ap()

    xc_res = dscr("xc_res", [CTX, D], F32)
    daqT = dscr("daqT", [384, T], BF16)
    dakT = dscr("dakT", [384, T], BF16)
    dav = dscr("dav", [T, 384], BF16)
    mqT = dscr("mqT", [6, 96, T], BF16)
    mknT = dscr("mknT", [384, T], BF16)
    mkrT = dscr("mkrT", [32, T], BF16)
    mv = dscr("mv", [T, 384], BF16)
    hq0T = dscr("hq0T", [2, HG_H, 128, T], BF16)
    hk0 = dscr("hk0", [2, HG_H, T, 128], BF16)
    hdk = dscr("hdk", [2, HG_H, 128, NCH], F32)
    hvs = dscr("hvs", [T, 256], BF16)
    hgs = dscr("hgs", [T, 256], BF16)
    ycat = dscr("ycat", [T, D], BF16)
    hoi = dscr("hoi", [T, 256], F32)
    h2s = dscr("h2s", [8, 128, T], BF16)

    k = K(nc, es)
    k.limit = LIMIT
    pe, act, dve, pool, sp = k.pe, k.act, k.dve, k.pool, k.sp

    def sb(name, shape, dt):
        return es.enter_context(SBT(name, list(shape), dt))

    PS = es.enter_context(nc.psum_tensor("PS", [128, 8, 512], F32))

    def psf(b, n=512):
        return PS[:, b, 0:n]

    def psbf(b):
        return PS[:, b, :].bitcast(BF16)

    psB = [Buf(None, ps=True) for _ in range(8)]

    cst = sb("cst", [128, 9, 128], F32)
    cstB = Buf(cst)
    k.dma(sp, cstB, None, cst[:], consts_f[:, :, :], "cst")
    identb = sb("identb", [128, 128], BF16)
    identB = Buf(identb)
    k.op(dve, [cstB], [identB], lambda e: e.tensor_copy(out=identb[:], in_=cst[:, 0, :]))
    ident_f = cst[:, 0, :]
    ones_f = cst[:, 1, :]
    eps_t = sb("eps_t", [128, 1], F32)
    epsB = Buf(eps_t)
    k.op(dve, [], [epsB], lambda e: e.memset(eps_t[:], EPS))

    bmod = sb("bmod", [128, DEPTH, 48], F32)
    gmix = sb("gmix", [128, DEPTH, 8], F32)
    gmlp = sb("gmlp", [128, DEPTH, 8], F32)
    gcq = sb("gcq", [128, DEPTH, 2], F32)
    gckv = sb("gckv", [128, DEPTH, 1], F32)
    lbraw = sb("lbraw", [128, 2 * HG_H, DEPTH], F32)
    smallB = Buf(None)
    k.dma(sp, smallB, None, bmod[:], b_modT[:, :, :], "small")
    k.dma(sp, smallB, None, gmix[:], g_mixT[:, :, :], "small")
    k.dma(sp, smallB, None, gmlp[:], g_mlpT[:, :, :], "small")
    k.dma(sp, smallB, None, gcq[:], g_cqT[:, :, :], "small")
    k.dma(sp, smallB, None, gckv[:], g_ckvT[:, :, :], "small")
    k.dma(sp, smallB, None, lbraw[:], hg_lbT.rearrange("p a h l -> p (a h) l"), "small")
    cT = sb("cT", [128, 8, 2], F32)
    k.dma(sp, smallB, None, cT[:], cT_in[:, :, :], "small")
    lamraw = sb("lamraw", [128, DEPTH, 128], F32)
    k.dma(sp, smallB, None, lamraw[:], da_lambda.partition_broadcast(128), "small")
    dag = sb("dag", [128, DEPTH, 64], F32)
    k.dma(sp, smallB, None, dag[:], da_g.partition_broadcast(128), "small")
    hgg = sb("hgg", [128, DEPTH, 64], F32)
    k.dma(sp, smallB, None, hgg[:], hg_g.partition_broadcast(128), "small")

    scT = sb("scT", [128, 8, 2], F32)
    scB = Buf(scT)
    k.op(act, [smallB], [scB], lambda e: e.activation(out=scT[:], in_=cT[:], func=AF.Silu))

    lbe = sb("lbe", [128, 2 * HG_H, DEPTH], F32)
    lbs = sb("lbs", [128, 2 * HG_H], F32)
    lb = sb("lb", [128, 2 * HG_H, DEPTH], F32)
    oml = sb("oml", [128, 2 * HG_H, DEPTH], F32)
    lbB = Buf(None)
    k.op(act, [smallB], [lbB], lambda e: e.activation(out=lbe[:], in_=lbraw[:], func=AF.Exp))
    k.op(dve, [lbB], [lbB], lambda e: e.tensor_reduce(out=lbs[:], in_=lbe[:], axis=AX.X, op=ALU.add))
    k.op(dve, [lbB], [lbB], lambda e: e.reciprocal(out=lbs[:], in_=lbs[:]))
    for l in range(DEPTH):
        k.op(dve, [lbB], [lbB], lambda e, l=l: e.tensor_tensor(out=lbe[:, :, l], in0=lbe[:, :, l], in1=lbs[:], op=ALU.mult))
    k.op(dve, [lbB], [lbB], lambda e: e.memset(lb[:, :, 0], 0.0))
    for l in range(1, DEPTH):
        k.op(dve, [lbB], [lbB], lambda e, l=l: e.tensor_tensor(out=lb[:, :, l], in0=lb[:, :, l - 1], in1=lbe[:, :, l], op=ALU.add))
    k.op(dve, [lbB], [lbB], lambda e: e.tensor_scalar(out=oml[:], in0=lb[:], scalar1=-1.0, scalar2=1.0, op0=ALU.mult, op1=ALU.add))

    k.dma(sp, None, None, xc_res[:, :], ctx_in[:, :], "xc_copy")

    modsb = sb("modsb", [128, 48, 2], F32)
    A1 = sb("A1", [128, 8, 2], F32)
    A2 = sb("A2", [128, 8, 2], F32)
    lam_t = sb("lam_t", [128, 4], F32)
    dagl = sb("dagl", [128, 64], F32)
    modB = Buf(None)

    k.barrier()

    def make_gates(ph_sb, mi):
        gt = ph_sb("gate_bc", [128, 2, D], F32)
        dg = ph_sb("dg", [128, 128], F32)
        dgB = Buf(dg)
        gB = Buf(gt)
        for j in range(2):
            for half in range(2):
                bank = 1 + half
                for cc in range(4):
                    ch = half * 4 + cc
                    k.op(dve, [modB, cstB], [dgB], lambda e, ch=ch, j=j: e.tensor_scalar(
                        out=dg[:], in0=ident_f, scalar1=modsb[:, mi * 8 + ch, j:j + 1], scalar2=None, op0=ALU.mult))
                    k.op(pe, [dgB, cstB], [psB[bank]], lambda e, bank=bank, cc=cc: e.matmul(
                        PS[:, bank, cc * 128:(cc + 1) * 128], lhsT=ones_f, rhs=dg[:], start=True, stop=True, skip_group_check=True))
                k.op(act, [psB[bank]], [gB], lambda e, j=j, half=half, bank=bank: e.activation(
                    out=gt[:, j, half * 512:(half + 1) * 512], in_=PS[:, bank, :], func=AF.Copy))
        return gt, gB

    try:
      for l in range(DEPTH):
        last = l == DEPTH - 1
        lam_init = 0.8 - 0.6 * math.exp(-0.3 * l)
        x_src = x_in if l == 0 else out

        with ExitStack() as ph:
            def psb_(name, shape, dt):
                return ph.enter_context(SBT(name, list(shape), dt))
            wm = [psb_("wm%d" % i, [128, 8, 1024], F32) for i in range(2)]
            wmB = [Buf(wm[i]) for i in range(2)]
            msb = psb_("msb", [128, 48, 2], F32)
            for pc in range(6):
                bi = pc % 2
                for kk in range(8):
                    k.dma(sp, wmB[bi], None, wm[bi][:, kk, :], w_mod[l, kk * 128:(kk + 1) * 128, pc * 1024:(pc + 1) * 1024], ("wm", bi))

                def mm(e, pc=pc, bi=bi):
                    ins = None
                    for j in range(8):
                        for kk in range(8):
                            ins = e.matmul(PS[:, 0, (pc * 8 + j) * 2:(pc * 8 + j) * 2 + 2], lhsT=wm[bi][:, kk, j * 128:(j + 1) * 128],
                                           rhs=scT[:, kk, :], start=(kk == 0), stop=(kk == 7), skip_group_check=True)
                    return ins
                k.op(pe, [wmB[bi], scB], [psB[0]], mm)
            k.op(dve, [psB[0], smallB], [modB], lambda e: e.tensor_tensor(
                out=modsb[:], in0=PS[:, 0, 0:96].rearrange("p (j c) -> p j c", c=2),
                in1=bmod[:, l, :].unsqueeze(2).broadcast_to([128, 48, 2]), op=ALU.add))
            for (At, gsrc, mi) in ((A1, gmix, 1), (A2, gmlp, 4)):
                k.op(dve, [modB], [modB], lambda e, At=At, mi=mi: e.tensor_scalar(
                    out=At[:], in0=modsb[:, mi * 8:(mi + 1) * 8, :], scalar1=1.0, scalar2=None, op0=ALU.add))
                k.op(dve, [modB], [modB], lambda e, At=At, gsrc=gsrc: e.tensor_tensor(
                    out=At[:], in0=At[:], in1=gsrc[:, l, :].unsqueeze(2).broadcast_to([128, 8, 2]), op=ALU.mult))
            lt = psb_("lt", [128, 2, 32], F32)
            k.op(dve, [smallB], [modB], lambda e: e.tensor_tensor(
                out=lt[:], in0=lamraw[:, l, :].rearrange("p (a b c) -> p a b c", a=2, b=2)[:, :, 0, :],
                in1=lamraw[:, l, :].rearrange("p (a b c) -> p a b c", a=2, b=2)[:, :, 1, :], op=ALU.mult))
            k.op(dve, [modB], [modB], lambda e: e.tensor_reduce(out=lam_t[:, 1:3], in_=lt[:], axis=AX.X, op=ALU.add))
            k.op(act, [modB], [modB], lambda e: e.activation(out=lam_t[:, 1:3], in_=lam_t[:, 1:3], func=AF.Exp))
            k.op(dve, [modB], [modB], lambda e: e.tensor_tensor(out=lam_t[:, 0:1], in0=lam_t[:, 2:3], in1=lam_t[:, 1:2], op=ALU.subtract))
            k.op(dve, [modB], [modB], lambda e: e.tensor_scalar(out=lam_t[:, 0:1], in0=lam_t[:, 0:1], scalar1=-lam_init, scalar2=None, op0=ALU.add))
            k.op(dve, [smallB], [modB], lambda e: e.tensor_scalar(out=dagl[:], in0=dag[:, l, :], scalar1=1.0 - lam_init, scalar2=None, op0=ALU.mult))
            k.barrier()
        chk("P0")

        with ExitStack() as ph:
            def psb_(name, shape, dt):
                return ph.enter_context(SBT(name, list(shape), dt))
            win = psb_("win", [128, 8, D_IN], BF16)
            wuq = psb_("wuq", [128, 2, 576], BF16)
            wukn = psb_("wukn", [128, 384], BF16)
            wuv = psb_("wuv", [128, 384], BF16)
            wB = Buf(None)
            with ExitStack() as st_:
                stg = [st_.enter_context(SBT("stg%d" % i, [128, D_IN], F32)) for i in range(2)]
                stgB = [Buf(stg[i]) for i in range(2)]
                cnt = 0
                for kk in range(8):
                    bi = cnt % 2
                    cnt += 1
                    k.dma(sp, stgB[bi], None, stg[bi][:, :], w_in[l, kk * 128:(kk + 1) * 128, :], ("stg", bi))
                    k.op(pool if kk % 2 else dve, [stgB[bi]], [wB], lambda e, kk=kk, bi=bi: e.tensor_copy(out=win[:, kk, :], in_=stg[bi][:, :]))
                for kk in range(2):
                    bi = cnt % 2
                    cnt += 1
                    k.dma(sp, stgB[bi], None, stg[bi][:, 0:576], w_uq[l, kk * 128:(kk + 1) * 128, :], ("stg", bi))
                    k.op(dve, [stgB[bi]], [wB], lambda e, kk=kk, bi=bi: e.tensor_copy(out=wuq[:, kk, :], in_=stg[bi][:, 0:576]))
                for (wt, src) in ((wukn, w_ukn), (wuv, w_uv)):
                    bi = cnt % 2
                    cnt += 1
                    k.dma(sp, stgB[bi], None, stg[bi][:, 0:384], src[l, :, :], ("stg", bi))
                    k.op(dve, [stgB[bi]], [wB], lambda e, wt=wt, bi=bi: e.tensor_copy(out=wt[:], in_=stg[bi][:, 0:384]))


            k.barrier()
            rope = psb_("rope", [128, NT, 2, 16], F32)
            ropeB = Buf(rope)
            k.dma(sp, ropeB, None, rope[:], rope_cs[:, :, :, :], "rope")
            oit = [psb_("oit%d" % i, [128, 256], F32) for i in range(2)]
            oitB = [Buf(None) for i in range(2)]
            xt = [psb_("xt%d" % i, [128, D], F32) for i in range(2)]
            xtB = [Buf(xt[i]) for i in range(2)]
            junk = psb_("junk", [128, D], F32)
            junkB = Buf(junk)
            st = psb_("st", [128, 8], F32)
            stB = Buf(st)
            xn = psb_("xn", [128, D], BF16)
            xnB = Buf(xn)
            hT = psb_("hT", [128, 8, 128], BF16)
            hTB = Buf(hT)
            tmA = psb_("tmA", [128, 1280], BF16)
            tmAB = Buf(tmA)
            rt = [psb_("rt%d" % i, [128, 24, 2, 8], F32) for i in range(4)]
            rtB = Buf(None)
            cqn = psb_("cqn", [128, 384], BF16)
            cqnB = Buf(cqn)
            cqT = psb_("cqT", [128, 3, 128], BF16)
            cqTB = Buf(cqT)
            fmo = [psb_("fmo%d" % i, [128, 1024], BF16) for i in range(2)]
            fmoB = [Buf(fmo[i]) for i in range(2)]
            qu = psb_("qu", [128, 6, 128], BF16)
            quB = Buf(qu)
            qrt = [psb_("qrt%d" % i, [128, 6, 2, 8], F32) for i in range(4)]
            mvt = psb_("mvt", [128, 384], BF16)
            mvtB = Buf(mvt)
            hvt = psb_("hvt", [128, 512], BF16)
            hvtB = Buf(hvt)
            sig = psb_("sig", [128, 8, 128], F32)
            sgn = psb_("sgn", [128, 8, 128], F32)
            gT = psb_("gT", [128, 8, 128], F32)
            hgB = Buf(None)
            gtm = psb_("gtm", [128, 2, 128], F32)
            gtmB = [Buf(None), Buf(None)]
            eq = psb_("eq", [128, 2, 128], F32)
            ek = psb_("ek", [128, 2, 128], F32)
            e0 = psb_("e0", [128, 2, 128], F32)
            eB = [Buf(None), Buf(None)]
            q0T = psb_("q0T", [128, 2, 128], BF16)
            k1T = psb_("k1T", [128, 2, 128], BF16)
            k0T = psb_("k0T", [128, 2, 128], BF16)
            qkB = [Buf(None), Buf(None)]
            k0tm = psb_("k0tm", [128, 2, 128], BF16)
            k0tmB = [Buf(None), Buf(None)]
            atm = psb_("atm", [128, 2, 128], BF16)
            atmB = [Buf(None), Buf(None)]
            dkt = psb_("dkt", [128, 2, 4], F32)
            dktB = [Buf(None), Buf(None)]
            qTs = psb_("qTs", [128, 4, 128], F32)
            qTsB = Buf(qTs)
            gtm4 = [psb_("gtm4_%d" % i, [128, 512], F32) for i in range(2)]
            eq4 = [psb_("eq4_%d" % i, [128, 512], F32) for i in range(2)]
            ek4 = [psb_("ek4_%d" % i, [128, 512], F32) for i in range(2)]
            e04 = [psb_("e04_%d" % i, [128, 512], F32) for i in range(2)]
            q0T4 = [psb_("q0T4_%d" % i, [128, 512], BF16) for i in range(2)]
            k1T4 = [psb_("k1T4_%d" % i, [128, 512], BF16) for i in range(2)]
            k0T4 = [psb_("k0T4_%d" % i, [128, 512], BF16) for i in range(2)]
            k0tm4 = [psb_("k0tm4_%d" % i, [128, 512], BF16) for i in range(2)]
            atm4 = [psb_("atm4_%d" % i, [128, 512], BF16) for i in range(2)]
            dkt4 = [psb_("dkt4_%d" % i, [128, 4, 4], F32) for i in range(2)]

            k.op(pool, [], [tmAB], lambda e: e.memset(tmA[:], 0.0))
            k.op(pool, [], [quB], lambda e: e.memset(qu[:], 0.0))

            def rms_stats(src_ap, col, nfeat, srcB):
                k.op(act, [srcB], [junkB], lambda e: e.activation(out=junk[:, 0:nfeat], in_=src_ap, func=AF.Square))
                k.op(dve, [junkB], [stB], lambda e: e.tensor_reduce(out=st[:, col:col + 1], in_=junk[:, 0:nfeat], axis=AX.X, op=ALU.add))
                k.op(act, [stB, epsB], [stB], lambda e: e.activation(out=st[:, col:col + 1], in_=st[:, col:col + 1], func=AF.Ln, scale=1.0 / nfeat, bias=eps_t[:]))
                k.op(act, [stB], [stB], lambda e: e.activation(out=st[:, col:col + 1], in_=st[:, col:col + 1], func=AF.Exp, scale=-0.5))

            k.barrier()
            for ti in range(NTT if stop not in ("P1a", "P1b") else (1 if stop == "P1a" else 3)):
                is_ctx = ti < NCT
                mj = 1 if is_ctx else 0
                t0 = ti * 128
                src = xc_res[t0:t0 + 128, :] if is_ctx else x_src[t0 - CTX:t0 - CTX + 128, :]
                xb_ = xt[ti % 2]
                xB_ = xtB[ti % 2]
                k.dma(sp, xB_, None, xb_[:], src, ("xt", ti % 2))
                k.op(act, [xB_], [junkB, stB], lambda e: e.activation(out=junk[:], in_=xb_[:], func=AF.Square, accum_out=st[:, 0:1]))
                k.op(act, [stB, epsB], [stB], lambda e: e.activation(out=st[:, 0:1], in_=st[:, 0:1], func=AF.Ln, scale=1.0 / D, bias=eps_t[:]))
                k.op(act, [stB], [stB], lambda e: e.activation(out=st[:, 0:1], in_=st[:, 0:1], func=AF.Exp, scale=-0.5))
                k.op(dve, [xB_, stB], [xnB], lambda e: e.tensor_scalar(out=xn[:], in0=xb_[:], scalar1=st[:, 0:1], scalar2=None, op0=ALU.mult))

                def tr8(e):
                    ins = None
                    for c in range(8):
                        ins = e.transpose(out=psbf(0)[:, c * 128:(c + 1) * 128], in_=xn[:, c * 128:(c + 1) * 128], identity=identb[:])
                    return ins
                k.op(pe, [xnB, identB], [psB[0]], tr8)
                for c in range(8):
                    k.op(act, [psB[0], modB], [hTB], lambda e, c=c: e.activation(
                        out=hT[:, c, :], in_=psbf(0)[:, c * 128:(c + 1) * 128], func=AF.Identity,
                        scale=A1[:, c, mj:mj + 1], bias=modsb[:, 0 * 8 + c, mj:mj + 1]))
                def tm1(e):
                    ins = None
                    for bnk, (c0, c1) in enumerate(((0, 512), (512, 1024), (1024, 1536), (1536, 1568))):
                        for kk in range(8):
                            ins = e.matmul(PS[:, 1 + bnk, 0:c1 - c0], lhsT=hT[:, kk, :], rhs=win[:, kk, c0:c1], start=(kk == 0), stop=(kk == 7))
                    return ins
                k.op(pe, [hTB, wB], [psB[1], psB[2], psB[3], psB[4]], tm1)
                if is_ctx:
                    k.op(dve, [psB[1]], [tmAB], lambda e: e.tensor_copy(out=tmA[:, 0:512], in_=PS[:, 1, :]))
                    k.op(dve, [psB[2]], [tmAB], lambda e: e.tensor_copy(out=tmA[:, 512:1024], in_=PS[:, 2, :]))
                    k.op(dve, [psB[3]], [tmAB], lambda e: e.tensor_copy(out=tmA[:, 1024:1152], in_=PS[:, 3, 0:128]))
                    k.op(dve, [psB[4]], [tmAB], lambda e: e.tensor_copy(out=tmA[:, 1152:1184], in_=PS[:, 4, 0:32]))
                else:
                    lt_i = ti - NCT
                    for (psrc, nh, h0, dst0) in ((PS[:, 1, :], 16, 0, 0), (PS[:, 2, 0:256], 8, 16, 512), (PS[:, 4, 0:32], 1, 24, 1152)):
                        pv = psrc.rearrange("p (h a b f) -> p h a b f", a=2, b=2, f=8)
                        x1 = pv[:, :, :, 0, :]
                        x2 = pv[:, :, :, 1, :]
                        cos = rope[:, lt_i, 0, :].rearrange("p (a f) -> p a f", a=2).unsqueeze(1).broadcast_to([128, nh, 2, 8])
                        sin = rope[:, lt_i, 1, :].rearrange("p (a f) -> p a f", a=2).unsqueeze(1).broadcast_to([128, nh, 2, 8])
                        dv = tmA[:, dst0:dst0 + nh * 32].rearrange("p (h a b f) -> p h a b f", a=2, b=2, f=8)
                        bankB = psB[1] if h0 == 0 else (psB[2] if h0 == 16 else psB[4])
                        r0, r1, r2, r3 = (rt[i][:, h0:h0 + nh, :, :] if h0 + nh <= 24 else None for i in range(4))
                        if h0 == 24:
                            r0, r1, r2, r3 = (rt[i][:, 0:1, :, :] for i in range(4))
                        k.op(dve, [bankB, ropeB], [rtB], lambda e, x1=x1, cos=cos, r0=r0: e.tensor_tensor(out=r0, in0=x1, in1=cos, op=ALU.mult))
                        k.op(dve, [bankB, ropeB], [rtB], lambda e, x2=x2, sin=sin, r1=r1: e.tensor_tensor(out=r1, in0=x2, in1=sin, op=ALU.mult))
                        k.op(dve, [bankB, ropeB], [rtB], lambda e, x2=x2, cos=cos, r2=r2: e.tensor_tensor(out=r2, in0=x2, in1=cos, op=ALU.mult))
                        k.op(dve, [bankB, ropeB], [rtB], lambda e, x1=x1, sin=sin, r3=r3: e.tensor_tensor(out=r3, in0=x1, in1=sin, op=ALU.mult))
                        k.op(pool, [rtB], [tmAB], lambda e, dv=dv, r0=r0, r1=r1: e.tensor_tensor(out=dv[:, :, :, 0, :], in0=r0, in1=r1, op=ALU.subtract))
                        k.op(pool, [rtB], [tmAB], lambda e, dv=dv, r2=r2, r3=r3: e.tensor_tensor(out=dv[:, :, :, 1, :], in0=r2, in1=r3, op=ALU.add))
                    k.op(act, [psB[2], psB[3]], [tmAB], lambda e: e.activation(out=tmA[:, 768:1024], in_=PS[:, 2, 256:512], func=AF.Copy))
                    k.op(act, [psB[3]], [tmAB], lambda e: e.activation(out=tmA[:, 1024:1152], in_=PS[:, 3, 0:128], func=AF.Copy))
                k.dma(pool, None, tmAB, dav[t0:t0 + 128, :], tmA[:, 768:1152], "st_dav")
                rms_stats(PS[:, 3, 128:384], 1, 256, psB[3])
                rms_stats(PS[:, 3, 384:512], 2, 128, psB[3])
                k.op(dve, [psB[3], stB], [cqnB], lambda e: e.tensor_scalar(out=cqn[:, 0:256], in0=PS[:, 3, 128:384], scalar1=st[:, 1:2], scalar2=None, op0=ALU.mult))
                k.op(dve, [psB[3], stB], [cqnB], lambda e: e.tensor_scalar(out=cqn[:, 256:384], in0=PS[:, 3, 384:512], scalar1=st[:, 2:3], scalar2=None, op0=ALU.mult))
                def trA(e):
                    ins = None
                    for c in range(6):
                        ins = e.transpose(out=psbf(5)[:, c * 128:(c + 1) * 128], in_=tmA[:, c * 128:(c + 1) * 128], identity=identb[:])
                    return ins
                k.op(pe, [tmAB, identB], [psB[5]], trA)
                fb = fmo[0]
                fB = fmoB[0]
                k.op(act, [psB[5]], [fB], lambda e: e.activation(out=fb[:, 0:768], in_=psbf(5)[:, 0:768], func=AF.Copy))
                k.dma(pool, None, fB, daqT.rearrange("(c p) t -> p c t", p=128)[:, :, t0:t0 + 128], fb[:, 0:384].rearrange("p (c t) -> p c t", c=3), "st_daq")
                k.dma(pool, None, fB, dakT.rearrange("(c p) t -> p c t", p=128)[:, :, t0:t0 + 128], fb[:, 384:768].rearrange("p (c t) -> p c t", c=3), "st_dak")

                def trB(e):
                    ins = None
                    for c in range(3):
                        ins = e.transpose(out=psbf(6)[:, c * 128:(c + 1) * 128], in_=cqn[:, c * 128:(c + 1) * 128], identity=identb[:])
                    ins = e.transpose(out=psbf(6)[:, 384:512], in_=tmA[:, 1152:1280], identity=identb[:])
                    return ins
                k.op(pe, [cqnB, tmAB, identB], [psB[6]], trB)
                for c in range(3):
                    sc_ap = gcq[:, l, c:c + 1] if c < 2 else gckv[:, l, 0:1]
                    k.op(act, [psB[6], smallB], [cqTB], lambda e, c=c, sc_ap=sc_ap: e.activation(
                        out=cqT[:, c, :], in_=psbf(6)[:, c * 128:(c + 1) * 128], func=AF.Identity, scale=sc_ap))
                fb1 = fmo[1]
                fB1 = fmoB[1]
                k.op(dve, [psB[6]], [fB1], lambda e: e.tensor_copy(out=fb1[0:32, 0:128], in_=psbf(6)[0:32, 384:512]))
                k.dma(pool, None, fB1, mkrT[:, t0:t0 + 128], fb1[0:32, 0:128], "st_kr")
                def upq(e):
                    ins = None
                    for bnk, (c0, c1) in ((1, (0, 512)), (2, (512, 576))):
                        for kk in range(2):
                            ins = e.matmul(PS[:, bnk, 0:c1 - c0], lhsT=cqT[:, kk, :], rhs=wuq[:, kk, c0:c1], start=(kk == 0), stop=(kk == 1))
                    return ins
                k.op(pe, [cqTB, wB], [psB[1], psB[2]], upq)
                k.op(act, [psB[1]], [quB], lambda e: e.activation(out=qu[:, 0:5, 0:96], in_=PS[:, 1, 0:480].rearrange("p (h c) -> p h c", c=96), func=AF.Copy))
                k.op(act, [psB[1]], [quB], lambda e: e.activation(out=qu[:, 5, 0:32], in_=PS[:, 1, 480:512], func=AF.Copy))
                k.op(act, [psB[2]], [quB], lambda e: e.activation(out=qu[:, 5, 32:96], in_=PS[:, 2, 0:64], func=AF.Copy))
                if not is_ctx:
                    lt_i = ti - NCT
                    qv = qu[:, :, 64:96].rearrange("p h (a b f) -> p h a b f", a=2, b=2)
                    x1 = qv[:, :, :, 0, :]
                    x2 = qv[:, :, :, 1, :]
                    cos = rope[:, lt_i, 0, :].rearrange("p (a f) -> p a f", a=2).unsqueeze(1).broadcast_to([128, 6, 2, 8])
                    sin = rope[:, lt_i, 1, :].rearrange("p (a f) -> p a f", a=2).unsqueeze(1).broadcast_to([128, 6, 2, 8])
                    k.op(dve, [quB, ropeB], [rtB], lambda e: e.tensor_tensor(out=qrt[0][:], in0=x1, in1=cos, op=ALU.mult))
                    k.op(dve, [quB, ropeB], [rtB], lambda e: e.tensor_tensor(out=qrt[1][:], in0=x2, in1=sin, op=ALU.mult))
                    k.op(dve, [quB, ropeB], [rtB], lambda e: e.tensor_tensor(out=qrt[2][:], in0=x2, in1=cos, op=ALU.mult))
                    k.op(dve, [quB, ropeB], [rtB], lambda e: e.tensor_tensor(out=qrt[3][:], in0=x1, in1=sin, op=ALU.mult))
                    k.op(pool, [rtB], [quB], lambda e: e.tensor_tensor(out=x1, in0=qrt[0][:], in1=qrt[1][:], op=ALU.subtract))
                    k.op(pool, [rtB], [quB], lambda e: e.tensor_tensor(out=x2, in0=qrt[2][:], in1=qrt[3][:], op=ALU.add))
                def trQ(e):
                    ins = None
                    for h in range(6):
                        ins = e.transpose(out=psbf(5)[:, h * 128:(h + 1) * 128], in_=qu[:, h, :], identity=identb[:])
                    return ins
                k.op(pe, [quB, identB], [psB[5]], trQ)
                k.op(act, [psB[5]], [fB], lambda e: e.activation(out=fb[:, 0:768], in_=psbf(5)[:, 0:768], func=AF.Copy))
                k.dma(pool, None, fB, mqT.rearrange("h d t -> d h t")[:, :, t0:t0 + 128], fb[0:96, 0:768].rearrange("p (h t) -> p h t", h=6), "st_mq")
                def upk(e):
                    ins = None
                    for c in range(3):
                        ins = e.matmul(PS[:, 6, c * 128:(c + 1) * 128], lhsT=wukn[:, c * 128:(c + 1) * 128], rhs=cqT[:, 2, :], start=True, stop=True, skip_group_check=True)
                    ins = e.matmul(PS[:, 7, 0:384], lhsT=cqT[:, 2, :], rhs=wuv[:, :], start=True, stop=True)
                    return ins
                k.op(pe, [cqTB, wB], [psB[6], psB[7]], upk)
                k.op(dve, [psB[6]], [fB1], lambda e: e.tensor_copy(out=fb1[:, 128:512], in_=PS[:, 6, 0:384]))
                k.dma(pool, None, fB1, mknT.rearrange("(c p) t -> p c t", p=128)[:, :, t0:t0 + 128], fb1[:, 128:512].rearrange("p (c t) -> p c t", c=3), "st_mkn")
                k.op(act, [psB[7]], [mvtB], lambda e: e.activation(out=mvt[:], in_=PS[:, 7, 0:384], func=AF.Copy))
                k.dma(pool, None, mvtB, mv[t0:t0 + 128, :], mvt[:], "st_mv")
                def tm2(e):
                    ins = None
                    for kk in range(8):
                        ins = e.matmul(PS[:, 1, :], lhsT=hT[:, kk, :], rhs=win[:, kk, C_HV:C_HV + 512], start=(kk == 0), stop=(kk == 7))
                    return ins
                k.op(pe, [hTB, wB], [psB[1]], tm2)
                k.op(dve, [psB[1]], [hvtB], lambda e: e.tensor_copy(out=hvt[:, 0:256], in_=PS[:, 1, 0:256]))
                k.op(act, [psB[1]], [hvtB], lambda e: e.activation(out=hvt[:, 256:512], in_=PS[:, 1, 256:512], func=AF.Silu))
                k.dma(pool, None, hvtB, hvs[t0:t0 + 128, :], hvt[:, 0:256], "st_hv")
                k.dma(pool, None, hvtB, hgs[t0:t0 + 128, :], hvt[:, 256:512], "st_hg")
                def fmp(e):
                    ins = None
                    for g, c0 in enumerate((C_HQ, C_HZF, C_HZB)):
                        for h in range(4):
                            for kk in range(8):
                                ins = e.matmul(PS[:, 2 + g, h * 128:(h + 1) * 128], lhsT=win[:, kk, c0 + h * 128:c0 + (h + 1) * 128], rhs=hT[:, kk, :],
                                               start=(kk == 0), stop=(kk == 7), skip_group_check=True)
                    return ins
                k.op(pe, [hTB, wB], [psB[2], psB[3], psB[4]], fmp)
                k.op(dve, [psB[2]], [qTsB], lambda e: e.tensor_copy(out=qTs[:].rearrange("p h t -> p (h t)"), in_=PS[:, 2, :]))
                for d_ in range(2):
                    k.op(act, [psB[3 + d_]], [hgB], lambda e, d_=d_: e.activation(
                        out=sig[:, d_ * 4:(d_ + 1) * 4, :].rearrange("p h t -> p (h t)"), in_=PS[:, 3 + d_, :], func=AF.Sigmoid))
                    k.op(act, [psB[3 + d_]], [hgB], lambda e, d_=d_: e.activation(
                        out=sgn[:, d_ * 4:(d_ + 1) * 4, :].rearrange("p h t -> p (h t)"), in_=PS[:, 3 + d_, :], func=AF.Sigmoid, scale=-1.0))
                for dh in range(8):
                    k.op(dve, [hgB, lbB], [hgB], lambda e, dh=dh: e.tensor_scalar(
                        out=sig[:, dh, :], in0=sig[:, dh, :], scalar1=oml[:, dh, l:l + 1], scalar2=lb[:, dh, l:l + 1], op0=ALU.mult, op1=ALU.add))
                k.op(act, [hgB], [hgB], lambda e: e.activation(out=gT[:], in_=sig[:], func=AF.Ln))
                for dh in range(8):
                    k.op(pool, [hgB, lbB], [hgB], lambda e, dh=dh: e.tensor_scalar(
                        out=sgn[:, dh, :], in0=sgn[:, dh, :], scalar1=oml[:, dh, l:l + 1], scalar2=None, op0=ALU.mult))
                for d_ in range(2):
                    mi = 2 + 3 * d_
                    bT, bC, bE = (0, 1, 2) if d_ == 0 else (4, 5, 6)
                    sg4 = sgn[:, d_ * 4:(d_ + 1) * 4, :].rearrange("p h t -> p (h t)")
                    def trg(e, d_=d_, bT=bT):
                        ins = None
                        for h in range(4):
                            ins = e.transpose(out=PS[:, bT, h * 128:(h + 1) * 128], in_=gT[:, d_ * 4 + h, :], identity=ident_f)
                        return ins
                    k.op(pe, [hgB, cstB], [psB[bT]], trg)
                    k.op(dve, [psB[bT]], [gtmB[d_]], lambda e, d_=d_, bT=bT: e.tensor_copy(out=gtm4[d_][:], in_=PS[:, bT, :]))
                    def cum(e, d_=d_, mi=mi, bC=bC, bE=bE):
                        ins = None
                        for h in range(4):
                            e.matmul(PS[:, bC, h * 128:(h + 1) * 128], lhsT=gtm4[d_][:, h * 128:(h + 1) * 128], rhs=cst[:, mi, :], start=True, stop=True, skip_group_check=True)
                            ins = e.matmul(PS[:, bE, h * 128:(h + 1) * 128], lhsT=gtm4[d_][:, h * 128:(h + 1) * 128], rhs=cst[:, mi + 1, :], start=True, stop=True, skip_group_check=True)
                        return ins
                    k.op(pe, [gtmB[d_], cstB], [psB[bC], psB[bE]], cum)
                    k.op(act, [psB[bC]], [eB[d_]], lambda e, d_=d_, bC=bC: e.activation(out=eq4[d_][:], in_=PS[:, bC, :], func=AF.Exp))
                    k.op(act, [psB[bC]], [eB[d_]], lambda e, d_=d_, bC=bC: e.activation(out=ek4[d_][:], in_=PS[:, bC, :], func=AF.Exp, scale=-1.0))
                    k.op(act, [psB[bE]], [eB[d_]], lambda e, d_=d_, bE=bE: e.activation(out=e04[d_][:], in_=PS[:, bE, :], func=AF.Exp))
                    k.op(dve, [eB[d_], qTsB], [qkB[d_]], lambda e, d_=d_: e.tensor_tensor(out=q0T4[d_][:], in0=qTs[:].rearrange("p h t -> p (h t)"), in1=eq4[d_][:], op=ALU.mult))
                    k.op(dve, [eB[d_], hgB], [qkB[d_]], lambda e, d_=d_, sg4=sg4: e.tensor_tensor(out=k1T4[d_][:], in0=sg4, in1=ek4[d_][:], op=ALU.mult))
                    k.op(pool, [eB[d_], hgB], [qkB[d_]], lambda e, d_=d_, sg4=sg4: e.tensor_tensor(out=k0T4[d_][:], in0=sg4, in1=e04[d_][:], op=ALU.mult))
                    off = 31 if d_ == 0 else 0
                    k.op(dve, [eB[d_]], [dktB[d_]], lambda e, d_=d_, off=off: e.tensor_copy(
                        out=dkt4[d_][:], in_=eq4[d_][:].rearrange("p (h c j) -> p h c j", h=4, c=4)[:, :, :, off]))
                    ch0 = t0 // 32
                    k.dma(pool, None, dktB[d_], hdk[d_].rearrange("h k c -> k h c")[:, :, ch0:ch0 + 4], dkt4[d_][:], ("st_dk", d_))
                    k.dma(pool, None, qkB[d_], hq0T[d_].rearrange("h k t -> k h t")[:, :, t0:t0 + 128], q0T4[d_][:].rearrange("p (h t) -> p h t", h=4), ("st_q0", d_))
                    def trk(e, d_=d_, bT=bT):
                        ins = None
                        for h in range(4):
                            ins = e.transpose(out=psbf(bT)[:, h * 128:(h + 1) * 128], in_=k0T4[d_][:, h * 128:(h + 1) * 128], identity=identb[:])
                        return ins
                    k.op(pe, [qkB[d_], identB], [psB[bT]], trk)
                    k.op(act, [psB[bT]], [k0tmB[d_]], lambda e, d_=d_, bT=bT: e.activation(out=k0tm4[d_][:], in_=psbf(bT)[:, 0:512], func=AF.Copy))
                    k.dma(pool, None, k0tmB[d_], hk0[d_].rearrange("h t k -> t h k")[t0:t0 + 128, :, :], k0tm4[d_][:].rearrange("p (h k) -> p h k", h=4), ("st_k0", d_))
                    def mat(e, d_=d_, bC=bC):
                        ins = None
                        for h in range(4):
                            ins = e.matmul(PS[:, bC, h * 128:(h + 1) * 128], lhsT=k1T4[d_][:, h * 128:(h + 1) * 128], rhs=q0T4[d_][:, h * 128:(h + 1) * 128], start=True, stop=True, skip_group_check=True)
                        return ins
                    k.op(pe, [qkB[d_]], [psB[bC]], mat)
                    k.op(dve, [psB[bC], cstB], [atmB[d_]], lambda e, d_=d_, mi=mi, bC=bC: e.tensor_tensor(
                        out=atm4[d_][:].rearrange("p (h t) -> p h t", h=4), in0=PS[:, bC, :].rearrange("p (h t) -> p h t", h=4),
                        in1=cst[:, mi + 2, :].unsqueeze(1).broadcast_to([128, 4, 128]), op=ALU.mult))
                    def mo_(e, d_=d_):
                        ins = None
                        for h in range(4):
                            ins = e.matmul(PS[:, 7, h * 64:(h + 1) * 64], lhsT=atm4[d_][:, h * 128:(h + 1) * 128], rhs=hvt[:, h * 64:(h + 1) * 64],
                                           start=(d_ == 0 and h == 0), stop=(d_ == 1), skip_group_check=True)
                        return ins
                    k.op(pe, [atmB[d_], hvtB], [psB[7]], mo_)
                k.op(dve, [psB[7]], [oitB[ti % 2]], lambda e, ti=ti: e.tensor_copy(out=oit[ti % 2][:], in_=PS[:, 7, 0:256]))
                k.dma(pool, None, oitB[ti % 2], hoi[t0:t0 + 128, :], oit[ti % 2][:], ("st_oi", ti % 2))
            k.barrier()
        chk("P1")
        chk("P1a")
        chk("P1b")

        with ExitStack() as ph:
            def psb_(name, shape, dt):
                return ph.enter_context(SBT(name, list(shape), dt))
            o_acc = psb_("o_acc", [128, NTT, 256], F32)
            oaccB = [Buf(None) for _ in range(NTT)]
            for g0 in range(0, NTT, 8):
                g1 = min(NTT, g0 + 8)
                tk = k.dma(sp, oaccB[g0], None, o_acc[:, g0:g1, :], hoi[g0 * 128:g1 * 128, :].rearrange("(t p) c -> p t c", p=128), "ld_oacc")
            for ti in range(NTT):
                oaccB[ti].w = [tk]

            S32 = psb_("S32", [128, 8, 64], F32)
            S16 = psb_("S16", [128, 8, 64], BF16)
            SB = [Buf(None) for _ in range(2)]
            for d_ in range(2):
                k.op(dve, [], [SB[d_]], lambda e, d_=d_: e.memset(S32[:, d_ * 4:(d_ + 1) * 4, :], 0.0))
                k.op(dve, [], [SB[d_]], lambda e, d_=d_: e.memset(S16[:, d_ * 4:(d_ + 1) * 4, :], 0.0))
            NB2 = 2
            lq = [[psb_("lq%d_%d" % (d_, i), [128, 4, 128], BF16) for i in range(NB2)] for d_ in range(2)]
            lk = [[psb_("lk%d_%d" % (d_, i), [128, 4, 128], BF16) for i in range(NB2)] for d_ in range(2)]
            ld = [[psb_("ld%d_%d" % (d_, i), [128, 4, 4], F32) for i in range(NB2)] for d_ in range(2)]
            lv = [[psb_("lv%d_%d" % (d_, i), [128, 256], BF16) for i in range(NB2)] for d_ in range(2)]
            lB = [[Buf(None) for i in range(NB2)] for d_ in range(2)]
            vm = [[psb_("vm%d_%d" % (d_, i), [128, 4, 256], BF16) for i in range(NB2)] for d_ in range(2)]
            vmB = [[Buf(None) for i in range(NB2)] for d_ in range(2)]
            pbank = {0: (0, 1), 1: (2, 3)}
            for s_ in range(NTT):
                for d_ in range(2):
                    if d_ == 0:
                        ti = s_
                    else:
                        ti = (NCT - 1 - s_) if s_ < NCT else (NTT - 1 - (s_ - NCT))
                    t0 = ti * 128
                    bi = s_ % NB2
                    B_ = lB[d_][bi]
                    k.dma(sp, B_, None, lq[d_][bi][:], hq0T[d_].rearrange("h k t -> k h t")[:, :, t0:t0 + 128], ("l2", d_, bi))
                    k.dma(sp, B_, None, lk[d_][bi][:], hk0[d_].rearrange("h t k -> t h k")[t0:t0 + 128, :, :], ("l2", d_, bi))
                    k.dma(sp, B_, None, ld[d_][bi][:], hdk[d_].rearrange("h k c -> k h c")[:, :, t0 // 32:t0 // 32 + 4], ("l2", d_, bi))
                    k.dma(sp, B_, None, lv[d_][bi][:], hvs[t0:t0 + 128, :], ("l2", d_, bi))
                    for c in range(4):
                        k.op(pool, [B_, cstB], [vmB[d_][bi]], lambda e, c=c, d_=d_, bi=bi: e.tensor_scalar(
                            out=vm[d_][bi][:, c, :], in0=lv[d_][bi][:], scalar1=cst[:, 8, c:c + 1], scalar2=None, op0=ALU.mult))
                    bo, bs = pbank[d_]
                    Sd32 = S32[:, d_ * 4:(d_ + 1) * 4, :]
                    Sd16 = S16[:, d_ * 4:(d_ + 1) * 4, :]
                    for c in (range(4) if d_ == 0 else range(3, -1, -1)):
                        def mo(e, d_=d_, bi=bi, bo=bo):
                            ins = None
                            for h in range(4):
                                ins = e.matmul(PS[:, bo, h * 64:(h + 1) * 64], lhsT=lq[d_][bi][:, h, :], rhs=S16[:, d_ * 4 + h, :], start=True, stop=True, skip_group_check=True)
                            return ins
                        k.op(pe, [B_, SB[d_]], [psB[bo]], mo)
                        k.op(dve, [psB[bo]], [oaccB[ti]], lambda e, bo=bo, c=c, ti=ti: e.tensor_tensor(
                            out=o_acc[c * 32:(c + 1) * 32, ti, :], in0=o_acc[c * 32:(c + 1) * 32, ti, :],
                            in1=PS[c * 32:(c + 1) * 32, bo, 0:256], op=ALU.add))
                        def ms(e, d_=d_, bi=bi, bs=bs, c=c):
                            ins = None
                            for h in range(4):
                                ins = e.matmul(PS[:, bs, h * 64:(h + 1) * 64], lhsT=lk[d_][bi][:, h, :], rhs=vm[d_][bi][:, c, h * 64:(h + 1) * 64], start=True, stop=True, skip_group_check=True)
                            return ins
                        k.op(pe, [B_, vmB[d_][bi]], [psB[bs]], ms)
                        k.op(dve, [B_, SB[d_]], [SB[d_]], lambda e, Sd32=Sd32, d_=d_, bi=bi, c=c: e.tensor_tensor(
                            out=Sd32, in0=Sd32, in1=ld[d_][bi][:, :, c:c + 1].broadcast_to([128, 4, 64]), op=ALU.mult))
                        k.op(dve, [psB[bs], SB[d_]], [SB[d_]], lambda e, Sd32=Sd32, bs=bs: e.tensor_tensor(
                            out=Sd32, in0=Sd32, in1=PS[:, bs, 0:256].rearrange("p (h v) -> p h v", h=4), op=ALU.add))
                        k.op(act, [SB[d_]], [SB[d_]], lambda e, Sd32=Sd32, Sd16=Sd16: e.activation(out=Sd16, in_=Sd32, func=AF.Copy))
            lg = [psb_("lg%d" % i, [128, 256], BF16) for i in range(2)]
            lgB = [Buf(None) for i in range(2)]
            sq = psb_("sq", [128, 4, 64], F32)
            sqB = Buf(None)
            ss = psb_("ss", [128, 4], F32)
            yo = [psb_("yo%d" % i, [128, 256], BF16) for i in range(2)]
            yoB = [Buf(None) for i in range(2)]
            for ti in range(NTT):
                if last and ti < NCT:
                    continue
                t0 = ti * 128
                bi = ti % 2
                k.dma(sp, lgB[bi], None, lg[bi][:], hgs[t0:t0 + 128, :], ("lg", bi))
                ov = o_acc[:, ti, :].rearrange("p (h v) -> p h v", h=4)
                k.op(pool, [oaccB[ti]], [sqB], lambda e, ov=ov: e.tensor_tensor(out=sq[:], in0=ov, in1=ov, op=ALU.mult))
                k.op(dve, [sqB], [sqB], lambda e: e.tensor_reduce(out=ss[:], in_=sq[:], axis=AX.X, op=ALU.add))
                k.op(act, [sqB, epsB], [sqB], lambda e: e.activation(out=ss[:], in_=ss[:], func=AF.Ln, scale=1.0 / 64, bias=eps_t[:]))
                k.op(act, [sqB], [sqB], lambda e: e.activation(out=ss[:], in_=ss[:], func=AF.Exp, scale=-0.5))
                k.op(dve, [sqB, oaccB[ti]], [sqB], lambda e, ov=ov: e.tensor_tensor(out=sq[:], in0=ov, in1=ss[:].unsqueeze(2).broadcast_to([128, 4, 64]), op=ALU.mult))
                k.op(dve, [sqB, smallB], [sqB], lambda e: e.tensor_tensor(out=sq[:], in0=sq[:], in1=hgg[:, l, :].unsqueeze(1).broadcast_to([128, 4, 64]), op=ALU.mult))
                k.op(dve, [sqB, lgB[bi]], [yoB[bi]], lambda e, bi=bi: e.tensor_tensor(out=yo[bi][:], in0=sq[:].rearrange("p h v -> p (h v)"), in1=lg[bi][:], op=ALU.mult))
                k.dma(pool, None, yoB[bi], ycat[t0:t0 + 128, 768:1024], yo[bi][:], ("st_yo", bi))
            k.barrier()
        chk("P2")

        for which in ("da", "mla"):
            with ExitStack() as ph:
                def psb_(name, shape, dt):
                    return ph.enter_context(SBT(name, list(shape), dt))
                if which == "da":
                    units = [(c, hh) for c in range(3) for hh in range(2)]
                    scale = 32 ** -0.5
                else:
                    units = [(h, 0) for h in range(6)]
                    scale = 96 ** -0.5
                nacc = 2 if which == "da" else 1
                KT = psb_("KT", [128, T], BF16)
                KTB = Buf(KT)
                QM = [psb_("QM%d" % j, [128, T], BF16) for j in range(4 if which == "da" else 1)]
                QMB = Buf(None)
                VA = psb_("VA", [128, NTT, 2, 65], BF16)
                VAB = Buf(VA)
                if which == "da":
                    for j in range(4):
                        k.op(pool, [], [QMB], lambda e, j=j: e.memset(QM[j][:], 0.0))
                k.op(pool, [], [VAB], lambda e: e.memset(VA[:], 1.0))
                NPB = 3
                pt = [psb_("pt%d" % i, [128, 2, 512], BF16) for i in range(NPB)]
                ptB = [Buf(None) for i in range(NPB)]
                osb = psb_("osb", [65, 2, 512], F32)
                osbB = Buf(None)
                otm = [psb_("otm%d" % i, [128, 2, 65], F32) for i in range(4)]
                rr = [psb_("rr%d" % i, [128, 4], F32) for i in range(4)]
                oo = [psb_("oo%d" % i, [128, 64], F32) for i in range(4)]
                o2 = psb_("o2", [128, 64], F32)
                otmB = [Buf(None) for i in range(4)]
                yb = [psb_("yb%d" % i, [128, 64], BF16) for i in range(4)]
                ybB = [Buf(None) for i in range(4)]
                sbank = [(0, 1), (2, 3)]
                asets = [(4, 5), (6, 7)]
                chunk_ctr = [0]
                pend = []
                cur_chunk = -1

                def make_epilogue(aset, q0, nq, ycol):
                    nsub = nq // 128

                    def stage1():
                        for cm in range(nacc):
                            k.op(dve, [psB[aset[cm]]], [osbB], lambda e, cm=cm: e.tensor_copy(out=osb[:, cm, 0:nq], in_=PS[0:65, aset[cm], 0:nq]))
                        for qs in range(nsub):
                            def trO(e, qs=qs):
                                ins = None
                                for cm in range(nacc):
                                    ins = e.transpose(out=PS[:, aset[cm], qs * 66:qs * 66 + 65], in_=osb[:, cm, qs * 128:(qs + 1) * 128], identity=cst[0:65, 0, 0:65])
                                return ins
                            k.op(pe, [osbB, cstB], [psB[aset[i]] for i in range(nacc)], trO)
                        for qs in range(nsub):
                            for cm in range(nacc):
                                k.op(dve, [psB[aset[cm]]], [otmB[qs]], lambda e, qs=qs, cm=cm: e.tensor_copy(
                                    out=otm[qs][:, cm, :], in_=PS[:, aset[cm], qs * 66:qs * 66 + 65]))
                            if which == "da":
                                k.op(dve, [otmB[qs]], [otmB[qs]], lambda e, qs=qs: e.reciprocal(out=rr[qs][:, 0:2], in_=otm[qs][:, :, 64]))
                                k.op(dve, [otmB[qs], modB], [otmB[qs]], lambda e, qs=qs: e.tensor_tensor(out=rr[qs][:, 1:2], in0=rr[qs][:, 1:2], in1=lam_t[:, 0:1], op=ALU.mult))
                                k.op(dve, [otmB[qs]], [otmB[qs]], lambda e, qs=qs: e.tensor_scalar(out=oo[qs][:], in0=otm[qs][:, 0, 0:64], scalar1=rr[qs][:, 0:1], scalar2=None, op0=ALU.mult))
                                k.op(dve, [otmB[qs]], [otmB[qs]], lambda e, qs=qs: e.scalar_tensor_tensor(out=oo[qs][:], in0=otm[qs][:, 1, 0:64], scalar=rr[qs][:, 1:2], in1=oo[qs][:], op0=ALU.mult, op1=ALU.add))
                                k.op(dve, [otmB[qs]], [otmB[qs]], lambda e, qs=qs: e.tensor_tensor(out=o2[:], in0=oo[qs][:], in1=oo[qs][:], op=ALU.mult))
                                k.op(dve, [otmB[qs]], [otmB[qs]], lambda e, qs=qs: e.tensor_reduce(out=rr[qs][:, 2:3], in_=o2[:], axis=AX.X, op=ALU.add))
                            else:
                                k.op(dve, [otmB[qs]], [otmB[qs]], lambda e, qs=qs: e.reciprocal(out=rr[qs][:, 0:1], in_=otm[qs][:, 0, 64:65]))

                    def stage2():
                        for qs in range(nsub):
                            if which == "da":
                                k.op(act, [otmB[qs], epsB], [otmB[qs]], lambda e, qs=qs: e.activation(out=rr[qs][:, 2:3], in_=rr[qs][:, 2:3], func=AF.Ln, scale=1.0 / 64, bias=eps_t[:]))
                                k.op(act, [otmB[qs]], [otmB[qs]], lambda e, qs=qs: e.activation(out=rr[qs][:, 2:3], in_=rr[qs][:, 2:3], func=AF.Exp, scale=-0.5))
                        for qs in range(nsub):
                            if which == "da":
                                k.op(dve, [otmB[qs], modB], [ybB[qs]], lambda e, qs=qs: e.scalar_tensor_tensor(
                                    out=yb[qs][:], in0=oo[qs][:], scalar=rr[qs][:, 2:3], in1=dagl[:], op0=ALU.mult, op1=ALU.mult))
                            else:
                                k.op(dve, [otmB[qs]], [ybB[qs]], lambda e, qs=qs: e.tensor_scalar(out=yb[qs][:], in0=otm[qs][:, 0, 0:64], scalar1=rr[qs][:, 0:1], scalar2=None, op0=ALU.mult))
                            r0 = q0 + qs * 128
                            k.dma(pool, None, ybB[qs], ycat[r0:r0 + 128, ycol:ycol + 64], yb[qs][:], ("st_y", qs))
                    return [stage1, stage2]

                for (c, hh) in units:
                    if which == "da":
                        if c != cur_chunk:
                            cur_chunk = c
                            k.dma(sp, KTB, None, KT[:], dakT[c * 128:(c + 1) * 128, :], "ld_kt")
                            for j in range(4):
                                k.dma(sp, QMB, None, QM[j][j * 32:(j + 1) * 32, :], daqT[c * 128 + j * 32:c * 128 + (j + 1) * 32, :], "ld_qm")
                            for g0 in range(0, NTT, 8):
                                g1 = min(NTT, g0 + 8)
                                for h2 in range(2):
                                    k.dma(sp, VAB, None, VA[:, g0:g1, h2, 0:64],
                                          dav[g0 * 128:g1 * 128, c * 128 + h2 * 64:c * 128 + (h2 + 1) * 64].rearrange("(kt p) e -> p kt e", p=128), "ld_va")
                        head = c * 2 + hh
                        ycol = head * 64
                    else:
                        h = c
                        k.dma(sp, KTB, None, KT[0:64, :], mknT[h * 64:(h + 1) * 64, :], "ld_kt")
                        k.dma(sp, KTB, None, KT[64:96, :], mkrT[:, :], "ld_kt")
                        k.dma(sp, QMB, None, QM[0][0:96, :], mqT[h, :, :], "ld_qm")
                        for g0 in range(0, NTT, 8):
                            g1 = min(NTT, g0 + 8)
                            k.dma(sp, VAB, None, VA[:, g0:g1, 0, 0:64],
                                  mv[g0 * 128:g1 * 128, h * 64:(h + 1) * 64].rearrange("(kt p) e -> p kt e", p=128), "ld_va")
                        ycol = 384 + h * 64
                    qchunks = []
                    if not last:
                        qchunks.append((0, CTX, NCT))
                    for q0 in range(CTX, T, 512):
                        qchunks.append((q0, min(512, T - q0), NTT))
                    for (q0, nq, nkt) in qchunks:
                        aset = asets[chunk_ctr[0] % 2]
                        chunk_ctr[0] += 1
                        nit = nkt if which == "da" else nkt // 2

                        def scores(it):
                            sbk = sbank[it % 2]
                            def f(e):
                                ins = None
                                for cm in range(2):
                                    if which == "da":
                                        rhs = QM[hh * 2 + cm][:, q0:q0 + nq]
                                        lhsT = KT[:, it * 128:(it + 1) * 128]
                                    else:
                                        kt = it * 2 + cm
                                        rhs = QM[0][0:96, q0:q0 + nq]
                                        lhsT = KT[0:96, kt * 128:(kt + 1) * 128]
                                    ins = e.matmul(PS[:, sbk[cm], 0:nq], lhsT=lhsT, rhs=rhs, start=True, stop=True)
                                return ins
                            k.op(pe, [KTB, QMB], [psB[sbk[0]], psB[sbk[1]]], f)

                        def expo(it):
                            sbk = sbank[it % 2]
                            pb = it % NPB
                            k.op(act, [psB[sbk[0]], psB[sbk[1]]], [ptB[pb]], lambda e: e.activation(
                                out=pt[pb][:, :, 0:nq], in_=PS[:, sbk[0]:sbk[0] + 2, 0:nq], func=AF.Exp, scale=scale))

                        def pv(it):
                            pb = it % NPB
                            def f(e):
                                ins = None
                                for cm in range(2):
                                    if which == "da":
                                        vsel = VA[:, it, hh, :]
                                        ab = aset[cm]
                                        st_, sp_ = (it == 0), (it == nit - 1)
                                    else:
                                        kt = it * 2 + cm
                                        vsel = VA[:, kt, 0, :]
                                        ab = aset[0]
                                        st_, sp_ = (kt == 0), (kt == nkt - 1)
                                    ins = e.matmul(PS[0:65, ab, 0:nq], lhsT=vsel, rhs=pt[pb][:, cm, 0:nq], start=st_, stop=sp_)
                                return ins
                            k.op(pe, [VAB, ptB[pb]], [psB[aset[i]] for i in range(nacc)] if it == 0 else [], f)
                        e1 = min(4, nit + 1)
                        e2 = min(24, nit + 1)
                        for it in range(nit + 2):
                            if it < nit:
                                scores(it)
                                expo(it)
                            if it >= 2:
                                pv(it - 2)
                            if pend and it == e1:
                                pend[0][0]()
                            if pend and it == e2:
                                pend[0][1]()
                                pend.pop(0)
                        tok_last = (pe.sem, pe.cnt, pe)
                        for i in range(nacc):
                            psB[aset[i]].w = [tok_last]
                            psB[aset[i]].r = []
                        pend.append(make_epilogue(aset, q0, nq, ycol))
                while pend:
                    pend[0][0]()
                    pend[0][1]()
                    pend.pop(0)
                k.barrier()
            chk("P3" if which == "da" else "P4")

        tiles = list(range(NTT)) if not last else list(range(NCT, NTT))
        with ExitStack() as ph:
            def psb_(name, shape, dt):
                return ph.enter_context(SBT(name, list(shape), dt))
            wo = psb_("wo", [128, 8, D], BF16)
            wB = Buf(None)
            with ExitStack() as st_:
                stg = [st_.enter_context(SBT("stg5a_%d" % i, [128, D], F32)) for i in range(2)]
                stgB = [Buf(None) for i in range(2)]
                for kk in range(8):
                    bi = kk % 2
                    k.dma(sp, stgB[bi], None, stg[bi][:, :], w_out[l, kk * 128:(kk + 1) * 128, :], ("stg5a", bi))
                    k.op(dve, [stgB[bi]], [wB], lambda e, kk=kk, bi=bi: e.tensor_copy(out=wo[:, kk, :], in_=stg[bi][:, :]))
            k.barrier()
            gate_bc, gateB = make_gates(psb_, 2)
            yt = [psb_("yt%d" % i, [128, D], BF16) for i in range(2)]
            ytB = [Buf(None) for i in range(2)]
            yT = psb_("yT", [128, 8, 128], BF16)
            yTB = Buf(None)
            xt = [psb_("x5_%d" % i, [128, D], F32) for i in range(2)]
            xtB = [Buf(None) for i in range(2)]
            junk = psb_("junk5", [128, D], BF16)
            junkB = Buf(None)
            junk5f = psb_("junk5f", [128, D], F32)
            j5B = Buf(None)
            st = psb_("st5", [128, 4], F32)
            stB = Buf(None)
            xn = psb_("xn5", [128, D], BF16)
            xnB = Buf(None)
            h2T = [psb_("h2T%d" % i, [128, 8, 128], BF16) for i in range(2)]
            h2B = [Buf(None) for i in range(2)]
            for idx, ti in enumerate(tiles):
                is_ctx = ti < NCT
                mj = 1 if is_ctx else 0
                t0 = ti * 128
                bi = idx % 2
                xsrc = xc_res[t0:t0 + 128, :] if is_ctx else x_src[t0 - CTX:t0 - CTX + 128, :]
                xdst = xc_res[t0:t0 + 128, :] if is_ctx else out[t0 - CTX:t0 - CTX + 128, :]
                k.dma(sp, ytB[bi], None, yt[bi][:], ycat[t0:t0 + 128, :], ("yt", bi))
                k.dma(sp, xtB[bi], None, xt[bi][:], xsrc, ("x5", bi))

                def tr8(e, bi=bi):
                    ins = None
                    for c in range(8):
                        ins = e.transpose(out=psbf(0)[:, c * 128:(c + 1) * 128], in_=yt[bi][:, c * 128:(c + 1) * 128], identity=identb[:])
                    return ins
                k.op(pe, [ytB[bi], identB], [psB[0]], tr8)
                k.op(act, [psB[0]], [yTB], lambda e: e.activation(out=yT[:].rearrange("p c t -> p (c t)"), in_=psbf(0)[:, :], func=AF.Copy))

                def mmo(e):
                    ins = None
                    for half in range(2):
                        for kk in range(8):
                            ins = e.matmul(PS[:, 1 + half, :], lhsT=yT[:, kk, :], rhs=wo[:, kk, half * 512:(half + 1) * 512], start=(kk == 0), stop=(kk == 7))
                    return ins
                k.op(pe, [yTB, wB], [psB[1], psB[2]], mmo)
                xb_ = xt[bi]
                xB_ = xtB[bi]
                for half in range(2):
                    k.op(dve, [psB[1 + half], gateB], [j5B], lambda e, half=half, xb_=xb_: e.tensor_tensor(
                        out=junk5f[:, half * 512:(half + 1) * 512], in0=PS[:, 1 + half, :], in1=gate_bc[:, mj, half * 512:(half + 1) * 512], op=ALU.mult))
                k.op(pool, [xB_, j5B], [xB_], lambda e, xb_=xb_: e.tensor_tensor(out=xb_[:], in0=xb_[:], in1=junk5f[:], op=ALU.add))
                k.dma(pool, None, xB_, xdst, xb_[:], ("st_x1", bi))
                k.op(act, [xB_], [junkB, stB], lambda e, xb_=xb_: e.activation(out=junk[:], in_=xb_[:], func=AF.Square, accum_out=st[:, 0:1]))
                k.op(act, [stB, epsB], [stB], lambda e: e.activation(out=st[:, 0:1], in_=st[:, 0:1], func=AF.Ln, scale=1.0 / D, bias=eps_t[:]))
                k.op(act, [stB], [stB], lambda e: e.activation(out=st[:, 0:1], in_=st[:, 0:1], func=AF.Exp, scale=-0.5))
                k.op(dve, [xB_, stB], [xnB], lambda e, xb_=xb_: e.tensor_scalar(out=xn[:], in0=xb_[:], scalar1=st[:, 0:1], scalar2=None, op0=ALU.mult))

                def tr8b(e):
                    ins = None
                    for c in range(8):
                        ins = e.transpose(out=psbf(3)[:, c * 128:(c + 1) * 128], in_=xn[:, c * 128:(c + 1) * 128], identity=identb[:])
                    return ins
                k.op(pe, [xnB, identB], [psB[3]], tr8b)
                for c in range(8):
                    k.op(act, [psB[3], modB], [h2B[bi]], lambda e, c=c, bi=bi: e.activation(
                        out=h2T[bi][:, c, :], in_=psbf(3)[:, c * 128:(c + 1) * 128], func=AF.Identity,
                        scale=A2[:, c, mj:mj + 1], bias=modsb[:, 3 * 8 + c, mj:mj + 1]))
                k.dma(pool, None, h2B[bi], h2s.rearrange("c p t -> p c t")[:, :, t0:t0 + 128], h2T[bi][:], ("st_h2", bi))
            k.barrier()
        chk("P5a")

        with ExitStack() as ph:
            def psb_(name, shape, dt):
                return ph.enter_context(SBT(name, list(shape), dt))
            w1 = psb_("w1", [128, 8, D_FF], BF16)
            w2 = psb_("w2", [128, 32, D], BF16)
            wB = Buf(None)
            with ExitStack() as st_:
                stg = [st_.enter_context(SBT("stg5b_%d" % i, [128, 2048], F32)) for i in range(2)]
                stgB = [Buf(None) for i in range(2)]
                cnt = 0
                for kk in range(8):
                    for hf in range(2):
                        bi = cnt % 2
                        cnt += 1
                        k.dma(sp, stgB[bi], None, stg[bi][:, :], w_ff1[l, kk * 128:(kk + 1) * 128, hf * 2048:(hf + 1) * 2048], ("stg5b", bi))
                        k.op(pool if cnt % 2 else dve, [stgB[bi]], [wB], lambda e, kk=kk, bi=bi, hf=hf: e.tensor_copy(out=w1[:, kk, hf * 2048:(hf + 1) * 2048], in_=stg[bi][:, :]))
                for kk in range(0, 32, 2):
                    bi = cnt % 2
                    cnt += 1
                    k.dma(sp, stgB[bi], None, stg[bi][:, :].rearrange("p (a e) -> p a e", a=2),
                          w_ff2[l, kk * 128:(kk + 2) * 128, :].rearrange("(a p) e -> p a e", p=128), ("stg5b", bi))
                    k.op(pool if cnt % 2 else dve, [stgB[bi]], [wB], lambda e, kk=kk, bi=bi: e.tensor_copy(
                        out=w2[:, kk:kk + 2, :], in_=stg[bi][:, :].rearrange("p (a e) -> p a e", a=2)))
            k.barrier()
            gate_bc, gateB = make_gates(psb_, 5)
            if last:
                gfin = psb_("gfin", [128, D], F32)
                gfinB = Buf(None)
                k.dma(sp, gfinB, None, gfin[:], g_final.partition_broadcast(128), "gfin")
            h2T = [psb_("h2Tb%d" % i, [128, 8, 128], BF16) for i in range(2)]
            h2B = [Buf(None) for i in range(2)]
            xt = [psb_("x5b_%d" % i, [128, D], F32) for i in range(2)]
            xtB = [Buf(None) for i in range(2)]
            rl = [psb_("rl%d" % i, [128, 512], F32) for i in range(2)]
            rlB = [Buf(None) for i in range(2)]
            uT = psb_("uT", [128, 32, 128], BF16)
            uTB = Buf(None)
            gp = psb_("gp", [128, D], F32)
            gpB = Buf(None)
            junk = psb_("junk5b", [128, D], BF16)
            junkB = Buf(None)
            st = psb_("st5b", [128, 4], F32)
            stB = Buf(None)
            for idx, ti in enumerate(tiles):
                is_ctx = ti < NCT
                mj = 1 if is_ctx else 0
                t0 = ti * 128
                bi = idx % 2
                xdst = xc_res[t0:t0 + 128, :] if is_ctx else out[t0 - CTX:t0 - CTX + 128, :]
                k.dma(sp, h2B[bi], None, h2T[bi][:], h2s.rearrange("c p t -> p c t")[:, :, t0:t0 + 128], ("ld_h2", bi))
                k.dma(sp, xtB[bi], None, xt[bi][:], xdst, ("x5b", bi))
                for g in range(8):
                    bank = 3 + g % 4
                    def f1(e, g=g, bank=bank, bi=bi):
                        ins = None
                        for j in range(4):
                            fc = g * 4 + j
                            for kk in range(8):
                                ins = e.matmul(PS[:, bank, j * 128:(j + 1) * 128], lhsT=w1[:, kk, fc * 128:(fc + 1) * 128], rhs=h2T[bi][:, kk, :],
                                               start=(kk == 0), stop=(kk == 7), skip_group_check=True)
                        return ins
                    k.op(pe, [h2B[bi], wB], [psB[bank]], f1)
                    k.op(act, [psB[bank]], [rlB[g % 2]], lambda e, g=g, bank=bank: e.activation(out=rl[g % 2][:], in_=PS[:, bank, :], func=AF.Relu))
                    k.op(dve if g % 2 else pool, [rlB[g % 2]], [uTB], lambda e, g=g: e.tensor_tensor(
                        out=uT[:, g * 4:(g + 1) * 4, :].rearrange("p c t -> p (c t)"), in0=rl[g % 2][:], in1=rl[g % 2][:], op=ALU.mult))
                def f2(e):
                    ins = None
                    for half in range(2):
                        for fc in range(32):
                            ins = e.matmul(PS[:, 1 + half, :], lhsT=uT[:, fc, :], rhs=w2[:, fc, half * 512:(half + 1) * 512], start=(fc == 0), stop=(fc == 31))
                    return ins
                k.op(pe, [uTB, wB], [psB[1], psB[2]], f2)
                xb_ = xt[bi]
                xB_ = xtB[bi]
                for half in range(2):
                    k.op(dve, [psB[1 + half], gateB], [gpB], lambda e, half=half: e.tensor_tensor(
                        out=gp[:, half * 512:(half + 1) * 512], in0=PS[:, 1 + half, :], in1=gate_bc[:, mj, half * 512:(half + 1) * 512], op=ALU.mult))
                k.op(pool, [gpB, xB_], [xB_], lambda e, xb_=xb_: e.tensor_tensor(out=xb_[:], in0=xb_[:], in1=gp[:], op=ALU.add))
                if last:
                    k.op(act, [xB_], [junkB, stB], lambda e, xb_=xb_: e.activation(out=junk[:], in_=xb_[:], func=AF.Square, accum_out=st[:, 1:2]))
                    k.op(act, [stB, epsB], [stB], lambda e: e.activation(out=st[:, 1:2], in_=st[:, 1:2], func=AF.Ln, scale=1.0 / D, bias=eps_t[:]))
                    k.op(act, [stB], [stB], lambda e: e.activation(out=st[:, 1:2], in_=st[:, 1:2], func=AF.Exp, scale=-0.5))
                    k.op(dve, [xB_, stB, gfinB], [xB_], lambda e, xb_=xb_: e.scalar_tensor_tensor(
                        out=xb_[:], in0=xb_[:], scalar=st[:, 1:2], in1=gfin[:], op0=ALU.mult, op1=ALU.mult))
                k.dma(pool, None, xB_, xdst, xb_[:], ("st_x2", bi))
            k.barrier()

    except _Stop:
        print("STOPPED at", stop, "nops", k.nops)
        k.barrier()
    es.close()
    return nc


def rope_tables(SEQ):
    rows = SEQ // GRID_W
    row = np.repeat(np.arange(rows, dtype=np.float32), GRID_W)
    col = np.tile(np.arange(GRID_W, dtype=np.float32), rows)
    inv = (10000.0 ** (-np.arange(8, dtype=np.float32) / 8)).astype(np.float32)
    ang = np.stack([row[:, None] * inv, col[:, None] * inv], axis=1).astype(np.float32)
    return np.cos(ang).reshape(SEQ, 16).astype(np.float32), np.sin(ang).reshape(SEQ, 16).astype(np.float32)


def make_consts():
    c = np.zeros((128, 9, 128), np.float32)
    c[:, 0, :] = np.eye(128, dtype=np.float32)
    c[:, 1, :] = 1.0
    s = np.arange(128)[:, None]
    t = np.arange(128)[None, :]
    same = (s // 32) == (t // 32)
    c[:, 2, :] = (same & (s <= t)).astype(np.float32)
    c[:, 3, :] = (same & (s > t)).astype(np.float32)
    c[:, 4, :] = (same & (s <= t)).astype(np.float32)
    c[:, 5, :] = (same & (s >= t)).astype(np.float32)
    c[:, 6, :] = (same & (s < t)).astype(np.float32)
    c[:, 7, :] = (same & (s >= t)).astype(np.float32)
    for cc in range(4):
        c[cc * 32:(cc + 1) * 32, 8, cc] = 1.0
    return c


def prep_inputs(inputs, b, SEQ, CTX, DEPTH):
    f = lambda a: np.ascontiguousarray(np.asarray(a, dtype=np.float32))
    cos, sin = rope_tables(SEQ)
    NT = SEQ // 128
    rope_cs = np.stack([cos.reshape(NT, 128, 16), sin.reshape(NT, 128, 16)], axis=2).transpose(1, 0, 2, 3)
    cT = np.stack([f(inputs["c"])[b].reshape(8, 128).T, f(inputs["c_ctx"]).reshape(8, 128).T], axis=2)
    w_ukv = f(inputs["mla_w_ukv"]).reshape(DEPTH, 128, 6, 128)
    m = {
        "x": f(inputs["x"])[b],
        "ctx": f(inputs["ctx"])[b],
        "cT": f(cT),
        "w_mod": f(inputs["w_mod"]),
        "b_modT": f(f(inputs["b_mod"]).reshape(DEPTH, 48, 128).transpose(2, 0, 1)),
        "g_mixT": f(f(inputs["g_mix"]).reshape(DEPTH, 8, 128).transpose(2, 0, 1)),
        "g_mlpT": f(f(inputs["g_mlp"]).reshape(DEPTH, 8, 128).transpose(2, 0, 1)),
        "w_in": f(inputs["w_in"]),
        "w_out": f(inputs["w_out"]),
        "da_lambda": f(f(inputs["da_lambda"]).reshape(DEPTH, 128)),
        "da_g": f(inputs["da_subln_g"]),
        "g_cqT": f(f(inputs["mla_g_cq"]).reshape(DEPTH, 2, 128).transpose(2, 0, 1)),
        "g_ckvT": f(f(inputs["mla_g_ckv"]).reshape(DEPTH, 1, 128).transpose(2, 0, 1)),
        "w_uq": f(inputs["mla_w_uq"]),
        "w_ukn": f(w_ukv[:, :, :, 0:64].reshape(DEPTH, 128, 384)),
        "w_uv": f(w_ukv[:, :, :, 64:128].reshape(DEPTH, 128, 384)),
        "hg_lbT": f(f(inputs["hg_lower_bounds"]).reshape(2, DEPTH, HG_H, 128).transpose(3, 0, 2, 1)),
        "hg_g": f(inputs["hg_norm_g"]),
        "w_ff1": f(inputs["w_ff1"]),
        "w_ff2": f(inputs["w_ff2"]),
        "g_final": f(f(inputs["g_final"]).reshape(1, D)),
        "rope_cs": f(rope_cs),
        "consts_f": make_consts(),
    }
    return m


_NC_CACHE = {}


STOP = None
LIMIT = None


def kernel(**inputs):
    x = np.asarray(inputs["x"])
    B, SEQ, _ = x.shape
    CTX = np.asarray(inputs["ctx"]).shape[1]
    DEPTH = np.asarray(inputs["w_in"]).shape[0]
    key = (SEQ, CTX, DEPTH)
    if key not in _NC_CACHE:
        _NC_CACHE[key] = build(SEQ, CTX, DEPTH, stop=STOP)
    nc = _NC_CACHE[key]
    in_maps = [prep_inputs(inputs, c % B, SEQ, CTX, DEPTH) for c in range(8)]
    res = run_bass_kernel_spmd(nc, in_maps, core_ids=list(range(8)))
    outp = np.stack([np.asarray(res.results[b]["out"], dtype=np.float32) for b in range(B)], axis=0)
    return outp
```
